# Optimizing a Trainium2 kernel written in Bass

```python
import math
import jax, jax.numpy as jnp
from jax import lax
import numpy as np

D_MODEL = 1024
BATCH = 4
SEQ = 8192
DEPTH = 1

MEM_LEN = 256
M_HEADS = 4
M_DQK = 128
M_DV = 256
CHUNK = 128
CONV_K = 4
D_HEADS = 8
D_HD = 64
D_DV = 2 * D_HD
Q_BLOCK = 128
C_HEADS = 4
C_DQK = 128
C_DV = 256
D_FF = 4 * D_MODEL
N_BRANCH = 3
EPS = 1e-6

M_QK_W = M_HEADS * M_DQK
M_V_W = M_HEADS * M_DV
D_Q_W = D_HEADS * 2 * D_HD
D_V_W = D_HEADS * D_DV
C_Q_W = C_HEADS * C_DQK
C_V_W = C_HEADS * C_DV
IN_SPLIT_SIZES = (M_QK_W, M_QK_W, M_V_W, M_HEADS, M_HEADS, M_V_W, D_Q_W, D_Q_W, D_V_W, C_Q_W)
IN_COLS = sum(IN_SPLIT_SIZES)

kernel_name = 'hybrid_mlstm_diffattn_memxattn_block'


def rms_norm(x, g):
    xf = x.astype(jnp.float32)
    y = xf * lax.rsqrt(jnp.mean(xf * xf, axis=-1, keepdims=True) + EPS)
    return (y * g.astype(jnp.float32)).astype(x.dtype)


def lambda_init(layer):
    return 0.8 - 0.6 * math.exp(-0.3 * layer)


def causal_depthwise_conv(x, w, b):
    c = x.shape[-1]
    y = lax.conv_general_dilated(x, w.astype(x.dtype).reshape(CONV_K, 1, c), window_strides=(1,),
                                 padding=[(CONV_K - 1, 0)], dimension_numbers=('NWC', 'WIO', 'NWC'),
                                 feature_group_count=c)
    return y + b.astype(x.dtype)


def mlstm_chunkwise(q, k, v, ig, fg_pre):
    bsz, seq, nh, dk = q.shape
    dv = v.shape[-1]
    nc = seq // CHUNK
    f32 = jnp.float32

    def to_chunks(t):
        t = t.astype(f32).reshape(bsz, nc, CHUNK, nh, -1)
        return t.transpose(1, 0, 3, 2, 4)

    qc = to_chunks(q * (dk ** -0.5))
    kc = to_chunks(k)
    vc = to_chunks(v)
    igc = to_chunks(ig[..., None])[..., 0]
    lfc = to_chunks(jax.nn.log_sigmoid(fg_pre.astype(f32))[..., None])[..., 0]
    causal = jnp.tril(jnp.ones((CHUNK, CHUNK), dtype=bool))

    def step(carry, xs):
        C, n, m = carry
        qq, kk, vv, ii, lf = xs
        bcum = jnp.cumsum(lf, axis=-1)
        logd = bcum[..., :, None] - bcum[..., None, :] + ii[..., None, :]
        logd = jnp.where(causal, logd, -jnp.inf)
        inter = bcum + m[..., None]
        m_loc = jnp.maximum(inter, jnp.max(logd, axis=-1))
        w_intra = jnp.exp(logd - m_loc[..., None])
        w_inter = jnp.exp(inter - m_loc)
        s = jnp.einsum('bhjd,bhsd->bhjs', qq, kk) * w_intra
        num = jnp.einsum('bhjs,bhsv->bhjv', s, vv) + w_inter[..., None] * jnp.einsum('bhjd,bhdv->bhjv', qq, C)
        den = jnp.sum(s, axis=-1) + w_inter * jnp.einsum('bhjd,bhd->bhj', qq, n)
        h = num / jnp.maximum(jnp.abs(den), jnp.exp(-m_loc))[..., None]
        b_end = bcum[..., -1]
        log_w = b_end[..., None] - bcum + ii
        m_new = jnp.maximum(b_end + m, jnp.max(log_w, axis=-1))
        w_s = jnp.exp(log_w - m_new[..., None])
        decay = jnp.exp(b_end + m - m_new)
        C_new = decay[..., None, None] * C + jnp.einsum('bhs,bhsd,bhsv->bhdv', w_s, kk, vv)
        n_new = decay[..., None] * n + jnp.einsum('bhs,bhsd->bhd', w_s, kk)
        return (C_new, n_new, m_new), h

    init = (jnp.zeros((bsz, nh, dk, dv), f32), jnp.zeros((bsz, nh, dk), f32), jnp.zeros((bsz, nh), f32))
    _, hc = lax.scan(step, init, (qc, kc, vc, igc, lfc))
    return hc.transpose(1, 0, 3, 2, 4).reshape(bsz, seq, nh, dv)


def diff_attention(q, k, v, q_g, k_g, lam, subln_g, lam_init):
    q = rms_norm(q, q_g)
    k = rms_norm(k, k_g)
    bsz, seq, nh, _, d = q.shape
    nb = seq // Q_BLOCK
    qb = q.reshape(bsz, nb, Q_BLOCK, nh, 2, d).transpose(1, 0, 3, 4, 2, 5)
    kt = k.transpose(0, 2, 3, 1, 4)
    vt = v.transpose(0, 2, 1, 3)
    kpos = jnp.arange(seq)
    scale = d ** -0.5

    def block(args):
        qblk, i = args
        s = jnp.einsum('bhcqd,bhckd->bhcqk', qblk, kt).astype(jnp.float32) * scale
        qpos = i * Q_BLOCK + jnp.arange(Q_BLOCK)
        s = jnp.where(kpos[None, :] <= qpos[:, None], s, -jnp.inf)
        p = jax.nn.softmax(s, axis=-1)
        a = p[:, :, 0] - lam * p[:, :, 1]
        return jnp.einsum('bhqk,bhkv->bhqv', a.astype(vt.dtype), vt)

    o = lax.map(block, (qb, jnp.arange(nb)))
    o = o.transpose(1, 0, 3, 2, 4).reshape(bsz, seq, nh, -1)
    o = rms_norm(o, subln_g) * (1.0 - lam_init)
    return o.reshape(bsz, seq, -1)


def memory_cross_attention(q, mem, mem_g, w_kv, q_g, k_g):
    bsz, seq, _ = q.shape
    qh = rms_norm(q.reshape(bsz, seq, C_HEADS, C_DQK), q_g)
    mkv = rms_norm(mem, mem_g) @ w_kv
    mk, mv = jnp.split(mkv, [C_Q_W], axis=-1)
    mk = rms_norm(mk.reshape(bsz, -1, C_HEADS, C_DQK), k_g)
    mv = mv.reshape(bsz, -1, C_HEADS, C_DV)
    s = jnp.einsum('bshd,bmhd->bhsm', qh, mk).astype(jnp.float32) * (C_DQK ** -0.5)
    p = jax.nn.softmax(s, axis=-1)
    o = jnp.einsum('bhsm,bmhv->bshv', p.astype(mv.dtype), mv)
    return o.reshape(bsz, seq, -1)


def setup_inputs(seed: int = 0) -> dict:
    key = jax.random.key(seed)
    ks = iter(jax.random.split(key, 40))

    def nrm(shape, scale):
        return scale * jax.random.normal(next(ks), shape, jnp.float32)

    def gain(shape):
        return 1.0 + nrm(shape, 0.02)

    L = DEPTH
    return {
        'x': nrm((BATCH, SEQ, D_MODEL), 1.0),
        'mem': nrm((BATCH, MEM_LEN, D_MODEL), 1.0),
        'norm_mix_g': gain((L, D_MODEL)),
        'w_in': nrm((L, D_MODEL, IN_COLS), D_MODEL ** -0.5),
        'b_igate': nrm((L, M_HEADS), 0.1),
        'b_fgate': jnp.linspace(3.0, 6.0, M_HEADS, dtype=jnp.float32)[None, :] + nrm((L, M_HEADS), 0.1),
        'conv_w': nrm((L, CONV_K, 2 * M_QK_W), CONV_K ** -0.5),
        'conv_b': nrm((L, 2 * M_QK_W), 0.02),
        'm_norm_g': gain((L, M_V_W)),
        'dq_norm_g': gain((L, D_HD)),
        'dk_norm_g': gain((L, D_HD)),
        'lam_q1': nrm((L, D_HD), 0.1),
        'lam_k1': nrm((L, D_HD), 0.1),
        'lam_q2': nrm((L, D_HD), 0.1),
        'lam_k2': nrm((L, D_HD), 0.1),
        'subln_g': gain((L, D_DV)),
        'cq_norm_g': gain((L, C_DQK)),
        'ck_norm_g': gain((L, C_DQK)),
        'mem_norm_g': gain((L, D_MODEL)),
        'w_mem_kv': nrm((L, D_MODEL, C_Q_W + C_V_W), D_MODEL ** -0.5),
        'w_gate': nrm((L, D_MODEL, N_BRANCH * D_MODEL), D_MODEL ** -0.5),
        'b_gate': nrm((L, N_BRANCH * D_MODEL), 0.02),
        'w_proj_m': nrm((L, M_V_W, D_MODEL), M_V_W ** -0.5),
        'w_proj_d': nrm((L, D_V_W, D_MODEL), D_V_W ** -0.5),
        'w_proj_c': nrm((L, C_V_W, D_MODEL), C_V_W ** -0.5),
        'w_out': nrm((L, D_MODEL, D_MODEL), D_MODEL ** -0.5),
        'norm_mlp_g': gain((L, D_MODEL)),
        'w_up': nrm((L, D_MODEL, D_FF), D_MODEL ** -0.5),
        'w_down': nrm((L, D_FF, D_MODEL), D_FF ** -0.5),
    }


def reference(x, mem, norm_mix_g, w_in, b_igate, b_fgate, conv_w, conv_b, m_norm_g, dq_norm_g, dk_norm_g,
              lam_q1, lam_k1, lam_q2, lam_k2, subln_g, cq_norm_g, ck_norm_g, mem_norm_g, w_mem_kv,
              w_gate, b_gate, w_proj_m, w_proj_d, w_proj_c, w_out, norm_mlp_g, w_up, w_down):
    split_idx = np.cumsum(np.array(IN_SPLIT_SIZES))[:-1].tolist()
    bsz, seq, _ = x.shape
    for l in range(DEPTH):
        h = rms_norm(x, norm_mix_g[l])
        proj = h @ w_in[l]
        mq, mk, mv, mi, mf, mo, dq, dk, dv, cq = jnp.split(proj, split_idx, axis=-1)

        qk = jax.nn.silu(causal_depthwise_conv(jnp.concatenate([mq, mk], axis=-1), conv_w[l], conv_b[l]))
        mq, mk = jnp.split(qk, 2, axis=-1)
        hm = mlstm_chunkwise(mq.reshape(bsz, seq, M_HEADS, M_DQK), mk.reshape(bsz, seq, M_HEADS, M_DQK),
                             mv.reshape(bsz, seq, M_HEADS, M_DV), mi + b_igate[l], mf + b_fgate[l]).astype(x.dtype)
        y_m = jax.nn.sigmoid(mo) * rms_norm(hm, m_norm_g[l].reshape(M_HEADS, M_DV)).reshape(bsz, seq, M_V_W)

        lam_i = lambda_init(l)
        lam = (jnp.exp(jnp.sum(lam_q1[l].astype(jnp.float32) * lam_k1[l].astype(jnp.float32)))
               - jnp.exp(jnp.sum(lam_q2[l].astype(jnp.float32) * lam_k2[l].astype(jnp.float32))) + lam_i)
        y_d = diff_attention(dq.reshape(bsz, seq, D_HEADS, 2, D_HD), dk.reshape(bsz, seq, D_HEADS, 2, D_HD),
                             dv.reshape(bsz, seq, D_HEADS, D_DV), dq_norm_g[l], dk_norm_g[l], lam,
                             subln_g[l], lam_i)

        y_c = memory_cross_attention(cq, mem, mem_norm_g[l], w_mem_kv[l], cq_norm_g[l], ck_norm_g[l])

        g_m, g_d, g_c = jnp.split(jax.nn.sigmoid(h @ w_gate[l] + b_gate[l]), N_BRANCH, axis=-1)
        merged = g_m * (y_m @ w_proj_m[l]) + g_d * (y_d @ w_proj_d[l]) + g_c * (y_c @ w_proj_c[l])
        x = x + merged @ w_out[l]

        h2 = rms_norm(x, norm_mlp_g[l])
        x = x + jnp.square(jax.nn.relu(h2 @ w_up[l])) @ w_down[l]
    return x
```

```python
import math
import numpy as np
import concourse.bass as bass
import concourse.mybir as mybir
from concourse.bass_utils import run_bass_kernel_spmd
from contextlib import ExitStack

F32 = mybir.dt.float32
BF16 = mybir.dt.bfloat16
AF = mybir.ActivationFunctionType
ALU = mybir.AluOpType

COMPUTE = ("pe", "act", "dve", "pool")
ALLENG = COMPUTE + ("sp",)
EPS = 1e-6
ARENA_WORDS = 53000


class T:
    __slots__ = ("name", "h", "w", "r", "dkey", "dcnt", "g", "sub")

    def __init__(self, name, h):
        self.name = name
        self.h = h
        self.w = None
        self.r = {}
        self.dkey = None
        self.dcnt = 0
        self.g = None
        self.sub = None

    def __getitem__(self, k):
        return self.h[k]


class PV:
    __slots__ = ("h", "banks")

    def __init__(self, h, banks):
        self.h = h
        self.banks = banks

    def __getitem__(self, k):
        return self.h[k]


class Prog:
    def __init__(self, nc):
        self.nc = nc
        self.es = ExitStack()
        self.q = {e: [] for e in ALLENG}
        self.cnt = {e: 0 for e in COMPUTE}
        self.waited = {e: {} for e in ALLENG}
        self.sems = {}
        self.pe_pending = []
        self.dma_tiles = []
        self.nkeys = 0
        self.arena = self.es.enter_context(nc.sbuf_tensor("arena", [128, ARENA_WORDS], F32))
        self.pairs = [self.es.enter_context(nc.psum_tensor("pbank%d" % i, [128, 1024], F32)) for i in range(4)]
        self.banks = [self.pairs[i // 2][:, (i % 2) * 512:(i % 2 + 1) * 512] for i in range(8)]
        self.bankT = [T("bank%d" % i, None) for i in range(8)]
        self.top = 0
        self.persist_top = 0
        self.pcache = {}

    @staticmethod
    def _shape_view(ap, shape):
        if len(shape) == 2:
            return ap
        if len(shape) == 3:
            return ap.rearrange("p (a b) -> p a b", a=shape[1])
        if len(shape) == 4:
            return ap.rearrange("p (a b c) -> p a b c", a=shape[1], b=shape[2])
        raise ValueError(shape)

    def sb(self, name, shape, dt=F32):
        n = int(np.prod(shape[1:]))
        words = n if dt == F32 else (n + 1) // 2
        off = self.top
        self.top = off + ((words + 15) // 16) * 16
        assert self.top <= ARENA_WORDS, ("SBUF overflow", name, self.top)
        ap = self.arena[0:shape[0], off:off + words]
        if dt != F32:
            ap = ap.bitcast(dt)
            if n != 2 * words:
                ap = ap[:, 0:n]
        return T(name, self._shape_view(ap, shape))

    def pst(self, name, bank, lo, shape, dt=F32):
        n = int(np.prod(shape[1:]))
        words = n if dt == F32 else (n + 1) // 2
        assert lo + words <= 512
        key = (bank, lo, words, tuple(shape), dt)
        t = self.pcache.get(key)
        if t is not None:
            return t
        ap = self.banks[bank][0:shape[0], lo:lo + words]
        if dt != F32:
            ap = ap.bitcast(dt)
        t = PV(self._shape_view(ap, shape), (self.bankT[bank],))
        self.pcache[key] = t
        return t

    def pst2(self, pair):
        key = ("pair", pair)
        t = self.pcache.get(key)
        if t is None:
            t = PV(self.pairs[pair][:, :].rearrange("p (a b) -> p a b", a=2), (self.bankT[2 * pair], self.bankT[2 * pair + 1]))
            self.pcache[key] = t
        return t

    def dram(self, name, shape, dt, kind="Internal"):
        h = self.nc.dram_tensor(name, list(shape), dt, kind=kind)
        return T(name, h.ap())

    def _sem(self, key):
        s = self.sems.get(key)
        if s is None:
            s = self.es.enter_context(self.nc.semaphore("s_" + str(key)))
            self.sems[key] = s
        return s

    def _deps(self, eng, reads, writes):
        waits = {}
        wd = self.waited[eng]

        def need(k, v):
            if eng == "pe" and k == "pe":
                return
            if wd.get(k, 0) >= v:
                return
            if waits.get(k, 0) < v:
                waits[k] = v

        for t in reads:
            if t.w is not None:
                need(*t.w)
        for t in writes:
            if t.w is not None:
                need(*t.w)
            for k, v in t.r.items():
                need(k, v)
        for k, v in waits.items():
            wd[k] = v
        return list(waits.items())

    @staticmethod
    def _mark(tok, reads, writes):
        k, v = tok
        for t in reads:
            if t.r.get(k, 0) < v:
                t.r[k] = v
        for t in writes:
            t.w = tok
            t.r = {}

    def op(self, eng, fn, reads=(), writes=(), inc=True):
        r2, w2 = [], []
        for t in reads:
            if isinstance(t, PV):
                w2.extend(t.banks)
            else:
                r2.append(t)
        for t in writes:
            if isinstance(t, PV):
                w2.extend(t.banks)
            else:
                w2.append(t)
        reads, writes = r2, w2
        waits = self._deps(eng, reads, writes)
        if not inc:
            assert eng == "pe"
            self.q[eng].append((waits, fn, None))
            self.pe_pending.append((reads, writes))
            return
        self.cnt[eng] += 1
        tok = (eng, self.cnt[eng])
        self.q[eng].append((waits, fn, (eng, 1)))
        if eng == "pe" and self.pe_pending:
            for r, w in self.pe_pending:
                self._mark(tok, r, w)
            self.pe_pending = []
        self._mark(tok, reads, writes)

    def dma(self, queue, out, in_, reads=(), writes=(), on=None):
        if on is None:
            on = writes[0] if writes else reads[0]
        isgrp = on.h is None
        if on.sub is None:
            on.sub = {}
        if queue not in on.sub:
            on.sub[queue] = T(on.name + "_" + queue, None)
        on = on.sub[queue]
        if isgrp:
            for t in writes:
                t.g = on
        if on.dkey is None:
            self.nkeys += 1
            on.dkey = "d%d" % self.nkeys
            self.dma_tiles.append(on)
        waits = self._deps(queue, reads, [] if isgrp else writes)
        on.dcnt += 1
        tok = (on.dkey, 16 * on.dcnt)
        self.q[queue].append((waits, lambda e, o=out, i=in_: e.dma_start(out=o, in_=i), (on.dkey, 16)))
        self._mark(tok, reads, writes)
        return tok

    def seal(self, grp, tiles):
        for t in tiles:
            t.w = (t.g.dkey, 16 * t.g.dcnt)

    def barrier(self):
        assert not self.pe_pending
        toks = [(e, self.cnt[e]) for e in COMPUTE if self.cnt[e] > 0]
        toks += [(t.dkey, 16 * t.dcnt) for t in self.dma_tiles]
        for e in ALLENG:
            wd = self.waited[e]
            waits = []
            for k, v in toks:
                if k == e:
                    continue
                if wd.get(k, 0) < v:
                    waits.append((k, v))
                    wd[k] = v
            if waits:
                self.q[e].append((waits, None, None))
        self.dma_tiles = []
        self.pcache = {}

    def phase(self):
        self.barrier()
        self.top = self.persist_top

    def emit(self):
        nc = self.nc
        for e in ALLENG:
            for waits, fn, inc in self.q[e]:
                for k, _ in waits:
                    self._sem(k)
                if inc is not None:
                    self._sem(inc[0])
        engobj = {"pe": "tensor", "act": "scalar", "dve": "vector", "pool": "gpsimd", "sp": "sync"}
        with nc.Block() as block:
            for e in ALLENG:
                lst = self.q[e]
                if not lst:
                    continue

                def body(eng, lst=lst):
                    for waits, fn, inc in lst:
                        for k, v in waits:
                            eng.wait_ge(self.sems[k], v)
                        if fn is not None:
                            ins = fn(eng)
                            if inc is not None:
                                ins.then_inc(self.sems[inc[0]], inc[1])

                getattr(block, engobj[e])(body)


O_MQ, O_MK, O_MV, O_MI, O_MF, O_MO, O_DQ, O_DK, O_DV, O_CQ = 0, 512, 1024, 2048, 2052, 2056, 3080, 4104, 5128, 6152
IN_COLS = 6664
C_ID, C_TRI, C_BLK, C_PF, C_NB, C_AB, C_BD, C_ONE, C_TRIS, NCST = 0, 128, 256, 384, 385, 386, 387, 391, 519, 647
LAM_INIT = 0.8 - 0.6 * math.exp(-0.3 * 0)


def build(NT, upto=99, dbg=False):
    nc = bass.Bass("TRN2", target_bir_lowering=False)
    P = Prog(nc)
    NTA = 2 * NT
    NTL_ALL = NTA // 512
    NTL_OWN = NT // 512
    NTL_PRE = NT // 512
    NCH = NTA // 128
    NBLK = NTA // 128
    NPB = NT // 128
    SC = 128 ** -0.5

    def din(name, shape):
        return nc.dram_tensor(name, list(shape), F32, kind="ExternalInput").ap()

    x_own = din("x_own", [NT, 1024]); x_pre = din("x_pre", [NT, 1024]); mem = din("mem", [256, 1024])
    cst = din("cst", [128, NCST])
    norm_mix_g = din("norm_mix_g", [1, 1024]); w_in = din("w_in", [1024, IN_COLS])
    b_igate = din("b_igate", [4, 1]); b_fgate = din("b_fgate", [4, 1])
    conv_w = din("conv_w", [4, 1024]); conv_b = din("conv_b", [8, 128]); m_norm_g = din("m_norm_g", [1, 1024])
    dq_norm_g = din("dq_norm_g", [64, 1]); dk_norm_g = din("dk_norm_g", [64, 1])
    lam_q1 = din("lam_q1", [1, 64]); lam_k1 = din("lam_k1", [1, 64]); lam_q2 = din("lam_q2", [1, 64]); lam_k2 = din("lam_k2", [1, 64])
    subln_g = din("subln_g", [1, 128]); cq_norm_g = din("cq_norm_g", [128, 1]); ck_norm_g = din("ck_norm_g", [128, 1])
    mem_norm_g = din("mem_norm_g", [1, 1024]); w_mem_kv = din("w_mem_kv", [1024, 1536])
    w_gate = din("w_gate", [1024, 3072]); b_gate = din("b_gate", [1, 3072])
    w_proj_m = din("w_proj_m", [1024, 1024]); w_proj_d = din("w_proj_d", [1024, 1024]); w_proj_c = din("w_proj_c", [1024, 1024])
    w_out = din("w_out", [1024, 1024]); norm_mlp_g = din("norm_mlp_g", [1, 1024])
    w_up = din("w_up", [1024, 4096]); w_down = din("w_down", [4096, 1024])
    skind = "ExternalOutput" if dbg else "Internal"
    out = nc.dram_tensor("out", [NT, 1024], F32, kind="ExternalOutput").ap()

    HT = P.dram("HT", [NTL_ALL, 128, 8, 512], BF16, skind)
    YTm = P.dram("YTm", [NTL_OWN, 128, 8, 512], BF16, skind)
    YTd = P.dram("YTd", [NTL_OWN, 128, 8, 512], BF16, skind)
    YTc = P.dram("YTc", [NTL_OWN, 128, 8, 512], BF16, skind)
    KT = P.dram("KT", [8, 128, NTA], BF16, skind)
    QT = P.dram("QT", [8, 128, NT], BF16, skind)
    VA = P.dram("VA", [8, 128, NBLK, 129], BF16, skind)
    XM = P.dram("XM", [NT, 1024], F32, skind)

    def mm(outT, out_ap, pairs, reads, inc=True, start=True, skip=False):
        n = len(pairs)
        for i, (l, r) in enumerate(pairs):
            st = (i == 0) and start
            sp = (i == n - 1)
            if skip:
                f = lambda e, l=l, r=r, st=st, sp=sp: e.matmul(out=out_ap, lhsT=l, rhs=r, start=st, stop=sp, skip_group_check=True)
            else:
                f = lambda e, l=l, r=r, st=st, sp=sp: e.matmul(out=out_ap, lhsT=l, rhs=r, start=st, stop=sp)
            P.op("pe", f, reads=reads, writes=[outT], inc=(inc and i == n - 1))

    def tr(outT, out_ap, in_ap, ident_ap, reads, inc=True):
        P.op("pe", lambda e: e.transpose(out=out_ap, in_=in_ap, identity=ident_ap), reads=reads, writes=[outT], inc=inc)

    def act(out_ap, in_ap, func, reads, writes, **kw):
        P.op("act", lambda e: e.activation(out=out_ap, in_=in_ap, func=func, **kw), reads=reads, writes=writes)

    def amul(out_ap, in_ap, m_ap, reads, writes):
        P.op("act", lambda e: e.mul(out=out_ap, in_=in_ap, mul=m_ap), reads=reads, writes=writes)

    def tt(eng, out_ap, a, b, op, reads, writes):
        P.op(eng, lambda e: e.tensor_tensor(out=out_ap, in0=a, in1=b, op=op), reads=reads, writes=writes)

    def ts(eng, out_ap, a, s1, s2, op0, op1, reads, writes):
        if s2 is None:
            P.op(eng, lambda e: e.tensor_scalar(out=out_ap, in0=a, scalar1=s1, scalar2=None, op0=op0), reads=reads, writes=writes)
        else:
            P.op(eng, lambda e: e.tensor_scalar(out=out_ap, in0=a, scalar1=s1, scalar2=s2, op0=op0, op1=op1), reads=reads, writes=writes)

    def stt(eng, out_ap, a, s, b, op0, op1, reads, writes):
        P.op(eng, lambda e: e.scalar_tensor_tensor(out=out_ap, in0=a, scalar=s, in1=b, op0=op0, op1=op1), reads=reads, writes=writes)

    def cp(eng, out_ap, in_ap, reads, writes):
        if eng == "act":
            P.op("act", lambda e: e.copy(out=out_ap, in_=in_ap), reads=reads, writes=writes)
        else:
            P.op(eng, lambda e: e.tensor_copy(out=out_ap, in_=in_ap), reads=reads, writes=writes)

    def recip(out_ap, in_ap, reads, writes):
        P.op("dve", lambda e: e.reciprocal(out=out_ap, in_=in_ap), reads=reads, writes=writes)

    def mset(eng, ap, val, writes):
        P.op(eng, lambda e: e.memset(ap, val), writes=writes)

    def scan(out_T, out_ap, d0, d1_T, d1, init, op1, extra_reads=()):
        P.op("dve", lambda e: e.tensor_tensor_scan(out=out_ap, data0=d0, data1=d1, initial=init, op0=ALU.mult, op1=op1),
             reads=[d1_T] + list(extra_reads), writes=[out_T])

    def load_w(name, src, K, N, grp):
        t = P.sb(name, [128, K // 128, N], BF16)
        v = src.rearrange("(kc p) n -> p kc n", p=128)
        step = max(1, 2048 // N) if N <= 2048 else 1
        for kc in range(0, K // 128, step):
            k1 = min(K // 128, kc + step)
            for c0 in range(0, N, 2048):
                c1 = min(N, c0 + 2048)
                P.dma("pool", t[:, kc:k1, c0:c1], v[:, kc:k1, c0:c1], writes=[t], on=grp)
        return t

    def bload(name, src, n, grp, queue="sp"):
        t = P.sb(name, [128, n])
        P.dma(queue, t[:], src.partition_broadcast(128), writes=[t], on=grp)
        return t

    def rsqrt(out_ap, in_ap, scale, reads, writes):
        act(out_ap, in_ap, AF.Ln, reads, writes, scale=scale, bias=EPS)
        act(out_ap, out_ap, AF.Exp, writes, writes, scale=-0.5)

    def rms_rows(xt_T, x_ap_fn, nsub, gB, hb_T, hb_ap_fn, scr_T, ss_T, D=1024):
        for j in range(nsub):
            act(scr_T[:], x_ap_fn(j), AF.Square, [xt_T], [scr_T, ss_T], accum_out=ss_T[:, j:j + 1])
        rsqrt(ss_T[:, 0:nsub], ss_T[:, 0:nsub], 1.0 / D, [ss_T], [ss_T])
        for j in range(nsub):
            stt("dve", hb_ap_fn(j), x_ap_fn(j), ss_T[:, j:j + 1], gB[:], ALU.mult, ALU.mult, [xt_T, ss_T, gB], [hb_T])

    def fm_norm(psT, ps_ap, n, onesb, inv_d, gcol, outT, out_ap, sqb, ssp, rs):
        act(sqb[:, 0:n], ps_ap, AF.Square, [psT], [sqb])
        mm(ssp, ssp[:, 0:n], [(onesb[:], sqb[:, 0:n])], [onesb, sqb])
        rsqrt(rs[:, 0:n], ssp[:, 0:n], inv_d, [ssp], [rs])
        stt("dve", out_ap, ps_ap, gcol[:, 0:1], rs[:, 0:n], ALU.mult, ALU.mult, [psT, gcol, rs], [outT])

    grp0 = T("grp0", None)
    cs = P.sb("cst", [128, NCST])
    P.dma("sp", cs[:], cst, writes=[cs], on=grp0)
    gq2 = P.sb("gq2", [128, 1]); gk2 = P.sb("gk2", [128, 1]); gcq = P.sb("gcq", [128, 1]); gck = P.sb("gck", [128, 1])
    for hf in range(2):
        P.dma("sp", gq2[64 * hf:64 * hf + 64, :], dq_norm_g, writes=[gq2], on=grp0)
        P.dma("sp", gk2[64 * hf:64 * hf + 64, :], dk_norm_g, writes=[gk2], on=grp0)
    P.dma("sp", gcq[:], cq_norm_g, writes=[gcq], on=grp0)
    P.dma("sp", gck[:], ck_norm_g, writes=[gck], on=grp0)
    gsub = bload("gsub", subln_g, 128, grp0)
    identb = P.sb("identb", [128, 128], BF16)
    blkb = P.sb("blkb", [128, 128], BF16)
    oneb = P.sb("oneb", [128, 128], BF16)
    trib = P.sb("trib", [128, 128], BF16)
    lam = P.sb("lam", [128, 4])
    zcol = P.sb("zcol", [128, 1])
    P.persist_top = P.top
    lamv = [bload("lam%d" % i, a, 64, grp0) for i, a in enumerate((lam_q1, lam_k1, lam_q2, lam_k2))]
    ljunk = P.sb("ljunk", [128, 64])
    P.seal(grp0, [cs, gq2, gk2, gcq, gck, gsub] + lamv)
    cp("dve", identb[:], cs[:, C_ID:C_ID + 128], [cs], [identb])
    cp("dve", blkb[:], cs[:, C_BLK:C_BLK + 128], [cs], [blkb])
    cp("dve", oneb[:], cs[:, C_ONE:C_ONE + 128], [cs], [oneb])
    cp("dve", trib[:], cs[:, C_TRI:C_TRI + 128], [cs], [trib])
    mset("dve", zcol[:], 0.0, [zcol])
    for i in range(2):
        tt("dve", ljunk[:], lamv[2 * i][:], lamv[2 * i + 1][:], ALU.mult, [lamv[2 * i], lamv[2 * i + 1]], [ljunk])
        act(ljunk[:], ljunk[:], AF.Copy, [ljunk], [ljunk, lam], accum_out=lam[:, i:i + 1])
    act(lam[:, 0:2], lam[:, 0:2], AF.Exp, [lam], [lam])
    tt("dve", lam[:, 2:3], lam[:, 0:1], lam[:, 1:2], ALU.subtract, [lam], [lam])
    ts("dve", lam[:, 3:4], lam[:, 2:3], LAM_INIT, -1.0, ALU.add, ALU.mult, [lam], [lam])
    ts("dve", gsub[:], gsub[:], 1.0 - LAM_INIT, None, ALU.mult, None, [gsub], [gsub])

    P.phase()
    grp1 = T("grp1", None)
    gmix = bload("gmix", norm_mix_g, 1024, grp1)
    P.seal(grp1, [gmix])
    xts = [P.sb("xt%d" % i, [128, 4, 1024]) for i in range(2)]
    hbs = [P.sb("hb%d" % i, [128, 4, 1024], BF16) for i in range(2)]
    hTs = [P.sb("hT%d" % i, [128, 8, 512], BF16) for i in range(2)]
    sss = [P.sb("ss%d" % i, [128, 4]) for i in range(2)]
    scr1 = P.sb("scr1", [128, 1024])
    for t in range(NTL_ALL):
        b = t % 2
        src = x_pre[t * 512:(t + 1) * 512, :] if t < NTL_PRE else x_own[(t - NTL_PRE) * 512:(t - NTL_PRE + 1) * 512, :]
        xt, hb, hT, ss = xts[b], hbs[b], hTs[b], sss[b]
        P.dma("sp", xt[:], src.rearrange("(j p) d -> p j d", p=128), writes=[xt])
        rms_rows(xt, lambda j, xt=xt: xt[:, j, :], 4, gmix, hb, lambda j, hb=hb: hb[:, j, :], scr1, ss)
        for kc in range(8):
            pt = P.pst("pt1", kc % 8, 0, [128, 512], BF16)
            for j in range(4):
                tr(pt, pt[:, j * 128:(j + 1) * 128], hb[:, j, kc * 128:(kc + 1) * 128], identb[:], [hb, identb], inc=(j == 3))
            cp("act" if kc % 2 else "dve", hT[:, kc, :], pt[:], [pt], [hT])
        P.dma("pool", HT[t], hT[:], reads=[hT], on=hT)
    if upto <= 1:
        return finish(nc, P)

    P.phase()
    grp2 = T("grp2", None)
    w_i = load_w("w_i", w_in[:, O_MI:O_MI + 4], 1024, 4, grp2)
    w_f = load_w("w_f", w_in[:, O_MF:O_MF + 4], 1024, 4, grp2)
    bi = P.sb("bi", [4, 1]); bfn = P.sb("bfn", [4, 1])
    P.dma("sp", bi[:], b_igate, writes=[bi], on=grp2)
    P.dma("sp", bfn[:], b_fgate, writes=[bfn], on=grp2)
    cwl = P.sb("cwl", [4, 1024]); cbl = P.sb("cbl", [8, 128])
    P.dma("sp", cwl[:], conv_w, writes=[cwl], on=grp2)
    P.dma("sp", cbl[:], conv_b, writes=[cbl], on=grp2)
    cw = P.sb("cw", [128, 8, 4]); cb = P.sb("cb", [128, 8])
    gmn = bload("gmn", m_norm_g, 1024, grp2)
    w_q = load_w("w_q", w_in[:, O_MQ:O_MQ + 512], 1024, 512, grp2)
    w_k = load_w("w_k", w_in[:, O_MK:O_MK + 512], 1024, 512, grp2)
    w_v = load_w("w_v", w_in[:, O_MV:O_MV + 1024], 1024, 1024, grp2)
    w_o = load_w("w_o", w_in[:, O_MO:O_MO + 1024], 1024, 1024, grp2)
    P.seal(grp2, [w_i, w_f, bi, bfn, cwl, cbl, gmn, w_q, w_k, w_v, w_o])
    ts("dve", bfn[:], bfn[:], -1.0, None, ALU.mult, None, [bfn], [bfn])
    pcw = P.pst("pcw", 3, 0, [128, 8, 4]); pcb = P.pst("pcb", 3, 64, [128, 8])
    for g in range(8):
        tr(pcw, pcw[:, g, :], cwl[0:4, g * 128:(g + 1) * 128], cs[0:4, C_ID:C_ID + 4], [cwl, cs], inc=(g == 7))
    tr(pcb, pcb[:], cbl[0:8, :], cs[0:8, C_ID:C_ID + 8], [cbl, cs])
    cp("dve", cw[:], pcw[:], [pcw], [cw])
    cp("dve", cb[:], pcb[:], [pcb], [cb])
    hTg = [P.sb("hTg%d" % i, [128, 8, 512], BF16) for i in range(2)]
    decB = P.sb("decB", [128, 4, NCH]); decS = P.sb("decS", [128, 4, NCH])
    WT = P.sb("WT", [128, NCH, 4]); FT = P.sb("FT", [128, NCH, 4])
    GE = P.sb("GE", [4, NCH]); GP = P.sb("GP", [4, NCH]); dec = P.sb("dec", [4, NCH]); dbd = P.sb("dbd", [4, 4, NCH])
    carB = P.sb("carB", [4, 1]); carG = P.sb("carG", [4, 1])
    top_2a = P.top

    GB = min(1024, NTA)
    NGB = NTA // GB
    CPB = GB // 128
    I_b = P.sb("I_b", [4, GB]); E_b = P.sb("E_b", [4, GB]); ones4 = P.sb("ones4", [4, GB])
    Bn = P.sb("Bn", [4, GB]); A_b = P.sb("A_b", [4, GB]); G_b = P.sb("G_b", [4, GB])
    mset("pool", ones4[:], 1.0, [ones4])
    mset("dve", carB[:], 0.0, [carB]); mset("dve", carG[:], 0.0, [carG])
    pwt = P.pst("pwt", 1, 0, [128, 4 * NCH]); pft = P.pst("pft", 2, 0, [128, 4 * NCH])
    v3 = lambda T_: T_[:].rearrange("p (c l) -> p c l", l=128)
    for gb in range(NGB):
        for tb in range(GB // 512):
            t = gb * (GB // 512) + tb
            hT = hTg[t % 2]
            P.dma("sp", hT[:], HT[t], writes=[hT])
            pgi = P.pst("pgi", 4 + 2 * (t % 2), 0, [4, 512]); pgf = P.pst("pgf", 5 + 2 * (t % 2), 0, [4, 512])
            mm(pgi, pgi[:], [(w_i[:, kc, :], hT[:, kc, :]) for kc in range(8)], [w_i, hT])
            mm(pgf, pgf[:], [(w_f[:, kc, :], hT[:, kc, :]) for kc in range(8)], [w_f, hT])
            ts("dve", I_b[:, tb * 512:(tb + 1) * 512], pgi[:], bi[:, 0:1], None, ALU.add, None, [pgi, bi], [I_b])
            act(E_b[:, tb * 512:(tb + 1) * 512], pgf[:], AF.Exp, [pgf, bfn], [E_b], scale=-1.0, bias=bfn[:, 0:1])
        npre = max(0, min(GB, NT - gb * GB))
        act(E_b[:], E_b[:], AF.Ln, [E_b], [E_b], bias=1.0)
        if npre:
            ts("dve", E_b[:, 0:npre], E_b[:, 0:npre], cs[0:4, C_PF:C_PF + 1], None, ALU.mult, None, [E_b, cs], [E_b])
        scan(Bn, Bn[:], ones4[:], E_b, E_b[:], carB[:, 0:1], ALU.add, [ones4, carB])
        cp("dve", carB[:], Bn[:, GB - 1:GB], [Bn], [carB])
        tt("dve", A_b[:], I_b[:], Bn[:], ALU.add, [I_b, Bn], [A_b])
        if npre:
            ts("dve", A_b[:, 0:npre], A_b[:, 0:npre], cs[0:4, C_NB:C_NB + 1], None, ALU.add, None, [A_b, cs], [A_b])
        scan(G_b, G_b[:], ones4[:], A_b, A_b[:], carG[:, 0:1], ALU.max, [ones4, carG])
        cp("dve", carG[:], G_b[:, GB - 1:GB], [G_b], [carG])
        Gend = v3(G_b)[:, :, 127:128]
        tt("dve", v3(I_b), v3(A_b), Gend.to_broadcast([4, CPB, 128]), ALU.subtract, [A_b, G_b], [I_b])
        act(I_b[:], I_b[:], AF.Exp, [I_b], [I_b])
        tt("dve", v3(E_b), v3(Bn), Gend.to_broadcast([4, CPB, 128]), ALU.subtract, [Bn, G_b], [E_b])
        act(E_b[:], E_b[:], AF.Exp, [E_b], [E_b])
        cp("dve", GE[:, gb * CPB:(gb + 1) * CPB], v3(G_b)[:, :, 127], [G_b], [GE])
        for cl in range(CPB):
            c = gb * CPB + cl
            tr(pwt, pwt[:, 4 * c:4 * c + 4], I_b[0:4, cl * 128:(cl + 1) * 128], cs[0:4, C_ID:C_ID + 4], [I_b, cs], inc=(cl == CPB - 1))
        for cl in range(CPB):
            c = gb * CPB + cl
            tr(pft, pft[:, 4 * c:4 * c + 4], E_b[0:4, cl * 128:(cl + 1) * 128], cs[0:4, C_ID:C_ID + 4], [E_b, cs], inc=(cl == CPB - 1))
    cp("dve", WT[:].rearrange("p a b -> p (a b)"), pwt[:], [pwt], [WT])
    cp("act", FT[:].rearrange("p a b -> p (a b)"), pft[:], [pft], [FT])
    mset("dve", GP[:, 0:1], 0.0, [GP])
    cp("dve", GP[:, 1:NCH], GE[:, 0:NCH - 1], [GE], [GP])
    tt("dve", dec[:], GP[:], GE[:], ALU.subtract, [GP, GE], [dec])
    act(dec[:], dec[:], AF.Exp, [dec], [dec])
    for h in range(4):
        ts("dve", dbd[:, h, :], dec[:], cs[0:4, C_BD + h:C_BD + h + 1], None, ALU.mult, None, [dec, cs], [dbd])
    pdb = P.pst("pdb", 0, 0, [128, 4 * NCH])
    mm(pdb, pdb[:], [(cs[0:4, C_ONE:C_ONE + 128], dbd[:].rearrange("p a b -> p (a b)"))], [cs, dbd])
    cp("dve", decB[:].rearrange("p a b -> p (a b)"), pdb[:], [pdb], [decB])
    ts("dve", decS[:], decB[:], SC, None, ALU.mult, None, [decB], [decS])

    P.barrier()
    P.top = top_2a
    Cst = [P.sb("C%d" % h, [128, 257]) for h in range(4)]
    Cd = [P.sb("Cd%d" % h, [128, 257], BF16) for h in range(4)]
    for h in range(4):
        mset("dve", Cst[h][:], 0.0, [Cst[h]])
        mset("pool", Cd[h][:], 0.0, [Cd[h]])
    XP = [[P.sb("XP%d%d" % (g, h), [128, 515]) for h in range(4)] for g in range(2)]
    for g in range(2):
        for h in range(4):
            mset("pool", XP[g][h][:, 0:3], 0.0, [XP[g][h]])
    cv = [P.sb("cv%d" % i, [128, 512]) for i in range(2)]
    qkT = [[P.sb("qkT%d%d" % (g, i), [128, 4, 512], BF16) for i in range(2)] for g in range(2)]
    vw = [P.sb("vw%d" % i, [128, 4, 257], BF16) for i in range(2)]
    og = P.sb("og", [128, 1024])
    ktok = [P.sb("ktok%d" % i, [128, 4, 128], BF16) for i in range(2)]
    SM = [P.sb("SM%d" % i, [128, 4, 128], BF16) for i in range(2)]
    hm = [P.sb("hm%d" % i, [128, 4, 256]) for i in range(2)]
    st2 = [P.sb("st2_%d" % i, [128, 8]) for i in range(2)]
    sq2 = P.sb("sq2", [128, 256])
    t2 = P.sb("t2", [128, 1024])
    ymb = P.sb("ymb", [128, 1024], BF16)
    ymT = [P.sb("ymT%d" % i, [128, 8, 512], BF16) for i in range(2)]
    pj = [0]
    pend_tail = [None]

    def proj_group(t, g, h):
        hT = hTg[t % 2]
        wsel = w_k if g == 0 else w_q
        dst = qkT[g][t % 2]
        pp = P.pst("pp", pj[0] % 2, 0, [128, 512]); pj[0] += 1
        mm(pp, pp[:], [(wsel[:, kc, h * 128:(h + 1) * 128], hT[:, kc, :]) for kc in range(8)], [wsel, hT])
        xp = XP[g][h]
        ch = (1 - g) * 4 + h
        cp("act", xp[:, 3:515], pp[:], [pp], [xp])
        c_ = cv[(g * 4 + h) % 2]
        ts("dve", c_[:], xp[:, 0:512], cw[:, ch, 0:1], cb[:, ch:ch + 1], ALU.mult, ALU.add, [xp, cw, cb], [c_])
        for j in range(1, 4):
            stt("dve", c_[:], xp[:, j:j + 512], cw[:, ch, j:j + 1], c_[:], ALU.mult, ALU.add, [xp, cw, c_], [c_])
        act(dst[:, h, :], c_[:], AF.Silu, [c_], [dst])
        cp("pool", xp[:, 0:3], xp[:, 512:515], [xp], [xp])

    def groups_of(t):
        gs = (0, 1) if (t >= NTL_PRE or t == NTL_PRE - 1) else (0,)
        return [(t, g, h) for g in gs for h in range(4)]

    P.dma("sp", hTg[0][:], HT[0], writes=[hTg[0]])
    for a_ in groups_of(0):
        proj_group(*a_)
    for t in range(NTL_ALL):
        own = t >= NTL_PRE
        hT = hTg[t % 2]
        kT = qkT[0][t % 2]; qT = qkT[1][t % 2]
        nxt = []
        if t + 1 < NTL_ALL:
            P.dma("sp", hTg[(t + 1) % 2][:], HT[t + 1], writes=[hTg[(t + 1) % 2]])
            nxt = groups_of(t + 1)
        per = (len(nxt) + 3) // 4
        ymT_t = ymT[t % 2]
        for cc in range(4):
            c = 4 * t + cc
            par = c % 2
            cols = slice(cc * 128, (cc + 1) * 128)
            vw_c = vw[par]; ktok_c = ktok[par]
            for g in range(2):
                pv = P.pst("pv", 2 + g, 0, [128, 512])
                mm(pv, pv[:], [(hT[:, kc, cols], w_v[:, kc, g * 512:(g + 1) * 512]) for kc in range(8)], [hT, w_v])
                tt("dve", vw_c[:, 2 * g:2 * g + 2, 0:256], pv[:].rearrange("p (a b) -> p a b", a=2),
                   WT[:, c, 2 * g:2 * g + 2].unsqueeze(2).to_broadcast([128, 2, 256]), ALU.mult, [pv, WT], [vw_c])
            cp("pool", vw_c[:, :, 256], WT[:, c, :], [WT], [vw_c])
            ptk = P.pst("ptk", 5, 256, [128, 4, 128], BF16)
            for h in range(4):
                tr(ptk, ptk[:, h, :], kT[:, h, cols], identb[:], [kT, identb], inc=(h == 3))
            cp("act", ktok_c[:], ptk[:], [ptk], [ktok_c])
            if own:
                for g in range(2):
                    po = P.pst("pp", pj[0] % 2, 0, [128, 512]); pj[0] += 1
                    mm(po, po[:], [(hT[:, kc, cols], w_o[:, kc, g * 512:(g + 1) * 512]) for kc in range(8)], [hT, w_o])
                    act(og[:, g * 512:(g + 1) * 512], po[:], AF.Sigmoid, [po], [og])
                pS = P.pst("pS", 4, 0, [128, 4, 128])
                for h in range(4):
                    mm(pS, pS[:, h, :], [(kT[:, h, cols], qT[:, h, cols])], [kT, qT], inc=(h == 3))
                SM_c = SM[par]
                tt("dve", SM_c[:], pS[:], cs[:, C_TRIS:C_TRIS + 128].unsqueeze(1).to_broadcast([128, 4, 128]), ALU.mult, [pS, cs], [SM_c])
                hm_c = hm[par]; st_c = st2[par]
                if pend_tail[0] is not None:
                    pend_tail[0]()
                    pend_tail[0] = None
            for h in range(4):
                if own:
                    pn = P.pst("pn", 6 + h % 2, 0, [128, 257])
                    mm(pn, pn[:], [(SM_c[:, h, :], vw_c[:, h, :]), (qT[:, h, cols], Cd[h][:])], [SM_c, vw_c, qT, Cd[h]])
                    act(st_c[:, h:h + 1], pn[:, 256:257], AF.Abs, [pn], [st_c])
                    ts("dve", st_c[:, h:h + 1], st_c[:, h:h + 1], FT[:, c, h:h + 1], None, ALU.max, None, [st_c, FT], [st_c])
                    recip(st_c[:, h:h + 1], st_c[:, h:h + 1], [st_c], [st_c])
                    amul(hm_c[:, h, :], pn[:, 0:256], st_c[:, h:h + 1], [pn, st_c], [hm_c])
                    act(sq2[:], hm_c[:, h, :], AF.Square, [hm_c], [sq2, st_c], accum_out=st_c[:, 4 + h:5 + h])
                pu = P.pst("pu", 6 + (h + 1) % 2, 0, [128, 257])
                mm(pu, pu[:], [(ktok_c[:, h, :], vw_c[:, h, :])], [ktok_c, vw_c])
                stt("dve", Cst[h][:], Cst[h][:], decB[:, h, c:c + 1], pu[:], ALU.mult, ALU.add, [Cst[h], decB, pu], [Cst[h]])
                if c + 1 < NCH:
                    amul(Cd[h][:], Cst[h][:], decS[:, h, c + 1:c + 2], [Cst[h], decS], [Cd[h]])
            if own:
                rsqrt(st_c[:, 4:8], st_c[:, 4:8], 1.0 / 256, [st_c], [st_c])
                tt("pool", t2[:], gmn[:], og[:], ALU.mult, [gmn, og], [t2])
                tt("dve", hm_c[:], hm_c[:], st_c[:, 4:8].unsqueeze(2).to_broadcast([128, 4, 256]), ALU.mult, [hm_c, st_c], [hm_c])
                tt("dve", ymb[:], hm_c[:].rearrange("p a b -> p (a b)"), t2[:], ALU.mult, [hm_c, t2], [ymb])

                def tail(ymT_t=ymT_t, cols=cols):
                    for half in range(2):
                        pty = P.pst("pty", 5, 0, [128, 4, 128], BF16)
                        for k4 in range(4):
                            kc = half * 4 + k4
                            tr(pty, pty[:, k4, :], ymb[:, kc * 128:(kc + 1) * 128], identb[:], [ymb, identb], inc=(k4 == 3))
                        cp("act" if half else "dve", ymT_t[:, half * 4:half * 4 + 4, cols], pty[:], [pty], [ymT_t])
                if cc == 3:
                    tail()
                else:
                    pend_tail[0] = tail
            for a_ in nxt[cc * per:(cc + 1) * per]:
                proj_group(*a_)
        if own:
            P.dma("pool", YTm[t - NTL_PRE], ymT_t[:], reads=[ymT_t], on=ymT_t)
    if upto <= 2:
        return finish(nc, P)

    P.phase()
    grp3 = T("grp3", None)
    w_dk = load_w("w_dk", w_in[:, O_DK:O_DK + 1024], 1024, 1024, grp3)
    w_dv = load_w("w_dv", w_in[:, O_DV:O_DV + 1024], 1024, 1024, grp3)
    w_dq = load_w("w_dq", w_in[:, O_DQ:O_DQ + 1024], 1024, 1024, grp3)
    P.seal(grp3, [w_dk, w_dv, w_dq])
    hT3 = [P.sb("hT3_%d" % i, [128, 8, 512], BF16) for i in range(2)]
    kt3 = [P.sb("kt3_%d" % i, [128, 8, 512], BF16) for i in range(2)]
    qt3 = [P.sb("qt3_%d" % i, [128, 8, 512], BF16) for i in range(2)]
    vt3 = [P.sb("vt3_%d" % i, [128, 4, 8, 129], BF16) for i in range(2)]
    sq3 = [P.sb("sq3_%d" % i, [128, 512], BF16) for i in range(2)]
    rs3 = [P.sb("rs3_%d" % i, [128, 512]) for i in range(2)]
    for i in range(2):
        mset("pool", vt3[i][:, :, :, 128:129], 1.0, [vt3[i]])
    n3 = 0
    for t in range(NTL_ALL):
        own = t >= NTL_PRE
        hT = hT3[t % 2]
        P.dma("sp", hT[:], HT[t], writes=[hT])
        for (wsel, gcol, dstl, dram_T, tcol, enabled) in ((w_dk, gk2, kt3, KT, t * 512, True), (w_dq, gq2, qt3, QT, (t - NTL_PRE) * 512, own)):
            if not enabled:
                continue
            dst = dstl[t % 2]
            pend = []
            for h in range(10):
                if h < 8:
                    pk = P.pst("pk3", n3 % 4, 0, [128, 512])
                    mm(pk, pk[:], [(wsel[:, kc, h * 128:(h + 1) * 128], hT[:, kc, :]) for kc in range(8)], [wsel, hT])
                    pend.append((pk, h, n3))
                    n3 += 1
                if h >= 2:
                    pk_, h_, n_ = pend.pop(0)
                    pss = P.pst("pss3", 4 + n_ % 2, 0, [128, 512])
                    fm_norm(pk_, pk_[:], 512, blkb, 1.0 / 64, gcol, dst, dst[:, h_, :], sq3[n_ % 2], pss, rs3[n_ % 2])
            P.dma("pool", dram_T[:, :, tcol:tcol + 512].rearrange("h p n -> p h n"), dst[:], reads=[dst], on=dst)
        vt = vt3[t % 2]
        for j in range(4):
            for g in range(2):
                pv = P.pst("pv3", 6 + g, 0, [128, 512])
                mm(pv, pv[:], [(hT[:, kc, j * 128:(j + 1) * 128], w_dv[:, kc, g * 512:(g + 1) * 512]) for kc in range(8)], [hT, w_dv])
                cp("act" if g else "dve", vt[:, j, 4 * g:4 * g + 4, 0:128], pv[:].rearrange("p (a b) -> p a b", a=4), [pv], [vt])
            P.dma("pool", VA[:, :, 4 * t + j, :].rearrange("h p e -> p h e"), vt[:, j, :, :], reads=[vt], on=vt)

    P.phase()
    kh = [[P.sb("kh%d_%d" % (i, c), [128, NTA], BF16) for c in range(2)] for i in range(2)]
    for i in range(2):
        mset("pool", kh[i][0][64:128, :], 0.0, [kh[i][0]])
        mset("pool", kh[i][1][0:64, :], 0.0, [kh[i][1]])
    vh = [P.sb("vh%d" % i, [128, NBLK, 129], BF16) for i in range(2)]
    qh = [P.sb("qh%d" % i, [128, NT], BF16) for i in range(2)]
    NST = 4
    LOOK = 3
    NPT = 6
    pTb = [P.sb("pT%d" % i, [128, 512], BF16) for i in range(NPT)]
    rr = [P.sb("rr%d" % i, [128, 12]) for i in range(2)]
    od = [P.sb("od%d" % i, [128, 4, 128]) for i in range(2)]
    odb = [P.sb("odb%d" % i, [128, 4, 128], BF16) for i in range(2)]
    sq4 = P.sb("sq4", [128, 128])
    ydT = [P.sb("ydT%d" % i, [128, 512], BF16) for i in range(2)]
    stb = [P.pst("st%d" % i, i, 0, [128, 512]) for i in range(NST)]
    accb = [P.pst("accb%d" % i, 5 + i, 0, [128, 512]) for i in range(3)]
    accs = {}
    for c in range(2):
        for i in range(4):
            n = c * 4 + i
            accs[(c, i)] = (accb[n // 3], (n % 3) * 129)
    ptd = P.pst("ptd", 4, 0, [128, 512], BF16)
    accS = [P.sb("accS%d" % i, [128, 8, 129]) for i in range(2)]
    nq = 0
    pending_e2 = None

    def zero_acc():
        mset("dve", accb[0][:, 0:387], 0.0, [accb[0]])
        mset("dve", accb[1][:, 0:387], 0.0, [accb[1]])
        mset("dve", accb[2][:, 0:258], 0.0, [accb[2]])
    zero_acc()
    for h in range(8):
        k_h, v_h, q_h = kh[h % 2], vh[h % 2], qh[h % 2]
        P.dma("sp", k_h[0][0:64, :], KT[h][0:64, :], writes=[k_h[0]])
        P.dma("sp", k_h[1][64:128, :], KT[h][64:128, :], writes=[k_h[1]])
        P.dma("sp", v_h[:], VA[h], writes=[v_h])
        P.dma("sp", q_h[:], QT[h], writes=[q_h])
        for t in range(NTL_OWN):
            nkb = NPB + 4 * t + 4
            nun = 2 * nkb
            par = nq % 2
            r_ = rr[par]; od_ = od[par]; ob_ = odb[par]; aS = accS[par]

            def s_mm(u):
                kb, c = u // 2, u % 2
                i0 = max(0, kb - (NPB + 4 * t))
                st = stb[u % NST]
                mm(st, st[:, i0 * 128:512], [(k_h[c][:, kb * 128:(kb + 1) * 128],
                                               q_h[:, t * 512 + i0 * 128:(t + 1) * 512])], [k_h[c], q_h])
            for u in range(min(LOOK, nun)):
                s_mm(u)
            for u in range(nun):
                if u == 6 and pending_e2 is not None:
                    pending_e2()
                    pending_e2 = None
                if u + LOOK < nun:
                    s_mm(u + LOOK)
                kb, c = u // 2, u % 2
                i0 = max(0, kb - (NPB + 4 * t))
                diag = kb >= NPB + 4 * t
                bias_ap = cs[:, C_AB:C_AB + 1] if kb < NPB else zcol[:, 0:1]
                st = stb[u % NST]
                pT = pTb[u % NPT]
                act(pT[:, i0 * 128:512], st[:, i0 * 128:512], AF.Exp, [st, cs, zcol], [pT], scale=0.125, bias=bias_ap)
                if diag:
                    tt("dve", pT[:, i0 * 128:(i0 + 1) * 128], pT[:, i0 * 128:(i0 + 1) * 128], trib[:], ALU.mult, [pT, trib], [pT])
                for i in range(i0, 4):
                    a, lo = accs[(c, i)]
                    mm(a, a[:, lo:lo + 129], [(pT[:, i * 128:(i + 1) * 128], v_h[:, kb, :])], [pT, v_h], start=False, skip=True,
                       inc=(i == 3))
            cp("dve", aS[:, 0:3, :], accb[0][:, 0:387].rearrange("p (a b) -> p a b", a=3), [accb[0]], [aS])
            cp("act", aS[:, 3:6, :], accb[1][:, 0:387].rearrange("p (a b) -> p a b", a=3), [accb[1]], [aS])
            cp("dve", aS[:, 6:8, :], accb[2][:, 0:258].rearrange("p (a b) -> p a b", a=2), [accb[2]], [aS])
            zero_acc()
            recip(r_[:, 0:8], aS[:, :, 128], [aS], [r_])
            ts("dve", r_[:, 4:8], r_[:, 4:8], lam[:, 3:4], None, ALU.mult, None, [r_, lam], [r_])
            for i in range(4):
                ts("dve", od_[:, i, :], aS[:, i, 0:128], r_[:, i:i + 1], None, ALU.mult, None, [aS, r_], [od_])
                stt("dve", od_[:, i, :], aS[:, 4 + i, 0:128], r_[:, 4 + i:5 + i], od_[:, i, :], ALU.mult, ALU.add, [aS, r_, od_], [od_])
            yd = ydT[par]; nq += 1

            def e2(r_=r_, od_=od_, ob_=ob_, yd=yd, t=t, h=h):
                for i in range(4):
                    P.op("dve", lambda e, i=i: e.scalar_tensor_tensor(out=sq4[:], in0=od_[:, i, :], scalar=1.0, in1=od_[:, i, :],
                                                                       op0=ALU.mult, op1=ALU.mult, accum_out=r_[:, 8 + i:9 + i]),
                         reads=[od_], writes=[sq4, r_])
                rsqrt(r_[:, 8:12], r_[:, 8:12], 1.0 / 128, [r_], [r_])
                for i in range(4):
                    stt("dve", ob_[:, i, :], od_[:, i, :], r_[:, 8 + i:9 + i], gsub[:], ALU.mult, ALU.mult, [od_, r_, gsub], [ob_])
                    tr(ptd, ptd[:, i * 128:(i + 1) * 128], ob_[:, i, :], identb[:], [ob_, identb], inc=(i == 3))
                cp("dve", yd[:], ptd[:], [ptd], [yd])
                P.dma("pool", YTd[t][:, h, :], yd[:], reads=[yd], on=yd)
            pending_e2 = e2
    if pending_e2 is not None:
        pending_e2()
    if upto <= 3:
        return finish(nc, P)

    P.phase()
    grp4 = T("grp4", None)
    w_cq = load_w("w_cq", w_in[:, O_CQ:O_CQ + 512], 1024, 512, grp4)
    wkv = load_w("wkv", w_mem_kv, 1024, 1536, grp4)
    gmem = bload("gmem", mem_norm_g, 1024, grp4)
    memt = P.sb("memt", [128, 2, 1024])
    P.dma("sp", memt[:], mem.rearrange("(j p) d -> p j d", p=128), writes=[memt], on=grp4)
    P.seal(grp4, [w_cq, wkv, gmem, memt])
    mkT = P.sb("mkT", [128, 4, 256], BF16)
    mva = P.sb("mva", [128, 2, 4, 257], BF16)
    hT4 = [P.sb("hT4_%d" % i, [128, 8, 512], BF16) for i in range(2)]
    qc = [P.sb("qc%d" % i, [128, 512], BF16) for i in range(2)]
    sq5 = [P.sb("sq5_%d" % i, [128, 512], BF16) for i in range(2)]
    rs5 = [P.sb("rs5_%d" % i, [128, 512]) for i in range(2)]
    pc = [[P.sb("pc%d_%d" % (i, mb), [128, 512], BF16) for mb in range(2)] for i in range(2)]
    rc = [P.sb("rc%d" % i, [128, 1]) for i in range(2)]
    ycb = [P.sb("ycb%d" % i, [128, 256], BF16) for i in range(2)]
    ycT = [P.sb("ycT%d" % i, [128, 8, 512], BF16) for i in range(2)]
    pq4 = [P.pst("pq4", i, 0, [128, 512]) for i in range(2)]
    pss4 = P.pst("pss4", 2, 0, [128, 512])
    ps4 = [P.pst("ps4", 3 + i, 0, [128, 512]) for i in range(2)]
    mscr = P.sb("mscr", [128, 1024]); mss = P.sb("mss", [128, 4])
    mhb = P.sb("mhb", [128, 2, 1024], BF16)
    mhT = P.sb("mhT", [128, 8, 256], BF16)
    rms_rows(memt, lambda j: memt[:, j, :], 2, gmem, mhb, lambda j: mhb[:, j, :], mscr, mss)
    for kc in range(8):
        pt = P.pst("pt_m", 7, 256 + 128 * (kc % 2), [128, 256], BF16)
        for j in range(2):
            tr(pt, pt[:, j * 128:(j + 1) * 128], mhb[:, j, kc * 128:(kc + 1) * 128], identb[:], [mhb, identb], inc=(j == 1))
        cp("act" if kc % 2 else "dve", mhT[:, kc, :], pt[:], [pt], [mhT])
    for h in range(4):
        pk = pq4[h % 2]
        mm(pk, pk[:, 0:256], [(wkv[:, kc, h * 128:(h + 1) * 128], mhT[:, kc, :]) for kc in range(8)], [wkv, mhT])
        fm_norm(pk, pk[:, 0:256], 256, oneb, 1.0 / 128, gck, mkT, mkT[:, h, :], sq5[h % 2], pss4, rs5[h % 2])
    mset("dve", mva[:, :, :, 256:257], 1.0, [mva])
    for j in range(2):
        for g in range(2):
            pv = ps4[g]
            mm(pv, pv[:], [(mhT[:, kc, j * 128:(j + 1) * 128], wkv[:, kc, 512 + g * 512:512 + (g + 1) * 512]) for kc in range(8)], [mhT, wkv])
            cp("act" if g else "dve", mva[:, j, 2 * g:2 * g + 2, 0:256], pv[:].rearrange("p (a b) -> p a b", a=2), [pv], [mva])
    n4 = 0
    for t in range(NTL_OWN):
        hT = hT4[t % 2]
        P.dma("sp", hT[:], HT[NTL_PRE + t], writes=[hT])
        yT = ycT[t % 2]
        pq_next = pq4[n4 % 2]
        mm(pq_next, pq_next[:], [(w_cq[:, kc, 0:128], hT[:, kc, :]) for kc in range(8)], [w_cq, hT])
        for h in range(4):
            pq = pq_next
            if h < 3:
                pq_next = pq4[(n4 + 1) % 2]
                mm(pq_next, pq_next[:], [(w_cq[:, kc, (h + 1) * 128:(h + 2) * 128], hT[:, kc, :]) for kc in range(8)], [w_cq, hT])
            q_ = qc[n4 % 2]
            fm_norm(pq, pq[:], 512, oneb, 1.0 / 128, gcq, q_, q_[:], sq5[n4 % 2], pss4, rs5[n4 % 2])
            for mb in range(2):
                ps_ = ps4[mb]
                mm(ps_, ps_[:], [(mkT[:, h, mb * 128:(mb + 1) * 128], q_[:])], [mkT, q_])
                act(pc[n4 % 2][mb][:], ps_[:], AF.Exp, [ps_], [pc[n4 % 2][mb]], scale=SC)
            for i in range(4):
                pa = P.pst("pa4", 5 + i % 2, 0, [128, 257])
                mm(pa, pa[:], [(pc[n4 % 2][mb][:, i * 128:(i + 1) * 128], mva[:, mb, h, :]) for mb in range(2)], [pc[n4 % 2][0], pc[n4 % 2][1], mva])
                r_ = rc[i % 2]; yb = ycb[i % 2]
                recip(r_[:], pa[:, 256:257], [pa], [r_])
                amul(yb[:], pa[:, 0:256], r_[:, 0:1], [pa, r_], [yb])
                pty = P.pst("pty4", 7, (i % 2) * 128, [128, 2, 128], BF16)
                for k2 in range(2):
                    tr(pty, pty[:, k2, :], yb[:, k2 * 128:(k2 + 1) * 128], identb[:], [yb, identb], inc=(k2 == 1))
                cp("dve", yT[:, 2 * h:2 * h + 2, i * 128:(i + 1) * 128], pty[:], [pty], [yT])
            n4 += 1
        P.dma("pool", YTc[t], yT[:], reads=[yT], on=yT)
    if upto <= 4:
        return finish(nc, P)

    P.phase()
    grp5 = T("grp5", None)
    w_g = load_w("w_g", w_gate, 1024, 3072, grp5)
    w_pj = [load_w("w_pj%d" % i, w, 1024, 1024, grp5) for i, w in enumerate((w_proj_m, w_proj_d, w_proj_c))]
    w_ot = load_w("w_ot", w_out, 1024, 1024, grp5)
    bgb = P.sb("bgb", [1, 3072], BF16)
    P.dma("pool", bgb[:], b_gate, writes=[bgb], on=grp5)
    P.seal(grp5, [w_g, w_ot, bgb] + w_pj)
    hT5 = [P.sb("hT5_%d" % i, [128, 8, 512], BF16) for i in range(2)]
    yT5 = [P.sb("yT5_%d" % b, [128, 8, 512], BF16) for b in range(3)]
    x5 = [P.sb("x5_%d" % i, [128, 1024]) for i in range(2)]
    gs5 = [P.sb("gs5_%d" % i, [128, 512]) for i in range(2)]
    tm5 = [P.sb("tm5_%d" % i, [128, 512]) for i in range(2)]
    mg5 = P.sb("mg5", [128, 1024])
    mgb = [P.sb("mgb%d" % i, [128, 1024], BF16) for i in range(2)]
    mT5 = [P.sb("mT5_%d" % i, [128, 8, 128], BF16) for i in range(2)]
    YTs = (YTm, YTd, YTc)
    n5 = 0
    for t in range(NTL_OWN):
        hT = hT5[t % 2]
        P.dma("sp", hT[:], HT[NTL_PRE + t], writes=[hT])
        for b in range(3):
            P.dma("sp", yT5[b][:], YTs[b][t], writes=[yT5[b]])
        for j in range(4):
            sub = t * 4 + j
            cols = slice(j * 128, (j + 1) * 128)
            x_ = x5[sub % 2]; mg = mg5; mb_ = mgb[sub % 2]
            P.dma("sp", x_[:], x_own[sub * 128:(sub + 1) * 128, :], writes=[x_])
            for b in range(3):
                yT = yT5[b]
                for g in range(2):
                    pg = P.pst("pg5", n5 % 2, 0, [128, 512])
                    pp = P.pst("pp5", 2 + n5 % 2, 0, [128, 512])
                    gcols = slice(b * 1024 + g * 512, b * 1024 + (g + 1) * 512)
                    mm(pg, pg[:], [(hT[:, kc, cols], w_g[:, kc, gcols]) for kc in range(8)] + [(oneb[0:1, 0:128], bgb[0:1, gcols])], [hT, w_g, oneb, bgb])
                    mm(pp, pp[:], [(yT[:, kc, cols], w_pj[b][:, kc, g * 512:(g + 1) * 512]) for kc in range(8)], [yT, w_pj[b]])
                    gs = gs5[n5 % 2]; tm = tm5[n5 % 2]
                    act(gs[:], pg[:], AF.Sigmoid, [pg], [gs])
                    mcol = mg[:, g * 512:(g + 1) * 512]
                    if b == 0:
                        tt("dve", mcol, gs[:], pp[:], ALU.mult, [gs, pp], [mg])
                    else:
                        tt("dve", tm[:], gs[:], pp[:], ALU.mult, [gs, pp], [tm])
                        if b == 1:
                            tt("pool", mcol, mcol, tm[:], ALU.add, [mg, tm], [mg])
                        else:
                            tt("pool", mb_[:, g * 512:(g + 1) * 512], mcol, tm[:], ALU.add, [mg, tm], [mb_])
                    n5 += 1
            mT = mT5[sub % 2]
            ptm = P.pst("ptm5", 4, 0, [128, 8, 128], BF16)
            for kc in range(8):
                tr(ptm, ptm[:, kc, :], mb_[:, kc * 128:(kc + 1) * 128], identb[:], [mb_, identb], inc=(kc == 7))
            cp("act", mT[:], ptm[:], [ptm], [mT])
            for g in range(2):
                po = P.pst("po5", 5 + g, 0, [128, 512])
                mm(po, po[:], [(mT[:, kc, :], w_ot[:, kc, g * 512:(g + 1) * 512]) for kc in range(8)], [mT, w_ot])
                tt("dve", x_[:, g * 512:(g + 1) * 512], po[:], x_[:, g * 512:(g + 1) * 512], ALU.add, [po, x_], [x_])
            P.dma("pool", XM[sub * 128:(sub + 1) * 128, :], x_[:], reads=[x_], on=x_)
    if upto <= 5:
        return finish(nc, P)

    P.phase()
    grp6 = T("grp6", None)
    gmlp = bload("gmlp", norm_mlp_g, 1024, grp6)
    P.seal(grp6, [gmlp])
    wu = [load_w("wu%d" % i, w_up[:, i * 1024:(i + 1) * 1024], 1024, 1024, T("g6u%d" % i, None)) for i in range(4)]
    wd = [load_w("wd%d" % i, w_down[i * 1024:(i + 1) * 1024, :], 1024, 1024, T("g6d%d" % i, None)) for i in range(4)]
    T6 = 256
    NS6 = T6 // 128
    xt6 = [P.sb("xt6_%d" % i, [128, NS6, 1024]) for i in range(2)]
    hb6 = P.sb("hb6", [128, NS6, 1024], BF16)
    hT6 = P.sb("hT6", [128, 8, T6], BF16)
    ss6 = P.sb("ss6", [128, 4]); scr6 = P.sb("scr6", [128, 1024])
    uT = P.sb("uT", [128, 32, T6], BF16)
    rl = [P.sb("rl%d" % i, [128, T6], BF16) for i in range(2)]
    ob6 = [P.sb("ob6_%d" % i, [128, 512]) for i in range(2)]
    n6 = 0
    for t in range(NT // T6):
        xt = xt6[t % 2]
        P.dma("sp", xt[:], XM[t * T6:(t + 1) * T6, :].rearrange("(j p) d -> p j d", p=128), writes=[xt])
        rms_rows(xt, lambda j, xt=xt: xt[:, j, :], NS6, gmlp, hb6, lambda j: hb6[:, j, :], scr6, ss6)
        for kc in range(8):
            pt = P.pst("pt6", kc % 2, 0, [128, T6], BF16)
            for j in range(NS6):
                tr(pt, pt[:, j * 128:(j + 1) * 128], hb6[:, j, kc * 128:(kc + 1) * 128], identb[:], [hb6, identb], inc=(j == NS6 - 1))
            cp("act" if kc % 2 else "dve", hT6[:, kc, :], pt[:], [pt], [hT6])
        for fc in range(32):
            pu = P.pst("pu6", 2 + fc % 3, 0, [128, T6])
            wsel = wu[fc // 8]
            mm(pu, pu[:], [(wsel[:, kc, (fc % 8) * 128:(fc % 8 + 1) * 128], hT6[:, kc, :]) for kc in range(8)], [wsel, hT6])
            r_ = rl[fc % 2]
            act(r_[:], pu[:], AF.Relu, [pu], [r_])
            tt("pool" if fc % 2 else "dve", uT[:, fc, :], r_[:], r_[:], ALU.mult, [r_], [uT])
        for j in range(NS6):
            for g in range(2):
                pd = P.pst("pd6", 5 + n6 % 3, 0, [128, 512])
                mm(pd, pd[:], [(uT[:, fc, j * 128:(j + 1) * 128], wd[fc // 8][:, fc % 8, g * 512:(g + 1) * 512]) for fc in range(32)], [uT] + wd)
                o_ = ob6[n6 % 2]
                tt("dve", o_[:], pd[:], xt[:, j, g * 512:(g + 1) * 512], ALU.add, [pd, xt], [o_])
                r0 = t * T6 + j * 128
                P.dma("pool", out[r0:r0 + 128, g * 512:(g + 1) * 512], o_[:], reads=[o_], on=o_)
                n6 += 1
    return finish(nc, P)


def finish(nc, P):
    P.barrier()
    P.emit()
    P.es.close()
    return nc


def make_consts(pre_valid):
    c = np.zeros((128, NCST), np.float32)
    c[:, C_ID:C_ID + 128] = np.eye(128, dtype=np.float32)
    tri = np.triu(np.ones((128, 128), np.float32))
    c[:, C_TRI:C_TRI + 128] = tri
    c[:, C_TRIS:C_TRIS + 128] = tri * np.float32(128 ** -0.5)
    blk = np.zeros((128, 128), np.float32)
    blk[:64, :64] = 1.0
    blk[64:, 64:] = 1.0
    c[:, C_BLK:C_BLK + 128] = blk
    c[:, C_ONE:C_ONE + 128] = 1.0
    c[:, C_PF] = 1.0 if pre_valid else 0.0
    c[:, C_NB] = 0.0 if pre_valid else -1.0e30
    c[:, C_AB] = 0.0 if pre_valid else -30000.0
    c[0:4, C_BD:C_BD + 4] = np.eye(4, dtype=np.float32)
    return c


_NC_CACHE = {}


def run(inputs, NT, n_pairs, upto=99, dbg=False):
    f = lambda a: np.ascontiguousarray(np.asarray(a, dtype=np.float32))
    key = (NT, upto, dbg)
    if key not in _NC_CACHE:
        _NC_CACHE[key] = build(NT, upto, dbg)
    nc = _NC_CACHE[key]
    x = f(inputs["x"]); mem = f(inputs["mem"])
    shared = {
        "norm_mix_g": f(inputs["norm_mix_g"]).reshape(1, 1024), "w_in": f(inputs["w_in"]).reshape(1024, IN_COLS),
        "b_igate": f(inputs["b_igate"]).reshape(4, 1), "b_fgate": f(inputs["b_fgate"]).reshape(4, 1),
        "conv_w": f(inputs["conv_w"]).reshape(4, 1024), "conv_b": f(inputs["conv_b"]).reshape(8, 128),
        "m_norm_g": f(inputs["m_norm_g"]).reshape(1, 1024),
        "dq_norm_g": f(inputs["dq_norm_g"]).reshape(64, 1), "dk_norm_g": f(inputs["dk_norm_g"]).reshape(64, 1),
        "lam_q1": f(inputs["lam_q1"]).reshape(1, 64), "lam_k1": f(inputs["lam_k1"]).reshape(1, 64),
        "lam_q2": f(inputs["lam_q2"]).reshape(1, 64), "lam_k2": f(inputs["lam_k2"]).reshape(1, 64),
        "subln_g": f(inputs["subln_g"]).reshape(1, 128),
        "cq_norm_g": f(inputs["cq_norm_g"]).reshape(128, 1), "ck_norm_g": f(inputs["ck_norm_g"]).reshape(128, 1),
        "mem_norm_g": f(inputs["mem_norm_g"]).reshape(1, 1024), "w_mem_kv": f(inputs["w_mem_kv"]).reshape(1024, 1536),
        "w_gate": f(inputs["w_gate"]).reshape(1024, 3072), "b_gate": f(inputs["b_gate"]).reshape(1, 3072),
        "w_proj_m": f(inputs["w_proj_m"]).reshape(1024, 1024), "w_proj_d": f(inputs["w_proj_d"]).reshape(1024, 1024),
        "w_proj_c": f(inputs["w_proj_c"]).reshape(1024, 1024), "w_out": f(inputs["w_out"]).reshape(1024, 1024),
        "norm_mlp_g": f(inputs["norm_mlp_g"]).reshape(1, 1024),
        "w_up": f(inputs["w_up"]).reshape(1024, 4096), "w_down": f(inputs["w_down"]).reshape(4096, 1024),
    }
    in_maps = []
    for b in range(n_pairs):
        for half in range(2):
            d = dict(shared)
            d["x_own"] = np.ascontiguousarray(x[b, half * NT:(half + 1) * NT])
            d["x_pre"] = np.ascontiguousarray(x[b, 0:NT]) if half == 1 else np.zeros((NT, 1024), np.float32)
            d["mem"] = np.ascontiguousarray(mem[b])
            d["cst"] = make_consts(half == 1)
            in_maps.append(d)
    res = run_bass_kernel_spmd(nc, in_maps, core_ids=list(range(2 * n_pairs)))
    return res


def kernel(**inputs):
    res = run(inputs, 4096, 4)
    outp = np.empty((4, 8192, 1024), np.float32)
    for b in range(4):
        for half in range(2):
            outp[b, half * 4096:(half + 1) * 4096] = res.results[2 * b + half]["out"]
    return outp
```

```python
import math
import numpy as np
import concourse.bass as bass
import concourse.mybir as mybir
from concourse.bass_utils import run_bass_kernel_spmd
from contextlib import ExitStack

F32 = mybir.dt.float32
BF16 = mybir.dt.bfloat16
AF = mybir.ActivationFunctionType
ALU = mybir.AluOpType

COMPUTE = ("pe", "act", "dve", "pool")
ALLENG = COMPUTE + ("sp",)
EPS = 1e-6
ARENA_WORDS = 53000


class T:
    __slots__ = ("name", "h", "w", "r", "dkey", "dcnt", "g", "sub")

    def __init__(self, name, h):
        self.name = name
        self.h = h
        self.w = None
        self.r = {}
        self.dkey = None
        self.dcnt = 0
        self.g = None
        self.sub = None

    def __getitem__(self, k):
        return self.h[k]


class PV:
    __slots__ = ("h", "banks")

    def __init__(self, h, banks):
        self.h = h
        self.banks = banks

    def __getitem__(self, k):
        return self.h[k]


class Prog:
    def __init__(self, nc):
        self.nc = nc
        self.es = ExitStack()
        self.q = {e: [] for e in ALLENG}
        self.cnt = {e: 0 for e in COMPUTE}
        self.waited = {e: {} for e in ALLENG}
        self.sems = {}
        self.pe_pending = []
        self.dma_tiles = []
        self.nkeys = 0
        self.arena = self.es.enter_context(nc.sbuf_tensor("arena", [128, ARENA_WORDS], F32))
        self.pairs = [self.es.enter_context(nc.psum_tensor("pbank%d" % i, [128, 1024], F32)) for i in range(4)]
        self.banks = [self.pairs[i // 2][:, (i % 2) * 512:(i % 2 + 1) * 512] for i in range(8)]
        self.bankT = [T("bank%d" % i, None) for i in range(8)]
        self.top = 0
        self.persist_top = 0
        self.pcache = {}

    @staticmethod
    def _shape_view(ap, shape):
        if len(shape) == 2:
            return ap
        if len(shape) == 3:
            return ap.rearrange("p (a b) -> p a b", a=shape[1])
        if len(shape) == 4:
            return ap.rearrange("p (a b c) -> p a b c", a=shape[1], b=shape[2])
        raise ValueError(shape)

    def sb(self, name, shape, dt=F32):
        n = int(np.prod(shape[1:]))
        words = n if dt == F32 else (n + 1) // 2
        off = self.top
        self.top = off + ((words + 15) // 16) * 16
        assert self.top <= ARENA_WORDS, ("SBUF overflow", name, self.top)
        ap = self.arena[0:shape[0], off:off + words]
        if dt != F32:
            ap = ap.bitcast(dt)
            if n != 2 * words:
                ap = ap[:, 0:n]
        return T(name, self._shape_view(ap, shape))

    def pst(self, name, bank, lo, shape, dt=F32):
        n = int(np.prod(shape[1:]))
        words = n if dt == F32 else (n + 1) // 2
        assert lo + words <= 512
        key = (bank, lo, words, tuple(shape), dt)
        t = self.pcache.get(key)
        if t is not None:
            return t
        ap = self.banks[bank][0:shape[0], lo:lo + words]
        if dt != F32:
            ap = ap.bitcast(dt)
        t = PV(self._shape_view(ap, shape), (self.bankT[bank],))
        self.pcache[key] = t
        return t

    def pst2(self, pair):
        key = ("pair", pair)
        t = self.pcache.get(key)
        if t is None:
            t = PV(self.pairs[pair][:, :].rearrange("p (a b) -> p a b", a=2), (self.bankT[2 * pair], self.bankT[2 * pair + 1]))
            self.pcache[key] = t
        return t

    def dram(self, name, shape, dt, kind="Internal"):
        h = self.nc.dram_tensor(name, list(shape), dt, kind=kind)
        return T(name, h.ap())

    def _sem(self, key):
        s = self.sems.get(key)
        if s is None:
            s = self.es.enter_context(self.nc.semaphore("s_" + str(key)))
            self.sems[key] = s
        return s

    def _deps(self, eng, reads, writes):
        waits = {}
        wd = self.waited[eng]

        def need(k, v):
            if eng == "pe" and k == "pe":
                return
            if wd.get(k, 0) >= v:
                return
            if waits.get(k, 0) < v:
                waits[k] = v

        for t in reads:
            if t.w is not None:
                need(*t.w)
        for t in writes:
            if t.w is not None:
                need(*t.w)
            for k, v in t.r.items():
                need(k, v)
        for k, v in waits.items():
            wd[k] = v
        return list(waits.items())

    @staticmethod
    def _mark(tok, reads, writes):
        k, v = tok
        for t in reads:
            if t.r.get(k, 0) < v:
                t.r[k] = v
        for t in writes:
            t.w = tok
            t.r = {}

    def op(self, eng, fn, reads=(), writes=(), inc=True):
        r2, w2 = [], []
        for t in reads:
            if isinstance(t, PV):
                w2.extend(t.banks)
            else:
                r2.append(t)
        for t in writes:
            if isinstance(t, PV):
                w2.extend(t.banks)
            else:
                w2.append(t)
        reads, writes = r2, w2
        waits = self._deps(eng, reads, writes)
        if not inc:
            assert eng == "pe"
            self.q[eng].append((waits, fn, None))
            self.pe_pending.append((reads, writes))
            return
        self.cnt[eng] += 1
        tok = (eng, self.cnt[eng])
        self.q[eng].append((waits, fn, (eng, 1)))
        if eng == "pe" and self.pe_pending:
            for r, w in self.pe_pending:
                self._mark(tok, r, w)
            self.pe_pending = []
        self._mark(tok, reads, writes)

    def dma(self, queue, out, in_, reads=(), writes=(), on=None):
        if on is None:
            on = writes[0] if writes else reads[0]
        isgrp = on.h is None
        if on.sub is None:
            on.sub = {}
        if queue not in on.sub:
            on.sub[queue] = T(on.name + "_" + queue, None)
        on = on.sub[queue]
        if isgrp:
            for t in writes:
                t.g = on
        if on.dkey is None:
            self.nkeys += 1
            on.dkey = "d%d" % self.nkeys
            self.dma_tiles.append(on)
        waits = self._deps(queue, reads, [] if isgrp else writes)
        on.dcnt += 1
        tok = (on.dkey, 16 * on.dcnt)
        self.q[queue].append((waits, lambda e, o=out, i=in_: e.dma_start(out=o, in_=i), (on.dkey, 16)))
        self._mark(tok, reads, writes)
        return tok

    def seal(self, grp, tiles):
        for t in tiles:
            t.w = (t.g.dkey, 16 * t.g.dcnt)

    def barrier(self):
        assert not self.pe_pending
        toks = [(e, self.cnt[e]) for e in COMPUTE if self.cnt[e] > 0]
        toks += [(t.dkey, 16 * t.dcnt) for t in self.dma_tiles]
        for e in ALLENG:
            wd = self.waited[e]
            waits = []
            for k, v in toks:
                if k == e:
                    continue
                if wd.get(k, 0) < v:
                    waits.append((k, v))
                    wd[k] = v
            if waits:
                self.q[e].append((waits, None, None))
        self.dma_tiles = []
        self.pcache = {}

    def phase(self):
        self.barrier()
        self.top = self.persist_top

    def emit(self):
        nc = self.nc
        for e in ALLENG:
            for waits, fn, inc in self.q[e]:
                for k, _ in waits:
                    self._sem(k)
                if inc is not None:
                    self._sem(inc[0])
        engobj = {"pe": "tensor", "act": "scalar", "dve": "vector", "pool": "gpsimd", "sp": "sync"}
        with nc.Block() as block:
            for e in ALLENG:
                lst = self.q[e]
                if not lst:
                    continue

                def body(eng, lst=lst):
                    for waits, fn, inc in lst:
                        for k, v in waits:
                            eng.wait_ge(self.sems[k], v)
                        if fn is not None:
                            ins = fn(eng)
                            if inc is not None:
                                ins.then_inc(self.sems[inc[0]], inc[1])

                getattr(block, engobj[e])(body)


O_MQ, O_MK, O_MV, O_MI, O_MF, O_MO, O_DQ, O_DK, O_DV, O_CQ = 0, 512, 1024, 2048, 2052, 2056, 3080, 4104, 5128, 6152
IN_COLS = 6664
C_ID, C_TRI, C_BLK, C_PF, C_NB, C_AB, C_BD, C_ONE, C_TRIS, NCST = 0, 128, 256, 384, 385, 386, 387, 391, 519, 647
LAM_INIT = 0.8 - 0.6 * math.exp(-0.3 * 0)


def build(NT, upto=99, dbg=False):
    nc = bass.Bass("TRN2", target_bir_lowering=False)
    P = Prog(nc)
    NTA = 2 * NT
    NTL_ALL = NTA // 512
    NTL_OWN = NT // 512
    NTL_PRE = NT // 512
    NCH = NTA // 128
    NBLK = NTA // 128
    NPB = NT // 128
    SC = 128 ** -0.5

    def din(name, shape):
        return nc.dram_tensor(name, list(shape), F32, kind="ExternalInput").ap()

    x_own = din("x_own", [NT, 1024]); x_pre = din("x_pre", [NT, 1024]); mem = din("mem", [256, 1024])
    cst = din("cst", [128, NCST])
    norm_mix_g = din("norm_mix_g", [1, 1024]); w_in = din("w_in", [1024, IN_COLS])
    b_igate = din("b_igate", [4, 1]); b_fgate = din("b_fgate", [4, 1])
    conv_w = din("conv_w", [4, 1024]); conv_b = din("conv_b", [8, 128]); m_norm_g = din("m_norm_g", [1, 1024])
    dq_norm_g = din("dq_norm_g", [64, 1]); dk_norm_g = din("dk_norm_g", [64, 1])
    lam_q1 = din("lam_q1", [1, 64]); lam_k1 = din("lam_k1", [1, 64]); lam_q2 = din("lam_q2", [1, 64]); lam_k2 = din("lam_k2", [1, 64])
    subln_g = din("subln_g", [1, 128]); cq_norm_g = din("cq_norm_g", [128, 1]); ck_norm_g = din("ck_norm_g", [128, 1])
    mem_norm_g = din("mem_norm_g", [1, 1024]); w_mem_kv = din("w_mem_kv", [1024, 1536])
    w_gate = din("w_gate", [1024, 3072]); b_gate = din("b_gate", [1, 3072])
    w_proj_m = din("w_proj_m", [1024, 1024]); w_proj_d = din("w_proj_d", [1024, 1024]); w_proj_c = din("w_proj_c", [1024, 1024])
    w_out = din("w_out", [1024, 1024]); norm_mlp_g = din("norm_mlp_g", [1, 1024])
    w_up = din("w_up", [1024, 4096]); w_down = din("w_down", [4096, 1024])
    skind = "ExternalOutput" if dbg else "Internal"
    out = nc.dram_tensor("out", [NT, 1024], F32, kind="ExternalOutput").ap()

    HT = P.dram("HT", [NTL_ALL, 128, 8, 512], BF16, skind)
    YTm = P.dram("YTm", [NTL_OWN, 128, 8, 512], BF16, skind)
    YTd = P.dram("YTd", [NTL_OWN, 128, 8, 512], BF16, skind)
    YTc = P.dram("YTc", [NTL_OWN, 128, 8, 512], BF16, skind)
    KT = P.dram("KT", [8, 128, NTA], BF16, skind)
    QT = P.dram("QT", [8, 128, NT], BF16, skind)
    VA = P.dram("VA", [8, 128, NBLK, 129], BF16, skind)
    XM = P.dram("XM", [NT, 1024], F32, skind)

    def mm(outT, out_ap, pairs, reads, inc=True, start=True, skip=False):
        n = len(pairs)
        for i, (l, r) in enumerate(pairs):
            st = (i == 0) and start
            sp = (i == n - 1)
            if skip:
                f = lambda e, l=l, r=r, st=st, sp=sp: e.matmul(out=out_ap, lhsT=l, rhs=r, start=st, stop=sp, skip_group_check=True)
            else:
                f = lambda e, l=l, r=r, st=st, sp=sp: e.matmul(out=out_ap, lhsT=l, rhs=r, start=st, stop=sp)
            P.op("pe", f, reads=reads, writes=[outT], inc=(inc and i == n - 1))

    def tr(outT, out_ap, in_ap, ident_ap, reads, inc=True):
        P.op("pe", lambda e: e.transpose(out=out_ap, in_=in_ap, identity=ident_ap), reads=reads, writes=[outT], inc=inc)

    def act(out_ap, in_ap, func, reads, writes, **kw):
        P.op("act", lambda e: e.activation(out=out_ap, in_=in_ap, func=func, **kw), reads=reads, writes=writes)

    def amul(out_ap, in_ap, m_ap, reads, writes):
        P.op("act", lambda e: e.mul(out=out_ap, in_=in_ap, mul=m_ap), reads=reads, writes=writes)

    def tt(eng, out_ap, a, b, op, reads, writes):
        P.op(eng, lambda e: e.tensor_tensor(out=out_ap, in0=a, in1=b, op=op), reads=reads, writes=writes)

    def ts(eng, out_ap, a, s1, s2, op0, op1, reads, writes):
        if s2 is None:
            P.op(eng, lambda e: e.tensor_scalar(out=out_ap, in0=a, scalar1=s1, scalar2=None, op0=op0), reads=reads, writes=writes)
        else:
            P.op(eng, lambda e: e.tensor_scalar(out=out_ap, in0=a, scalar1=s1, scalar2=s2, op0=op0, op1=op1), reads=reads, writes=writes)

    def stt(eng, out_ap, a, s, b, op0, op1, reads, writes):
        P.op(eng, lambda e: e.scalar_tensor_tensor(out=out_ap, in0=a, scalar=s, in1=b, op0=op0, op1=op1), reads=reads, writes=writes)

    def cp(eng, out_ap, in_ap, reads, writes):
        if eng == "act":
            P.op("act", lambda e: e.copy(out=out_ap, in_=in_ap), reads=reads, writes=writes)
        else:
            P.op(eng, lambda e: e.tensor_copy(out=out_ap, in_=in_ap), reads=reads, writes=writes)

    def recip(out_ap, in_ap, reads, writes):
        P.op("dve", lambda e: e.reciprocal(out=out_ap, in_=in_ap), reads=reads, writes=writes)

    def mset(eng, ap, val, writes):
        P.op(eng, lambda e: e.memset(ap, val), writes=writes)

    def scan(out_T, out_ap, d0, d1_T, d1, init, op1, extra_reads=()):
        P.op("dve", lambda e: e.tensor_tensor_scan(out=out_ap, data0=d0, data1=d1, initial=init, op0=ALU.mult, op1=op1),
             reads=[d1_T] + list(extra_reads), writes=[out_T])

    def load_w(name, src, K, N, grp):
        t = P.sb(name, [128, K // 128, N], BF16)
        v = src.rearrange("(kc p) n -> p kc n", p=128)
        step = max(1, 2048 // N) if N <= 2048 else 1
        for kc in range(0, K // 128, step):
            k1 = min(K // 128, kc + step)
            for c0 in range(0, N, 2048):
                c1 = min(N, c0 + 2048)
                P.dma("pool", t[:, kc:k1, c0:c1], v[:, kc:k1, c0:c1], writes=[t], on=grp)
        return t

    def bload(name, src, n, grp, queue="sp"):
        t = P.sb(name, [128, n])
        P.dma(queue, t[:], src.partition_broadcast(128), writes=[t], on=grp)
        return t

    def rsqrt(out_ap, in_ap, scale, reads, writes):
        act(out_ap, in_ap, AF.Ln, reads, writes, scale=scale, bias=EPS)
        act(out_ap, out_ap, AF.Exp, writes, writes, scale=-0.5)

    def rms_rows(xt_T, x_ap_fn, nsub, gB, hb_T, hb_ap_fn, scr_T, ss_T, D=1024):
        for j in range(nsub):
            act(scr_T[:], x_ap_fn(j), AF.Square, [xt_T], [scr_T, ss_T], accum_out=ss_T[:, j:j + 1])
        rsqrt(ss_T[:, 0:nsub], ss_T[:, 0:nsub], 1.0 / D, [ss_T], [ss_T])
        for j in range(nsub):
            stt("dve", hb_ap_fn(j), x_ap_fn(j), ss_T[:, j:j + 1], gB[:], ALU.mult, ALU.mult, [xt_T, ss_T, gB], [hb_T])

    def fm_norm(psT, ps_ap, n, onesb, inv_d, gcol, outT, out_ap, sqb, ssp, rs):
        act(sqb[:, 0:n], ps_ap, AF.Square, [psT], [sqb])
        mm(ssp, ssp[:, 0:n], [(onesb[:], sqb[:, 0:n])], [onesb, sqb])
        rsqrt(rs[:, 0:n], ssp[:, 0:n], inv_d, [ssp], [rs])
        stt("dve", out_ap, ps_ap, gcol[:, 0:1], rs[:, 0:n], ALU.mult, ALU.mult, [psT, gcol, rs], [outT])

    grp0 = T("grp0", None)
    cs = P.sb("cst", [128, NCST])
    P.dma("sp", cs[:], cst, writes=[cs], on=grp0)
    gq2 = P.sb("gq2", [128, 1]); gk2 = P.sb("gk2", [128, 1]); gcq = P.sb("gcq", [128, 1]); gck = P.sb("gck", [128, 1])
    for hf in range(2):
        P.dma("sp", gq2[64 * hf:64 * hf + 64, :], dq_norm_g, writes=[gq2], on=grp0)
        P.dma("sp", gk2[64 * hf:64 * hf + 64, :], dk_norm_g, writes=[gk2], on=grp0)
    P.dma("sp", gcq[:], cq_norm_g, writes=[gcq], on=grp0)
    P.dma("sp", gck[:], ck_norm_g, writes=[gck], on=grp0)
    gsub = bload("gsub", subln_g, 128, grp0)
    identb = P.sb("identb", [128, 128], BF16)
    blkb = P.sb("blkb", [128, 128], BF16)
    oneb = P.sb("oneb", [128, 128], BF16)
    trib = P.sb("trib", [128, 128], BF16)
    lam = P.sb("lam", [128, 4])
    zcol = P.sb("zcol", [128, 1])
    P.persist_top = P.top
    lamv = [bload("lam%d" % i, a, 64, grp0) for i, a in enumerate((lam_q1, lam_k1, lam_q2, lam_k2))]
    ljunk = P.sb("ljunk", [128, 64])
    P.seal(grp0, [cs, gq2, gk2, gcq, gck, gsub] + lamv)
    cp("dve", identb[:], cs[:, C_ID:C_ID + 128], [cs], [identb])
    cp("dve", blkb[:], cs[:, C_BLK:C_BLK + 128], [cs], [blkb])
    cp("dve", oneb[:], cs[:, C_ONE:C_ONE + 128], [cs], [oneb])
    cp("dve", trib[:], cs[:, C_TRI:C_TRI + 128], [cs], [trib])
    mset("dve", zcol[:], 0.0, [zcol])
    for i in range(2):
        tt("dve", ljunk[:], lamv[2 * i][:], lamv[2 * i + 1][:], ALU.mult, [lamv[2 * i], lamv[2 * i + 1]], [ljunk])
        act(ljunk[:], ljunk[:], AF.Copy, [ljunk], [ljunk, lam], accum_out=lam[:, i:i + 1])
    act(lam[:, 0:2], lam[:, 0:2], AF.Exp, [lam], [lam])
    tt("dve", lam[:, 2:3], lam[:, 0:1], lam[:, 1:2], ALU.subtract, [lam], [lam])
    ts("dve", lam[:, 3:4], lam[:, 2:3], LAM_INIT, -1.0, ALU.add, ALU.mult, [lam], [lam])
    ts("dve", gsub[:], gsub[:], 1.0 - LAM_INIT, None, ALU.mult, None, [gsub], [gsub])

    P.phase()
    grp1 = T("grp1", None)
    gmix = bload("gmix", norm_mix_g, 1024, grp1)
    P.seal(grp1, [gmix])
    xts = [P.sb("xt%d" % i, [128, 4, 1024]) for i in range(3)]
    hbs = [P.sb("hb%d" % i, [128, 4, 1024], BF16) for i in range(2)]
    hTs = [P.sb("hT%d" % i, [128, 8, 512], BF16) for i in range(2)]
    sss = [P.sb("ss%d" % i, [128, 4]) for i in range(2)]
    scr1 = P.sb("scr1", [128, 1024])
    for t in range(NTL_ALL):
        b = t % 2
        src = x_pre[t * 512:(t + 1) * 512, :] if t < NTL_PRE else x_own[(t - NTL_PRE) * 512:(t - NTL_PRE + 1) * 512, :]
        xt, hb, hT, ss = xts[t % 3], hbs[b], hTs[b], sss[b]
        P.dma("sp", xt[:], src.rearrange("(j p) d -> p j d", p=128), writes=[xt])
        rms_rows(xt, lambda j, xt=xt: xt[:, j, :], 4, gmix, hb, lambda j, hb=hb: hb[:, j, :], scr1, ss)
        for kc in range(8):
            pt = P.pst("pt1", kc % 8, 0, [128, 512], BF16)
            for j in range(4):
                tr(pt, pt[:, j * 128:(j + 1) * 128], hb[:, j, kc * 128:(kc + 1) * 128], identb[:], [hb, identb], inc=(j == 3))
            cp("act" if kc % 2 else "dve", hT[:, kc, :], pt[:], [pt], [hT])
        P.dma("pool", HT[t], hT[:], reads=[hT], on=hT)
    if upto <= 1:
        return finish(nc, P)

    P.phase()
    grp2 = T("grp2", None)
    w_i = load_w("w_i", w_in[:, O_MI:O_MI + 4], 1024, 4, grp2)
    w_f = load_w("w_f", w_in[:, O_MF:O_MF + 4], 1024, 4, grp2)
    bi = P.sb("bi", [4, 1]); bfn = P.sb("bfn", [4, 1])
    P.dma("sp", bi[:], b_igate, writes=[bi], on=grp2)
    P.dma("sp", bfn[:], b_fgate, writes=[bfn], on=grp2)
    cwl = P.sb("cwl", [4, 1024]); cbl = P.sb("cbl", [8, 128])
    P.dma("sp", cwl[:], conv_w, writes=[cwl], on=grp2)
    P.dma("sp", cbl[:], conv_b, writes=[cbl], on=grp2)
    cw = P.sb("cw", [128, 8, 4]); cb = P.sb("cb", [128, 8])
    gmn = bload("gmn", m_norm_g, 1024, grp2)
    w_q = load_w("w_q", w_in[:, O_MQ:O_MQ + 512], 1024, 512, grp2)
    w_k = load_w("w_k", w_in[:, O_MK:O_MK + 512], 1024, 512, grp2)
    w_v = load_w("w_v", w_in[:, O_MV:O_MV + 1024], 1024, 1024, grp2)
    w_o = load_w("w_o", w_in[:, O_MO:O_MO + 1024], 1024, 1024, grp2)
    P.seal(grp2, [w_i, w_f, bi, bfn, cwl, cbl, gmn, w_q, w_k, w_v, w_o])
    ts("dve", bfn[:], bfn[:], -1.0, None, ALU.mult, None, [bfn], [bfn])
    pcw = P.pst("pcw", 3, 0, [128, 8, 4]); pcb = P.pst("pcb", 3, 64, [128, 8])
    for g in range(8):
        tr(pcw, pcw[:, g, :], cwl[0:4, g * 128:(g + 1) * 128], cs[0:4, C_ID:C_ID + 4], [cwl, cs], inc=(g == 7))
    tr(pcb, pcb[:], cbl[0:8, :], cs[0:8, C_ID:C_ID + 8], [cbl, cs])
    cp("dve", cw[:], pcw[:], [pcw], [cw])
    cp("dve", cb[:], pcb[:], [pcb], [cb])
    hTg = [P.sb("hTg%d" % i, [128, 8, 512], BF16) for i in range(2)]
    decB = P.sb("decB", [128, 4, NCH]); decS = P.sb("decS", [128, 4, NCH])
    WT = P.sb("WT", [128, NCH, 4]); FT = P.sb("FT", [128, NCH, 4])
    GE = P.sb("GE", [4, NCH]); GP = P.sb("GP", [4, NCH]); dec = P.sb("dec", [4, NCH]); dbd = P.sb("dbd", [4, 4, NCH])
    carB = P.sb("carB", [4, 1]); carG = P.sb("carG", [4, 1])
    top_2a = P.top

    GB = min(1024, NTA)
    NGB = NTA // GB
    CPB = GB // 128
    I_b = P.sb("I_b", [4, GB]); E_b = P.sb("E_b", [4, GB]); ones4 = P.sb("ones4", [4, GB])
    Bn = P.sb("Bn", [4, GB]); A_b = P.sb("A_b", [4, GB]); G_b = P.sb("G_b", [4, GB])
    mset("pool", ones4[:], 1.0, [ones4])
    mset("dve", carB[:], 0.0, [carB]); mset("dve", carG[:], 0.0, [carG])
    pwt = P.pst("pwt", 1, 0, [128, 4 * NCH]); pft = P.pst("pft", 2, 0, [128, 4 * NCH])
    v3 = lambda T_: T_[:].rearrange("p (c l) -> p c l", l=128)
    for gb in range(NGB):
        for tb in range(GB // 512):
            t = gb * (GB // 512) + tb
            hT = hTg[t % 2]
            P.dma("sp", hT[:], HT[t], writes=[hT])
            pgi = P.pst("pgi", 4 + 2 * (t % 2), 0, [4, 512]); pgf = P.pst("pgf", 5 + 2 * (t % 2), 0, [4, 512])
            mm(pgi, pgi[:], [(w_i[:, kc, :], hT[:, kc, :]) for kc in range(8)], [w_i, hT])
            mm(pgf, pgf[:], [(w_f[:, kc, :], hT[:, kc, :]) for kc in range(8)], [w_f, hT])
            ts("dve", I_b[:, tb * 512:(tb + 1) * 512], pgi[:], bi[:, 0:1], None, ALU.add, None, [pgi, bi], [I_b])
            act(E_b[:, tb * 512:(tb + 1) * 512], pgf[:], AF.Exp, [pgf, bfn], [E_b], scale=-1.0, bias=bfn[:, 0:1])
        npre = max(0, min(GB, NT - gb * GB))
        act(E_b[:], E_b[:], AF.Ln, [E_b], [E_b], bias=1.0)
        if npre:
            ts("dve", E_b[:, 0:npre], E_b[:, 0:npre], cs[0:4, C_PF:C_PF + 1], None, ALU.mult, None, [E_b, cs], [E_b])
        scan(Bn, Bn[:], ones4[:], E_b, E_b[:], carB[:, 0:1], ALU.add, [ones4, carB])
        cp("dve", carB[:], Bn[:, GB - 1:GB], [Bn], [carB])
        tt("dve", A_b[:], I_b[:], Bn[:], ALU.add, [I_b, Bn], [A_b])
        if npre:
            ts("dve", A_b[:, 0:npre], A_b[:, 0:npre], cs[0:4, C_NB:C_NB + 1], None, ALU.add, None, [A_b, cs], [A_b])
        scan(G_b, G_b[:], ones4[:], A_b, A_b[:], carG[:, 0:1], ALU.max, [ones4, carG])
        cp("dve", carG[:], G_b[:, GB - 1:GB], [G_b], [carG])
        Gend = v3(G_b)[:, :, 127:128]
        tt("dve", v3(I_b), v3(A_b), Gend.to_broadcast([4, CPB, 128]), ALU.subtract, [A_b, G_b], [I_b])
        act(I_b[:], I_b[:], AF.Exp, [I_b], [I_b])
        tt("dve", v3(E_b), v3(Bn), Gend.to_broadcast([4, CPB, 128]), ALU.subtract, [Bn, G_b], [E_b])
        act(E_b[:], E_b[:], AF.Exp, [E_b], [E_b])
        cp("dve", GE[:, gb * CPB:(gb + 1) * CPB], v3(G_b)[:, :, 127], [G_b], [GE])
        for cl in range(CPB):
            c = gb * CPB + cl
            tr(pwt, pwt[:, 4 * c:4 * c + 4], I_b[0:4, cl * 128:(cl + 1) * 128], cs[0:4, C_ID:C_ID + 4], [I_b, cs], inc=(cl == CPB - 1))
        for cl in range(CPB):
            c = gb * CPB + cl
            tr(pft, pft[:, 4 * c:4 * c + 4], E_b[0:4, cl * 128:(cl + 1) * 128], cs[0:4, C_ID:C_ID + 4], [E_b, cs], inc=(cl == CPB - 1))
    cp("dve", WT[:].rearrange("p a b -> p (a b)"), pwt[:], [pwt], [WT])
    cp("act", FT[:].rearrange("p a b -> p (a b)"), pft[:], [pft], [FT])
    mset("dve", GP[:, 0:1], 0.0, [GP])
    cp("dve", GP[:, 1:NCH], GE[:, 0:NCH - 1], [GE], [GP])
    tt("dve", dec[:], GP[:], GE[:], ALU.subtract, [GP, GE], [dec])
    act(dec[:], dec[:], AF.Exp, [dec], [dec])
    for h in range(4):
        ts("dve", dbd[:, h, :], dec[:], cs[0:4, C_BD + h:C_BD + h + 1], None, ALU.mult, None, [dec, cs], [dbd])
    pdb = P.pst("pdb", 0, 0, [128, 4 * NCH])
    mm(pdb, pdb[:], [(cs[0:4, C_ONE:C_ONE + 128], dbd[:].rearrange("p a b -> p (a b)"))], [cs, dbd])
    cp("dve", decB[:].rearrange("p a b -> p (a b)"), pdb[:], [pdb], [decB])
    ts("dve", decS[:], decB[:], SC, None, ALU.mult, None, [decB], [decS])

    P.barrier()
    P.top = top_2a
    Cst = [P.sb("C%d" % h, [128, 257]) for h in range(4)]
    Cd = [P.sb("Cd%d" % h, [128, 257], BF16) for h in range(4)]
    for h in range(4):
        mset("dve", Cst[h][:], 0.0, [Cst[h]])
        mset("pool", Cd[h][:], 0.0, [Cd[h]])
    XP = [[P.sb("XP%d%d" % (g, h), [128, 515]) for h in range(4)] for g in range(2)]
    for g in range(2):
        for h in range(4):
            mset("pool", XP[g][h][:, 0:3], 0.0, [XP[g][h]])
    cv = [P.sb("cv%d" % i, [128, 512]) for i in range(2)]
    qkT = [[P.sb("qkT%d%d" % (g, i), [128, 4, 512], BF16) for i in range(2)] for g in range(2)]
    vw = [P.sb("vw%d" % i, [128, 4, 257], BF16) for i in range(2)]
    og = P.sb("og", [128, 1024])
    ktok = [P.sb("ktok%d" % i, [128, 4, 128], BF16) for i in range(2)]
    SM = [P.sb("SM%d" % i, [128, 4, 128], BF16) for i in range(2)]
    hm = [P.sb("hm%d" % i, [128, 4, 256]) for i in range(2)]
    st2 = [P.sb("st2_%d" % i, [128, 8]) for i in range(2)]
    sq2 = P.sb("sq2", [128, 256])
    t2 = P.sb("t2", [128, 1024])
    ymb = P.sb("ymb", [128, 1024], BF16)
    ymT = [P.sb("ymT%d" % i, [128, 8, 512], BF16) for i in range(2)]
    pj = [0]
    pend_tail = [None]

    def proj_group(t, g, h):
        hT = hTg[t % 2]
        wsel = w_k if g == 0 else w_q
        dst = qkT[g][t % 2]
        pp = P.pst("pp", pj[0] % 2, 0, [128, 512]); pj[0] += 1
        mm(pp, pp[:], [(wsel[:, kc, h * 128:(h + 1) * 128], hT[:, kc, :]) for kc in range(8)], [wsel, hT])
        xp = XP[g][h]
        ch = (1 - g) * 4 + h
        cp("act", xp[:, 3:515], pp[:], [pp], [xp])
        c_ = cv[(g * 4 + h) % 2]
        ts("dve", c_[:], xp[:, 0:512], cw[:, ch, 0:1], cb[:, ch:ch + 1], ALU.mult, ALU.add, [xp, cw, cb], [c_])
        for j in range(1, 4):
            stt("dve", c_[:], xp[:, j:j + 512], cw[:, ch, j:j + 1], c_[:], ALU.mult, ALU.add, [xp, cw, c_], [c_])
        act(dst[:, h, :], c_[:], AF.Silu, [c_], [dst])
        cp("pool", xp[:, 0:3], xp[:, 512:515], [xp], [xp])

    def groups_of(t):
        gs = (0, 1) if (t >= NTL_PRE or t == NTL_PRE - 1) else (0,)
        return [(t, g, h) for g in gs for h in range(4)]

    P.dma("sp", hTg[0][:], HT[0], writes=[hTg[0]])
    for a_ in groups_of(0):
        proj_group(*a_)
    for t in range(NTL_ALL):
        own = t >= NTL_PRE
        hT = hTg[t % 2]
        kT = qkT[0][t % 2]; qT = qkT[1][t % 2]
        nxt = []
        if t + 1 < NTL_ALL:
            P.dma("sp", hTg[(t + 1) % 2][:], HT[t + 1], writes=[hTg[(t + 1) % 2]])
            nxt = groups_of(t + 1)
        per = (len(nxt) + 3) // 4
        ymT_t = ymT[t % 2]
        for cc in range(4):
            c = 4 * t + cc
            par = c % 2
            cols = slice(cc * 128, (cc + 1) * 128)
            vw_c = vw[par]; ktok_c = ktok[par]
            for g in range(2):
                pv = P.pst("pv", 2 + g, 0, [128, 512])
                mm(pv, pv[:], [(hT[:, kc, cols], w_v[:, kc, g * 512:(g + 1) * 512]) for kc in range(8)], [hT, w_v])
                tt("dve", vw_c[:, 2 * g:2 * g + 2, 0:256], pv[:].rearrange("p (a b) -> p a b", a=2),
                   WT[:, c, 2 * g:2 * g + 2].unsqueeze(2).to_broadcast([128, 2, 256]), ALU.mult, [pv, WT], [vw_c])
            cp("pool", vw_c[:, :, 256], WT[:, c, :], [WT], [vw_c])
            ptk = P.pst("ptk", 5, 256, [128, 4, 128], BF16)
            for h in range(4):
                tr(ptk, ptk[:, h, :], kT[:, h, cols], identb[:], [kT, identb], inc=(h == 3))
            cp("act", ktok_c[:], ptk[:], [ptk], [ktok_c])
            if own:
                for g in range(2):
                    po = P.pst("pp", pj[0] % 2, 0, [128, 512]); pj[0] += 1
                    mm(po, po[:], [(hT[:, kc, cols], w_o[:, kc, g * 512:(g + 1) * 512]) for kc in range(8)], [hT, w_o])
                    act(og[:, g * 512:(g + 1) * 512], po[:], AF.Sigmoid, [po], [og])
                pS = P.pst("pS", 4, 0, [128, 4, 128])
                for h in range(4):
                    mm(pS, pS[:, h, :], [(kT[:, h, cols], qT[:, h, cols])], [kT, qT], inc=(h == 3))
                SM_c = SM[par]
                tt("dve", SM_c[:], pS[:], cs[:, C_TRIS:C_TRIS + 128].unsqueeze(1).to_broadcast([128, 4, 128]), ALU.mult, [pS, cs], [SM_c])
                hm_c = hm[par]; st_c = st2[par]
                if pend_tail[0] is not None:
                    pend_tail[0]()
                    pend_tail[0] = None
            for h in range(4):
                if own:
                    pn = P.pst("pn", 6 + h % 2, 0, [128, 257])
                    mm(pn, pn[:], [(SM_c[:, h, :], vw_c[:, h, :]), (qT[:, h, cols], Cd[h][:])], [SM_c, vw_c, qT, Cd[h]])
                    act(st_c[:, h:h + 1], pn[:, 256:257], AF.Abs, [pn], [st_c])
                    ts("dve", st_c[:, h:h + 1], st_c[:, h:h + 1], FT[:, c, h:h + 1], None, ALU.max, None, [st_c, FT], [st_c])
                    recip(st_c[:, h:h + 1], st_c[:, h:h + 1], [st_c], [st_c])
                    amul(hm_c[:, h, :], pn[:, 0:256], st_c[:, h:h + 1], [pn, st_c], [hm_c])
                    act(sq2[:], hm_c[:, h, :], AF.Square, [hm_c], [sq2, st_c], accum_out=st_c[:, 4 + h:5 + h])
                pu = P.pst("pu", 6 + (h + 1) % 2, 0, [128, 257])
                mm(pu, pu[:], [(ktok_c[:, h, :], vw_c[:, h, :])], [ktok_c, vw_c])
                stt("dve", Cst[h][:], Cst[h][:], decB[:, h, c:c + 1], pu[:], ALU.mult, ALU.add, [Cst[h], decB, pu], [Cst[h]])
                if c + 1 < NCH:
                    amul(Cd[h][:], Cst[h][:], decS[:, h, c + 1:c + 2], [Cst[h], decS], [Cd[h]])
            if own:
                rsqrt(st_c[:, 4:8], st_c[:, 4:8], 1.0 / 256, [st_c], [st_c])
                tt("pool", t2[:], gmn[:], og[:], ALU.mult, [gmn, og], [t2])
                tt("dve", hm_c[:], hm_c[:], st_c[:, 4:8].unsqueeze(2).to_broadcast([128, 4, 256]), ALU.mult, [hm_c, st_c], [hm_c])
                tt("dve", ymb[:], hm_c[:].rearrange("p a b -> p (a b)"), t2[:], ALU.mult, [hm_c, t2], [ymb])

                def tail(ymT_t=ymT_t, cols=cols):
                    for half in range(2):
                        pty = P.pst("pty", 5, 0, [128, 4, 128], BF16)
                        for k4 in range(4):
                            kc = half * 4 + k4
                            tr(pty, pty[:, k4, :], ymb[:, kc * 128:(kc + 1) * 128], identb[:], [ymb, identb], inc=(k4 == 3))
                        cp("act" if half else "dve", ymT_t[:, half * 4:half * 4 + 4, cols], pty[:], [pty], [ymT_t])
                if cc == 3:
                    tail()
                else:
                    pend_tail[0] = tail
            for a_ in nxt[cc * per:(cc + 1) * per]:
                proj_group(*a_)
        if own:
            P.dma("pool", YTm[t - NTL_PRE], ymT_t[:], reads=[ymT_t], on=ymT_t)
    if upto <= 2:
        return finish(nc, P)

    P.phase()
    grp3 = T("grp3", None)
    w_dk = load_w("w_dk", w_in[:, O_DK:O_DK + 1024], 1024, 1024, grp3)
    w_dv = load_w("w_dv", w_in[:, O_DV:O_DV + 1024], 1024, 1024, grp3)
    w_dq = load_w("w_dq", w_in[:, O_DQ:O_DQ + 1024], 1024, 1024, grp3)
    P.seal(grp3, [w_dk, w_dv, w_dq])
    hT3 = [P.sb("hT3_%d" % i, [128, 8, 512], BF16) for i in range(2)]
    kt3 = [P.sb("kt3_%d" % i, [128, 8, 512], BF16) for i in range(2)]
    qt3 = [P.sb("qt3_%d" % i, [128, 8, 512], BF16) for i in range(2)]
    vt3 = [P.sb("vt3_%d" % i, [128, 4, 8, 129], BF16) for i in range(2)]
    sq3 = [P.sb("sq3_%d" % i, [128, 512], BF16) for i in range(2)]
    rs3 = [P.sb("rs3_%d" % i, [128, 512]) for i in range(2)]
    for i in range(2):
        mset("pool", vt3[i][:, :, :, 128:129], 1.0, [vt3[i]])
    n3 = 0
    for t in range(NTL_ALL):
        own = t >= NTL_PRE
        hT = hT3[t % 2]
        P.dma("sp", hT[:], HT[t], writes=[hT])
        for (wsel, gcol, dstl, dram_T, tcol, enabled) in ((w_dk, gk2, kt3, KT, t * 512, True), (w_dq, gq2, qt3, QT, (t - NTL_PRE) * 512, own)):
            if not enabled:
                continue
            dst = dstl[t % 2]
            pend = []
            for h in range(10):
                if h < 8:
                    pk = P.pst("pk3", n3 % 4, 0, [128, 512])
                    mm(pk, pk[:], [(wsel[:, kc, h * 128:(h + 1) * 128], hT[:, kc, :]) for kc in range(8)], [wsel, hT])
                    pend.append((pk, h, n3))
                    n3 += 1
                if h >= 2:
                    pk_, h_, n_ = pend.pop(0)
                    pss = P.pst("pss3", 4 + n_ % 2, 0, [128, 512])
                    fm_norm(pk_, pk_[:], 512, blkb, 1.0 / 64, gcol, dst, dst[:, h_, :], sq3[n_ % 2], pss, rs3[n_ % 2])
            P.dma("pool", dram_T[:, :, tcol:tcol + 512].rearrange("h p n -> p h n"), dst[:], reads=[dst], on=dst)
        vt = vt3[t % 2]
        for j in range(4):
            for g in range(2):
                pv = P.pst("pv3", 6 + g, 0, [128, 512])
                mm(pv, pv[:], [(hT[:, kc, j * 128:(j + 1) * 128], w_dv[:, kc, g * 512:(g + 1) * 512]) for kc in range(8)], [hT, w_dv])
                cp("act" if g else "dve", vt[:, j, 4 * g:4 * g + 4, 0:128], pv[:].rearrange("p (a b) -> p a b", a=4), [pv], [vt])
            P.dma("pool", VA[:, :, 4 * t + j, :].rearrange("h p e -> p h e"), vt[:, j, :, :], reads=[vt], on=vt)

    P.phase()
    kh = [[P.sb("kh%d_%d" % (i, c), [128, NTA], BF16) for c in range(2)] for i in range(2)]
    for i in range(2):
        mset("pool", kh[i][0][64:128, :], 0.0, [kh[i][0]])
        mset("pool", kh[i][1][0:64, :], 0.0, [kh[i][1]])
    vh = [P.sb("vh%d" % i, [128, NBLK, 129], BF16) for i in range(2)]
    qh = [P.sb("qh%d" % i, [128, NT], BF16) for i in range(2)]
    NST = 4
    LOOK = 3
    NPT = 6
    pTb = [P.sb("pT%d" % i, [128, 512], BF16) for i in range(NPT)]
    rr = [P.sb("rr%d" % i, [128, 12]) for i in range(2)]
    od = [P.sb("od%d" % i, [128, 4, 128]) for i in range(2)]
    odb = [P.sb("odb%d" % i, [128, 4, 128], BF16) for i in range(2)]
    sq4 = P.sb("sq4", [128, 128])
    ydT = [P.sb("ydT%d" % i, [128, 512], BF16) for i in range(2)]
    stb = [P.pst("st%d" % i, i, 0, [128, 512]) for i in range(NST)]
    accb = [P.pst("accb%d" % i, 5 + i, 0, [128, 512]) for i in range(3)]
    accs = {}
    for c in range(2):
        for i in range(4):
            n = c * 4 + i
            accs[(c, i)] = (accb[n // 3], (n % 3) * 129)
    ptd = P.pst("ptd", 4, 0, [128, 512], BF16)
    accS = [P.sb("accS%d" % i, [128, 8, 129]) for i in range(2)]
    nq = 0
    pending_e2 = None

    def zero_acc():
        mset("dve", accb[0][:, 0:387], 0.0, [accb[0]])
        mset("dve", accb[1][:, 0:387], 0.0, [accb[1]])
        mset("dve", accb[2][:, 0:258], 0.0, [accb[2]])
    zero_acc()
    for h in range(8):
        k_h, v_h, q_h = kh[h % 2], vh[h % 2], qh[h % 2]
        P.dma("sp", k_h[0][0:64, :], KT[h][0:64, :], writes=[k_h[0]])
        P.dma("sp", k_h[1][64:128, :], KT[h][64:128, :], writes=[k_h[1]])
        P.dma("sp", v_h[:], VA[h], writes=[v_h])
        P.dma("sp", q_h[:], QT[h], writes=[q_h])
        for t in range(NTL_OWN):
            nkb = NPB + 4 * t + 4
            nun = 2 * nkb
            par = nq % 2
            r_ = rr[par]; od_ = od[par]; ob_ = odb[par]; aS = accS[par]

            def s_mm(u):
                kb, c = u // 2, u % 2
                i0 = max(0, kb - (NPB + 4 * t))
                st = stb[u % NST]
                mm(st, st[:, i0 * 128:512], [(k_h[c][:, kb * 128:(kb + 1) * 128],
                                               q_h[:, t * 512 + i0 * 128:(t + 1) * 512])], [k_h[c], q_h])
            for u in range(min(LOOK, nun)):
                s_mm(u)
            for u in range(nun):
                if u == 6 and pending_e2 is not None:
                    pending_e2()
                    pending_e2 = None
                if u + LOOK < nun:
                    s_mm(u + LOOK)
                kb, c = u // 2, u % 2
                i0 = max(0, kb - (NPB + 4 * t))
                diag = kb >= NPB + 4 * t
                bias_ap = cs[:, C_AB:C_AB + 1] if kb < NPB else zcol[:, 0:1]
                st = stb[u % NST]
                pT = pTb[u % NPT]
                act(pT[:, i0 * 128:512], st[:, i0 * 128:512], AF.Exp, [st, cs, zcol], [pT], scale=0.125, bias=bias_ap)
                if diag:
                    tt("dve", pT[:, i0 * 128:(i0 + 1) * 128], pT[:, i0 * 128:(i0 + 1) * 128], trib[:], ALU.mult, [pT, trib], [pT])
                for i in range(i0, 4):
                    a, lo = accs[(c, i)]
                    mm(a, a[:, lo:lo + 129], [(pT[:, i * 128:(i + 1) * 128], v_h[:, kb, :])], [pT, v_h], start=False, skip=True,
                       inc=(i == 3))
            cp("dve", aS[:, 0:3, :], accb[0][:, 0:387].rearrange("p (a b) -> p a b", a=3), [accb[0]], [aS])
            cp("act", aS[:, 3:6, :], accb[1][:, 0:387].rearrange("p (a b) -> p a b", a=3), [accb[1]], [aS])
            cp("dve", aS[:, 6:8, :], accb[2][:, 0:258].rearrange("p (a b) -> p a b", a=2), [accb[2]], [aS])
            zero_acc()
            recip(r_[:, 0:8], aS[:, :, 128], [aS], [r_])
            ts("dve", r_[:, 4:8], r_[:, 4:8], lam[:, 3:4], None, ALU.mult, None, [r_, lam], [r_])
            for i in range(4):
                ts("dve", od_[:, i, :], aS[:, i, 0:128], r_[:, i:i + 1], None, ALU.mult, None, [aS, r_], [od_])
                stt("dve", od_[:, i, :], aS[:, 4 + i, 0:128], r_[:, 4 + i:5 + i], od_[:, i, :], ALU.mult, ALU.add, [aS, r_, od_], [od_])
            yd = ydT[par]; nq += 1

            def e2(r_=r_, od_=od_, ob_=ob_, yd=yd, t=t, h=h):
                for i in range(4):
                    P.op("dve", lambda e, i=i: e.scalar_tensor_tensor(out=sq4[:], in0=od_[:, i, :], scalar=1.0, in1=od_[:, i, :],
                                                                       op0=ALU.mult, op1=ALU.mult, accum_out=r_[:, 8 + i:9 + i]),
                         reads=[od_], writes=[sq4, r_])
                rsqrt(r_[:, 8:12], r_[:, 8:12], 1.0 / 128, [r_], [r_])
                for i in range(4):
                    stt("dve", ob_[:, i, :], od_[:, i, :], r_[:, 8 + i:9 + i], gsub[:], ALU.mult, ALU.mult, [od_, r_, gsub], [ob_])
                    tr(ptd, ptd[:, i * 128:(i + 1) * 128], ob_[:, i, :], identb[:], [ob_, identb], inc=(i == 3))
                cp("dve", yd[:], ptd[:], [ptd], [yd])
                P.dma("pool", YTd[t][:, h, :], yd[:], reads=[yd], on=yd)
            pending_e2 = e2
    if pending_e2 is not None:
        pending_e2()
    if upto <= 3:
        return finish(nc, P)

    P.phase()
    grp4 = T("grp4", None)
    w_cq = load_w("w_cq", w_in[:, O_CQ:O_CQ + 512], 1024, 512, grp4)
    wkv = load_w("wkv", w_mem_kv, 1024, 1536, grp4)
    gmem = bload("gmem", mem_norm_g, 1024, grp4)
    memt = P.sb("memt", [128, 2, 1024])
    P.dma("sp", memt[:], mem.rearrange("(j p) d -> p j d", p=128), writes=[memt], on=grp4)
    P.seal(grp4, [w_cq, wkv, gmem, memt])
    mkT = P.sb("mkT", [128, 4, 256], BF16)
    mva = P.sb("mva", [128, 2, 4, 257], BF16)
    hT4 = [P.sb("hT4_%d" % i, [128, 8, 512], BF16) for i in range(2)]
    qc = [P.sb("qc%d" % i, [128, 512], BF16) for i in range(2)]
    sq5 = [P.sb("sq5_%d" % i, [128, 512], BF16) for i in range(2)]
    rs5 = [P.sb("rs5_%d" % i, [128, 512]) for i in range(2)]
    pc = [[P.sb("pc%d_%d" % (i, mb), [128, 512], BF16) for mb in range(2)] for i in range(2)]
    rc = [P.sb("rc%d" % i, [128, 1]) for i in range(2)]
    ycb = [P.sb("ycb%d" % i, [128, 256], BF16) for i in range(2)]
    ycT = [P.sb("ycT%d" % i, [128, 8, 512], BF16) for i in range(2)]
    pq4 = [P.pst("pq4", i, 0, [128, 512]) for i in range(2)]
    pss4 = P.pst("pss4", 2, 0, [128, 512])
    ps4 = [P.pst("ps4", 3 + i, 0, [128, 512]) for i in range(2)]
    mscr = P.sb("mscr", [128, 1024]); mss = P.sb("mss", [128, 4])
    mhb = P.sb("mhb", [128, 2, 1024], BF16)
    mhT = P.sb("mhT", [128, 8, 256], BF16)
    rms_rows(memt, lambda j: memt[:, j, :], 2, gmem, mhb, lambda j: mhb[:, j, :], mscr, mss)
    for kc in range(8):
        pt = P.pst("pt_m", 7, 256 + 128 * (kc % 2), [128, 256], BF16)
        for j in range(2):
            tr(pt, pt[:, j * 128:(j + 1) * 128], mhb[:, j, kc * 128:(kc + 1) * 128], identb[:], [mhb, identb], inc=(j == 1))
        cp("act" if kc % 2 else "dve", mhT[:, kc, :], pt[:], [pt], [mhT])
    for h in range(4):
        pk = pq4[h % 2]
        mm(pk, pk[:, 0:256], [(wkv[:, kc, h * 128:(h + 1) * 128], mhT[:, kc, :]) for kc in range(8)], [wkv, mhT])
        fm_norm(pk, pk[:, 0:256], 256, oneb, 1.0 / 128, gck, mkT, mkT[:, h, :], sq5[h % 2], pss4, rs5[h % 2])
    mset("dve", mva[:, :, :, 256:257], 1.0, [mva])
    for j in range(2):
        for g in range(2):
            pv = ps4[g]
            mm(pv, pv[:], [(mhT[:, kc, j * 128:(j + 1) * 128], wkv[:, kc, 512 + g * 512:512 + (g + 1) * 512]) for kc in range(8)], [mhT, wkv])
            cp("act" if g else "dve", mva[:, j, 2 * g:2 * g + 2, 0:256], pv[:].rearrange("p (a b) -> p a b", a=2), [pv], [mva])
    n4 = 0
    for t in range(NTL_OWN):
        hT = hT4[t % 2]
        P.dma("sp", hT[:], HT[NTL_PRE + t], writes=[hT])
        yT = ycT[t % 2]
        pq_next = pq4[n4 % 2]
        mm(pq_next, pq_next[:], [(w_cq[:, kc, 0:128], hT[:, kc, :]) for kc in range(8)], [w_cq, hT])
        for h in range(4):
            pq = pq_next
            if h < 3:
                pq_next = pq4[(n4 + 1) % 2]
                mm(pq_next, pq_next[:], [(w_cq[:, kc, (h + 1) * 128:(h + 2) * 128], hT[:, kc, :]) for kc in range(8)], [w_cq, hT])
            q_ = qc[n4 % 2]
            fm_norm(pq, pq[:], 512, oneb, 1.0 / 128, gcq, q_, q_[:], sq5[n4 % 2], pss4, rs5[n4 % 2])
            for mb in range(2):
                ps_ = ps4[mb]
                mm(ps_, ps_[:], [(mkT[:, h, mb * 128:(mb + 1) * 128], q_[:])], [mkT, q_])
                act(pc[n4 % 2][mb][:], ps_[:], AF.Exp, [ps_], [pc[n4 % 2][mb]], scale=SC)
            for i in range(4):
                pa = P.pst("pa4", 5 + i % 2, 0, [128, 257])
                mm(pa, pa[:], [(pc[n4 % 2][mb][:, i * 128:(i + 1) * 128], mva[:, mb, h, :]) for mb in range(2)], [pc[n4 % 2][0], pc[n4 % 2][1], mva])
                r_ = rc[i % 2]; yb = ycb[i % 2]
                recip(r_[:], pa[:, 256:257], [pa], [r_])
                amul(yb[:], pa[:, 0:256], r_[:, 0:1], [pa, r_], [yb])
                pty = P.pst("pty4", 7, (i % 2) * 128, [128, 2, 128], BF16)
                for k2 in range(2):
                    tr(pty, pty[:, k2, :], yb[:, k2 * 128:(k2 + 1) * 128], identb[:], [yb, identb], inc=(k2 == 1))
                cp("dve", yT[:, 2 * h:2 * h + 2, i * 128:(i + 1) * 128], pty[:], [pty], [yT])
            n4 += 1
        P.dma("pool", YTc[t], yT[:], reads=[yT], on=yT)
    if upto <= 4:
        return finish(nc, P)

    P.phase()
    grp5 = T("grp5", None)
    w_g = load_w("w_g", w_gate, 1024, 3072, grp5)
    w_pj = [load_w("w_pj%d" % i, w, 1024, 1024, grp5) for i, w in enumerate((w_proj_m, w_proj_d, w_proj_c))]
    w_ot = load_w("w_ot", w_out, 1024, 1024, grp5)
    bgb = P.sb("bgb", [1, 3072], BF16)
    P.dma("pool", bgb[:], b_gate, writes=[bgb], on=grp5)
    P.seal(grp5, [w_g, w_ot, bgb] + w_pj)
    hT5 = [P.sb("hT5_%d" % i, [128, 8, 512], BF16) for i in range(2)]
    yT5 = [P.sb("yT5_%d" % b, [128, 8, 512], BF16) for b in range(3)]
    x5 = [P.sb("x5_%d" % i, [128, 1024]) for i in range(2)]
    gs5 = [P.sb("gs5_%d" % i, [128, 512]) for i in range(2)]
    tm5 = [P.sb("tm5_%d" % i, [128, 512]) for i in range(2)]
    mg5 = P.sb("mg5", [128, 1024])
    mgb = [P.sb("mgb%d" % i, [128, 1024], BF16) for i in range(2)]
    mT5 = [P.sb("mT5_%d" % i, [128, 8, 128], BF16) for i in range(2)]
    YTs = (YTm, YTd, YTc)
    n5 = 0
    for t in range(NTL_OWN):
        hT = hT5[t % 2]
        P.dma("sp", hT[:], HT[NTL_PRE + t], writes=[hT])
        for b in range(3):
            P.dma("sp", yT5[b][:], YTs[b][t], writes=[yT5[b]])
        for j in range(4):
            sub = t * 4 + j
            cols = slice(j * 128, (j + 1) * 128)
            x_ = x5[sub % 2]; mg = mg5; mb_ = mgb[sub % 2]
            P.dma("sp", x_[:], x_own[sub * 128:(sub + 1) * 128, :], writes=[x_])
            for b in range(3):
                yT = yT5[b]
                for g in range(2):
                    pg = P.pst("pg5", n5 % 2, 0, [128, 512])
                    pp = P.pst("pp5", 2 + n5 % 2, 0, [128, 512])
                    gcols = slice(b * 1024 + g * 512, b * 1024 + (g + 1) * 512)
                    mm(pg, pg[:], [(hT[:, kc, cols], w_g[:, kc, gcols]) for kc in range(8)] + [(oneb[0:1, 0:128], bgb[0:1, gcols])], [hT, w_g, oneb, bgb])
                    mm(pp, pp[:], [(yT[:, kc, cols], w_pj[b][:, kc, g * 512:(g + 1) * 512]) for kc in range(8)], [yT, w_pj[b]])
                    gs = gs5[n5 % 2]; tm = tm5[n5 % 2]
                    act(gs[:], pg[:], AF.Sigmoid, [pg], [gs])
                    mcol = mg[:, g * 512:(g + 1) * 512]
                    if b == 0:
                        tt("dve", mcol, gs[:], pp[:], ALU.mult, [gs, pp], [mg])
                    else:
                        tt("dve", tm[:], gs[:], pp[:], ALU.mult, [gs, pp], [tm])
                        if b == 1:
                            tt("pool", mcol, mcol, tm[:], ALU.add, [mg, tm], [mg])
                        else:
                            tt("pool", mb_[:, g * 512:(g + 1) * 512], mcol, tm[:], ALU.add, [mg, tm], [mb_])
                    n5 += 1
            mT = mT5[sub % 2]
            ptm = P.pst("ptm5", 4, 0, [128, 8, 128], BF16)
            for kc in range(8):
                tr(ptm, ptm[:, kc, :], mb_[:, kc * 128:(kc + 1) * 128], identb[:], [mb_, identb], inc=(kc == 7))
            cp("act", mT[:], ptm[:], [ptm], [mT])
            for g in range(2):
                po = P.pst("po5", 5 + g, 0, [128, 512])
                mm(po, po[:], [(mT[:, kc, :], w_ot[:, kc, g * 512:(g + 1) * 512]) for kc in range(8)], [mT, w_ot])
                tt("dve", x_[:, g * 512:(g + 1) * 512], po[:], x_[:, g * 512:(g + 1) * 512], ALU.add, [po, x_], [x_])
            P.dma("pool", XM[sub * 128:(sub + 1) * 128, :], x_[:], reads=[x_], on=x_)
    if upto <= 5:
        return finish(nc, P)

    P.phase()
    grp6 = T("grp6", None)
    gmlp = bload("gmlp", norm_mlp_g, 1024, grp6)
    P.seal(grp6, [gmlp])
    wu = [load_w("wu%d" % i, w_up[:, i * 1024:(i + 1) * 1024], 1024, 1024, T("g6u%d" % i, None)) for i in range(4)]
    wd = [load_w("wd%d" % i, w_down[i * 1024:(i + 1) * 1024, :], 1024, 1024, T("g6d%d" % i, None)) for i in range(4)]
    T6 = 256
    NS6 = T6 // 128
    xt6 = [P.sb("xt6_%d" % i, [128, NS6, 1024]) for i in range(2)]
    hb6 = P.sb("hb6", [128, NS6, 1024], BF16)
    hT6 = P.sb("hT6", [128, 8, T6], BF16)
    ss6 = P.sb("ss6", [128, 4]); scr6 = P.sb("scr6", [128, 1024])
    uT = P.sb("uT", [128, 32, T6], BF16)
    rl = [P.sb("rl%d" % i, [128, T6], BF16) for i in range(2)]
    ob6 = [P.sb("ob6_%d" % i, [128, 512]) for i in range(2)]
    n6 = 0
    for t in range(NT // T6):
        xt = xt6[t % 2]
        P.dma("sp", xt[:], XM[t * T6:(t + 1) * T6, :].rearrange("(j p) d -> p j d", p=128), writes=[xt])
        rms_rows(xt, lambda j, xt=xt: xt[:, j, :], NS6, gmlp, hb6, lambda j: hb6[:, j, :], scr6, ss6)
        for kc in range(8):
            pt = P.pst("pt6", kc % 2, 0, [128, T6], BF16)
            for j in range(NS6):
                tr(pt, pt[:, j * 128:(j + 1) * 128], hb6[:, j, kc * 128:(kc + 1) * 128], identb[:], [hb6, identb], inc=(j == NS6 - 1))
            cp("act" if kc % 2 else "dve", hT6[:, kc, :], pt[:], [pt], [hT6])
        for fc in range(32):
            pu = P.pst("pu6", 2 + fc % 3, 0, [128, T6])
            wsel = wu[fc // 8]
            mm(pu, pu[:], [(wsel[:, kc, (fc % 8) * 128:(fc % 8 + 1) * 128], hT6[:, kc, :]) for kc in range(8)], [wsel, hT6])
            r_ = rl[fc % 2]
            act(r_[:], pu[:], AF.Relu, [pu], [r_])
            tt("pool" if fc % 2 else "dve", uT[:, fc, :], r_[:], r_[:], ALU.mult, [r_], [uT])
        for j in range(NS6):
            for g in range(2):
                pd = P.pst("pd6", 5 + n6 % 3, 0, [128, 512])
                mm(pd, pd[:], [(uT[:, fc, j * 128:(j + 1) * 128], wd[fc // 8][:, fc % 8, g * 512:(g + 1) * 512]) for fc in range(32)], [uT] + wd)
                o_ = ob6[n6 % 2]
                tt("dve", o_[:], pd[:], xt[:, j, g * 512:(g + 1) * 512], ALU.add, [pd, xt], [o_])
                r0 = t * T6 + j * 128
                P.dma("pool", out[r0:r0 + 128, g * 512:(g + 1) * 512], o_[:], reads=[o_], on=o_)
                n6 += 1
    return finish(nc, P)


def finish(nc, P):
    P.barrier()
    P.emit()
    P.es.close()
    return nc


def make_consts(pre_valid):
    c = np.zeros((128, NCST), np.float32)
    c[:, C_ID:C_ID + 128] = np.eye(128, dtype=np.float32)
    tri = np.triu(np.ones((128, 128), np.float32))
    c[:, C_TRI:C_TRI + 128] = tri
    c[:, C_TRIS:C_TRIS + 128] = tri * np.float32(128 ** -0.5)
    blk = np.zeros((128, 128), np.float32)
    blk[:64, :64] = 1.0
    blk[64:, 64:] = 1.0
    c[:, C_BLK:C_BLK + 128] = blk
    c[:, C_ONE:C_ONE + 128] = 1.0
    c[:, C_PF] = 1.0 if pre_valid else 0.0
    c[:, C_NB] = 0.0 if pre_valid else -1.0e30
    c[:, C_AB] = 0.0 if pre_valid else -30000.0
    c[0:4, C_BD:C_BD + 4] = np.eye(4, dtype=np.float32)
    return c


_NC_CACHE = {}


def run(inputs, NT, n_pairs, upto=99, dbg=False):
    f = lambda a: np.ascontiguousarray(np.asarray(a, dtype=np.float32))
    key = (NT, upto, dbg)
    if key not in _NC_CACHE:
        _NC_CACHE[key] = build(NT, upto, dbg)
    nc = _NC_CACHE[key]
    x = f(inputs["x"]); mem = f(inputs["mem"])
    shared = {
        "norm_mix_g": f(inputs["norm_mix_g"]).reshape(1, 1024), "w_in": f(inputs["w_in"]).reshape(1024, IN_COLS),
        "b_igate": f(inputs["b_igate"]).reshape(4, 1), "b_fgate": f(inputs["b_fgate"]).reshape(4, 1),
        "conv_w": f(inputs["conv_w"]).reshape(4, 1024), "conv_b": f(inputs["conv_b"]).reshape(8, 128),
        "m_norm_g": f(inputs["m_norm_g"]).reshape(1, 1024),
        "dq_norm_g": f(inputs["dq_norm_g"]).reshape(64, 1), "dk_norm_g": f(inputs["dk_norm_g"]).reshape(64, 1),
        "lam_q1": f(inputs["lam_q1"]).reshape(1, 64), "lam_k1": f(inputs["lam_k1"]).reshape(1, 64),
        "lam_q2": f(inputs["lam_q2"]).reshape(1, 64), "lam_k2": f(inputs["lam_k2"]).reshape(1, 64),
        "subln_g": f(inputs["subln_g"]).reshape(1, 128),
        "cq_norm_g": f(inputs["cq_norm_g"]).reshape(128, 1), "ck_norm_g": f(inputs["ck_norm_g"]).reshape(128, 1),
        "mem_norm_g": f(inputs["mem_norm_g"]).reshape(1, 1024), "w_mem_kv": f(inputs["w_mem_kv"]).reshape(1024, 1536),
        "w_gate": f(inputs["w_gate"]).reshape(1024, 3072), "b_gate": f(inputs["b_gate"]).reshape(1, 3072),
        "w_proj_m": f(inputs["w_proj_m"]).reshape(1024, 1024), "w_proj_d": f(inputs["w_proj_d"]).reshape(1024, 1024),
        "w_proj_c": f(inputs["w_proj_c"]).reshape(1024, 1024), "w_out": f(inputs["w_out"]).reshape(1024, 1024),
        "norm_mlp_g": f(inputs["norm_mlp_g"]).reshape(1, 1024),
        "w_up": f(inputs["w_up"]).reshape(1024, 4096), "w_down": f(inputs["w_down"]).reshape(4096, 1024),
    }
    in_maps = []
    for b in range(n_pairs):
        for half in range(2):
            d = dict(shared)
            d["x_own"] = np.ascontiguousarray(x[b, half * NT:(half + 1) * NT])
            d["x_pre"] = np.ascontiguousarray(x[b, 0:NT]) if half == 1 else np.zeros((NT, 1024), np.float32)
            d["mem"] = np.ascontiguousarray(mem[b])
            d["cst"] = make_consts(half == 1)
            in_maps.append(d)
    res = run_bass_kernel_spmd(nc, in_maps, core_ids=list(range(2 * n_pairs)))
    return res


def kernel(**inputs):
    res = run(inputs, 4096, 4)
    outp = np.empty((4, 8192, 1024), np.float32)
    for b in range(4):
        for half in range(2):
            outp[b, half * 4096:(half + 1) * 4096] = res.results[2 * b + half]["out"]
    return outp
```

```python
import math
import numpy as np
import concourse.bass as bass
import concourse.mybir as mybir
from concourse.bass_utils import run_bass_kernel_spmd
from contextlib import ExitStack

F32 = mybir.dt.float32
BF16 = mybir.dt.bfloat16
AF = mybir.ActivationFunctionType
ALU = mybir.AluOpType

COMPUTE = ("pe", "act", "dve", "pool")
ALLENG = COMPUTE + ("sp",)
EPS = 1e-6
ARENA_WORDS = 53000


class T:
    __slots__ = ("name", "h", "w", "r", "dkey", "dcnt", "g", "sub")

    def __init__(self, name, h):
        self.name = name
        self.h = h
        self.w = None
        self.r = {}
        self.dkey = None
        self.dcnt = 0
        self.g = None
        self.sub = None

    def __getitem__(self, k):
        return self.h[k]


class PV:
    __slots__ = ("h", "banks")

    def __init__(self, h, banks):
        self.h = h
        self.banks = banks

    def __getitem__(self, k):
        return self.h[k]


class Prog:
    def __init__(self, nc):
        self.nc = nc
        self.es = ExitStack()
        self.q = {e: [] for e in ALLENG}
        self.cnt = {e: 0 for e in COMPUTE}
        self.waited = {e: {} for e in ALLENG}
        self.sems = {}
        self.pe_pending = []
        self.dma_tiles = []
        self.nkeys = 0
        self.arena = self.es.enter_context(nc.sbuf_tensor("arena", [128, ARENA_WORDS], F32))
        self.pairs = [self.es.enter_context(nc.psum_tensor("pbank%d" % i, [128, 1024], F32)) for i in range(4)]
        self.banks = [self.pairs[i // 2][:, (i % 2) * 512:(i % 2 + 1) * 512] for i in range(8)]
        self.bankT = [T("bank%d" % i, None) for i in range(8)]
        self.top = 0
        self.persist_top = 0
        self.pcache = {}

    @staticmethod
    def _shape_view(ap, shape):
        if len(shape) == 2:
            return ap
        if len(shape) == 3:
            return ap.rearrange("p (a b) -> p a b", a=shape[1])
        if len(shape) == 4:
            return ap.rearrange("p (a b c) -> p a b c", a=shape[1], b=shape[2])
        raise ValueError(shape)

    def sb(self, name, shape, dt=F32):
        n = int(np.prod(shape[1:]))
        words = n if dt == F32 else (n + 1) // 2
        off = self.top
        self.top = off + ((words + 15) // 16) * 16
        assert self.top <= ARENA_WORDS, ("SBUF overflow", name, self.top)
        ap = self.arena[0:shape[0], off:off + words]
        if dt != F32:
            ap = ap.bitcast(dt)
            if n != 2 * words:
                ap = ap[:, 0:n]
        return T(name, self._shape_view(ap, shape))

    def pst(self, name, bank, lo, shape, dt=F32):
        n = int(np.prod(shape[1:]))
        words = n if dt == F32 else (n + 1) // 2
        assert lo + words <= 512
        key = (bank, lo, words, tuple(shape), dt)
        t = self.pcache.get(key)
        if t is not None:
            return t
        ap = self.banks[bank][0:shape[0], lo:lo + words]
        if dt != F32:
            ap = ap.bitcast(dt)
        t = PV(self._shape_view(ap, shape), (self.bankT[bank],))
        self.pcache[key] = t
        return t

    def pst2(self, pair):
        key = ("pair", pair)
        t = self.pcache.get(key)
        if t is None:
            t = PV(self.pairs[pair][:, :].rearrange("p (a b) -> p a b", a=2), (self.bankT[2 * pair], self.bankT[2 * pair + 1]))
            self.pcache[key] = t
        return t

    def dram(self, name, shape, dt, kind="Internal"):
        h = self.nc.dram_tensor(name, list(shape), dt, kind=kind)
        return T(name, h.ap())

    def _sem(self, key):
        s = self.sems.get(key)
        if s is None:
            s = self.es.enter_context(self.nc.semaphore("s_" + str(key)))
            self.sems[key] = s
        return s

    def _deps(self, eng, reads, writes):
        waits = {}
        wd = self.waited[eng]

        def need(k, v):
            if eng == "pe" and k == "pe":
                return
            if wd.get(k, 0) >= v:
                return
            if waits.get(k, 0) < v:
                waits[k] = v

        for t in reads:
            if t.w is not None:
                need(*t.w)
        for t in writes:
            if t.w is not None:
                need(*t.w)
            for k, v in t.r.items():
                need(k, v)
        for k, v in waits.items():
            wd[k] = v
        return list(waits.items())

    @staticmethod
    def _mark(tok, reads, writes):
        k, v = tok
        for t in reads:
            if t.r.get(k, 0) < v:
                t.r[k] = v
        for t in writes:
            t.w = tok
            t.r = {}

    def op(self, eng, fn, reads=(), writes=(), inc=True):
        r2, w2 = [], []
        for t in reads:
            if isinstance(t, PV):
                w2.extend(t.banks)
            else:
                r2.append(t)
        for t in writes:
            if isinstance(t, PV):
                w2.extend(t.banks)
            else:
                w2.append(t)
        reads, writes = r2, w2
        waits = self._deps(eng, reads, writes)
        if not inc:
            assert eng == "pe"
            self.q[eng].append((waits, fn, None))
            self.pe_pending.append((reads, writes))
            return
        self.cnt[eng] += 1
        tok = (eng, self.cnt[eng])
        self.q[eng].append((waits, fn, (eng, 1)))
        if eng == "pe" and self.pe_pending:
            for r, w in self.pe_pending:
                self._mark(tok, r, w)
            self.pe_pending = []
        self._mark(tok, reads, writes)

    def dma(self, queue, out, in_, reads=(), writes=(), on=None):
        if on is None:
            on = writes[0] if writes else reads[0]
        isgrp = on.h is None
        if on.sub is None:
            on.sub = {}
        if queue not in on.sub:
            on.sub[queue] = T(on.name + "_" + queue, None)
        on = on.sub[queue]
        if isgrp:
            for t in writes:
                t.g = on
        if on.dkey is None:
            self.nkeys += 1
            on.dkey = "d%d" % self.nkeys
            self.dma_tiles.append(on)
        waits = self._deps(queue, reads, [] if isgrp else writes)
        on.dcnt += 1
        tok = (on.dkey, 16 * on.dcnt)
        self.q[queue].append((waits, lambda e, o=out, i=in_: e.dma_start(out=o, in_=i), (on.dkey, 16)))
        self._mark(tok, reads, writes)
        return tok

    def seal(self, grp, tiles):
        for t in tiles:
            t.w = (t.g.dkey, 16 * t.g.dcnt)

    def barrier(self):
        assert not self.pe_pending
        toks = [(e, self.cnt[e]) for e in COMPUTE if self.cnt[e] > 0]
        toks += [(t.dkey, 16 * t.dcnt) for t in self.dma_tiles]
        for e in ALLENG:
            wd = self.waited[e]
            waits = []
            for k, v in toks:
                if k == e:
                    continue
                if wd.get(k, 0) < v:
                    waits.append((k, v))
                    wd[k] = v
            if waits:
                self.q[e].append((waits, None, None))
        self.dma_tiles = []
        self.pcache = {}

    def phase(self):
        self.barrier()
        self.top = self.persist_top

    def emit(self):
        nc = self.nc
        for e in ALLENG:
            for waits, fn, inc in self.q[e]:
                for k, _ in waits:
                    self._sem(k)
                if inc is not None:
                    self._sem(inc[0])
        engobj = {"pe": "tensor", "act": "scalar", "dve": "vector", "pool": "gpsimd", "sp": "sync"}
        with nc.Block() as block:
            for e in ALLENG:
                lst = self.q[e]
                if not lst:
                    continue

                def body(eng, lst=lst):
                    for waits, fn, inc in lst:
                        for k, v in waits:
                            eng.wait_ge(self.sems[k], v)
                        if fn is not None:
                            ins = fn(eng)
                            if inc is not None:
                                ins.then_inc(self.sems[inc[0]], inc[1])

                getattr(block, engobj[e])(body)


O_MQ, O_MK, O_MV, O_MI, O_MF, O_MO, O_DQ, O_DK, O_DV, O_CQ = 0, 512, 1024, 2048, 2052, 2056, 3080, 4104, 5128, 6152
IN_COLS = 6664
C_ID, C_TRI, C_BLK, C_PF, C_NB, C_AB, C_BD, C_ONE, C_TRIS, NCST = 0, 128, 256, 384, 385, 386, 387, 391, 519, 647
LAM_INIT = 0.8 - 0.6 * math.exp(-0.3 * 0)


def build(NT, upto=99, dbg=False):
    nc = bass.Bass("TRN2", target_bir_lowering=False)
    P = Prog(nc)
    NTA = 2 * NT
    NTL_ALL = NTA // 512
    NTL_OWN = NT // 512
    NTL_PRE = NT // 512
    NCH = NTA // 128
    NBLK = NTA // 128
    NPB = NT // 128
    SC = 128 ** -0.5

    def din(name, shape):
        return nc.dram_tensor(name, list(shape), F32, kind="ExternalInput").ap()

    x_own = din("x_own", [NT, 1024]); x_pre = din("x_pre", [NT, 1024]); mem = din("mem", [256, 1024])
    cst = din("cst", [128, NCST])
    norm_mix_g = din("norm_mix_g", [1, 1024]); w_in = din("w_in", [1024, IN_COLS])
    b_igate = din("b_igate", [4, 1]); b_fgate = din("b_fgate", [4, 1])
    conv_w = din("conv_w", [4, 1024]); conv_b = din("conv_b", [8, 128]); m_norm_g = din("m_norm_g", [1, 1024])
    dq_norm_g = din("dq_norm_g", [64, 1]); dk_norm_g = din("dk_norm_g", [64, 1])
    lam_q1 = din("lam_q1", [1, 64]); lam_k1 = din("lam_k1", [1, 64]); lam_q2 = din("lam_q2", [1, 64]); lam_k2 = din("lam_k2", [1, 64])
    subln_g = din("subln_g", [1, 128]); cq_norm_g = din("cq_norm_g", [128, 1]); ck_norm_g = din("ck_norm_g", [128, 1])
    mem_norm_g = din("mem_norm_g", [1, 1024]); w_mem_kv = din("w_mem_kv", [1024, 1536])
    w_gate = din("w_gate", [1024, 3072]); b_gate = din("b_gate", [1, 3072])
    w_proj_m = din("w_proj_m", [1024, 1024]); w_proj_d = din("w_proj_d", [1024, 1024]); w_proj_c = din("w_proj_c", [1024, 1024])
    w_out = din("w_out", [1024, 1024]); norm_mlp_g = din("norm_mlp_g", [1, 1024])
    w_up = din("w_up", [1024, 4096]); w_down = din("w_down", [4096, 1024])
    skind = "ExternalOutput" if dbg else "Internal"
    out = nc.dram_tensor("out", [NT, 1024], F32, kind="ExternalOutput").ap()

    HT = P.dram("HT", [NTL_ALL, 128, 8, 512], BF16, skind)
    YTm = P.dram("YTm", [NTL_OWN, 128, 8, 512], BF16, skind)
    YTd = P.dram("YTd", [NTL_OWN, 128, 8, 512], BF16, skind)
    YTc = P.dram("YTc", [NTL_OWN, 128, 8, 512], BF16, skind)
    KT = P.dram("KT", [8, 128, NTA], BF16, skind)
    QT = P.dram("QT", [8, 128, NT], BF16, skind)
    VA = P.dram("VA", [8, 128, NBLK, 129], BF16, skind)
    XM = P.dram("XM", [NT, 1024], F32, skind)

    def mm(outT, out_ap, pairs, reads, inc=True, start=True, skip=False):
        n = len(pairs)
        for i, (l, r) in enumerate(pairs):
            st = (i == 0) and start
            sp = (i == n - 1)
            if skip:
                f = lambda e, l=l, r=r, st=st, sp=sp: e.matmul(out=out_ap, lhsT=l, rhs=r, start=st, stop=sp, skip_group_check=True)
            else:
                f = lambda e, l=l, r=r, st=st, sp=sp: e.matmul(out=out_ap, lhsT=l, rhs=r, start=st, stop=sp)
            P.op("pe", f, reads=reads, writes=[outT], inc=(inc and i == n - 1))

    def tr(outT, out_ap, in_ap, ident_ap, reads, inc=True):
        P.op("pe", lambda e: e.transpose(out=out_ap, in_=in_ap, identity=ident_ap), reads=reads, writes=[outT], inc=inc)

    def act(out_ap, in_ap, func, reads, writes, **kw):
        P.op("act", lambda e: e.activation(out=out_ap, in_=in_ap, func=func, **kw), reads=reads, writes=writes)

    def amul(out_ap, in_ap, m_ap, reads, writes):
        P.op("act", lambda e: e.mul(out=out_ap, in_=in_ap, mul=m_ap), reads=reads, writes=writes)

    def tt(eng, out_ap, a, b, op, reads, writes):
        P.op(eng, lambda e: e.tensor_tensor(out=out_ap, in0=a, in1=b, op=op), reads=reads, writes=writes)

    def ts(eng, out_ap, a, s1, s2, op0, op1, reads, writes):
        if s2 is None:
            P.op(eng, lambda e: e.tensor_scalar(out=out_ap, in0=a, scalar1=s1, scalar2=None, op0=op0), reads=reads, writes=writes)
        else:
            P.op(eng, lambda e: e.tensor_scalar(out=out_ap, in0=a, scalar1=s1, scalar2=s2, op0=op0, op1=op1), reads=reads, writes=writes)

    def stt(eng, out_ap, a, s, b, op0, op1, reads, writes):
        P.op(eng, lambda e: e.scalar_tensor_tensor(out=out_ap, in0=a, scalar=s, in1=b, op0=op0, op1=op1), reads=reads, writes=writes)

    def cp(eng, out_ap, in_ap, reads, writes):
        if eng == "act":
            P.op("act", lambda e: e.copy(out=out_ap, in_=in_ap), reads=reads, writes=writes)
        else:
            P.op(eng, lambda e: e.tensor_copy(out=out_ap, in_=in_ap), reads=reads, writes=writes)

    def recip(out_ap, in_ap, reads, writes):
        P.op("dve", lambda e: e.reciprocal(out=out_ap, in_=in_ap), reads=reads, writes=writes)

    def mset(eng, ap, val, writes):
        P.op(eng, lambda e: e.memset(ap, val), writes=writes)

    def scan(out_T, out_ap, d0, d1_T, d1, init, op1, extra_reads=()):
        P.op("dve", lambda e: e.tensor_tensor_scan(out=out_ap, data0=d0, data1=d1, initial=init, op0=ALU.mult, op1=op1),
             reads=[d1_T] + list(extra_reads), writes=[out_T])

    def load_w(name, src, K, N, grp):
        t = P.sb(name, [128, K // 128, N], BF16)
        grp = T(name + "_g", None)
        v = src.rearrange("(kc p) n -> p kc n", p=128)
        step = max(1, 2048 // N) if N <= 2048 else 1
        for kc in range(0, K // 128, step):
            k1 = min(K // 128, kc + step)
            for c0 in range(0, N, 2048):
                c1 = min(N, c0 + 2048)
                P.dma("pool", t[:, kc:k1, c0:c1], v[:, kc:k1, c0:c1], writes=[t], on=grp)
        return t

    def bload(name, src, n, grp, queue="sp"):
        t = P.sb(name, [128, n])
        P.dma(queue, t[:], src.partition_broadcast(128), writes=[t], on=grp)
        return t

    def rsqrt(out_ap, in_ap, scale, reads, writes):
        act(out_ap, in_ap, AF.Ln, reads, writes, scale=scale, bias=EPS)
        act(out_ap, out_ap, AF.Exp, writes, writes, scale=-0.5)

    def rms_rows(xt_T, x_ap_fn, nsub, gB, hb_T, hb_ap_fn, scr_T, ss_T, D=1024):
        for j in range(nsub):
            act(scr_T[:], x_ap_fn(j), AF.Square, [xt_T], [scr_T, ss_T], accum_out=ss_T[:, j:j + 1])
        rsqrt(ss_T[:, 0:nsub], ss_T[:, 0:nsub], 1.0 / D, [ss_T], [ss_T])
        for j in range(nsub):
            stt("dve", hb_ap_fn(j), x_ap_fn(j), ss_T[:, j:j + 1], gB[:], ALU.mult, ALU.mult, [xt_T, ss_T, gB], [hb_T])

    def fm_norm(psT, ps_ap, n, onesb, inv_d, gcol, outT, out_ap, sqb, ssp, rs):
        act(sqb[:, 0:n], ps_ap, AF.Square, [psT], [sqb])
        mm(ssp, ssp[:, 0:n], [(onesb[:], sqb[:, 0:n])], [onesb, sqb])
        rsqrt(rs[:, 0:n], ssp[:, 0:n], inv_d, [ssp], [rs])
        stt("dve", out_ap, ps_ap, gcol[:, 0:1], rs[:, 0:n], ALU.mult, ALU.mult, [psT, gcol, rs], [outT])

    grp0 = T("grp0", None)
    cs = P.sb("cst", [128, NCST])
    P.dma("sp", cs[:], cst, writes=[cs], on=grp0)
    gq2 = P.sb("gq2", [128, 1]); gk2 = P.sb("gk2", [128, 1]); gcq = P.sb("gcq", [128, 1]); gck = P.sb("gck", [128, 1])
    for hf in range(2):
        P.dma("sp", gq2[64 * hf:64 * hf + 64, :], dq_norm_g, writes=[gq2], on=grp0)
        P.dma("sp", gk2[64 * hf:64 * hf + 64, :], dk_norm_g, writes=[gk2], on=grp0)
    P.dma("sp", gcq[:], cq_norm_g, writes=[gcq], on=grp0)
    P.dma("sp", gck[:], ck_norm_g, writes=[gck], on=grp0)
    gsub = bload("gsub", subln_g, 128, grp0)
    identb = P.sb("identb", [128, 128], BF16)
    blkb = P.sb("blkb", [128, 128], BF16)
    oneb = P.sb("oneb", [128, 128], BF16)
    trib = P.sb("trib", [128, 128], BF16)
    lam = P.sb("lam", [128, 4])
    zcol = P.sb("zcol", [128, 1])
    P.persist_top = P.top
    lamv = [bload("lam%d" % i, a, 64, grp0) for i, a in enumerate((lam_q1, lam_k1, lam_q2, lam_k2))]
    ljunk = P.sb("ljunk", [128, 64])
    P.seal(grp0, [cs, gq2, gk2, gcq, gck, gsub] + lamv)
    cp("dve", identb[:], cs[:, C_ID:C_ID + 128], [cs], [identb])
    cp("dve", blkb[:], cs[:, C_BLK:C_BLK + 128], [cs], [blkb])
    cp("dve", oneb[:], cs[:, C_ONE:C_ONE + 128], [cs], [oneb])
    cp("dve", trib[:], cs[:, C_TRI:C_TRI + 128], [cs], [trib])
    mset("dve", zcol[:], 0.0, [zcol])
    for i in range(2):
        tt("dve", ljunk[:], lamv[2 * i][:], lamv[2 * i + 1][:], ALU.mult, [lamv[2 * i], lamv[2 * i + 1]], [ljunk])
        act(ljunk[:], ljunk[:], AF.Copy, [ljunk], [ljunk, lam], accum_out=lam[:, i:i + 1])
    act(lam[:, 0:2], lam[:, 0:2], AF.Exp, [lam], [lam])
    tt("dve", lam[:, 2:3], lam[:, 0:1], lam[:, 1:2], ALU.subtract, [lam], [lam])
    ts("dve", lam[:, 3:4], lam[:, 2:3], LAM_INIT, -1.0, ALU.add, ALU.mult, [lam], [lam])
    ts("dve", gsub[:], gsub[:], 1.0 - LAM_INIT, None, ALU.mult, None, [gsub], [gsub])

    P.phase()
    grp1 = T("grp1", None)
    gmix = bload("gmix", norm_mix_g, 1024, grp1)
    P.seal(grp1, [gmix])
    xts = [P.sb("xt%d" % i, [128, 4, 1024]) for i in range(3)]
    hbs = [P.sb("hb%d" % i, [128, 4, 1024], BF16) for i in range(2)]
    hTs = [P.sb("hT%d" % i, [128, 8, 512], BF16) for i in range(2)]
    sss = [P.sb("ss%d" % i, [128, 4]) for i in range(2)]
    scr1 = P.sb("scr1", [128, 1024])
    for t in range(NTL_ALL):
        b = t % 2
        src = x_pre[t * 512:(t + 1) * 512, :] if t < NTL_PRE else x_own[(t - NTL_PRE) * 512:(t - NTL_PRE + 1) * 512, :]
        xt, hb, hT, ss = xts[t % 3], hbs[b], hTs[b], sss[b]
        P.dma("sp", xt[:], src.rearrange("(j p) d -> p j d", p=128), writes=[xt])
        rms_rows(xt, lambda j, xt=xt: xt[:, j, :], 4, gmix, hb, lambda j, hb=hb: hb[:, j, :], scr1, ss)
        for kc in range(8):
            pt = P.pst("pt1", kc % 8, 0, [128, 512], BF16)
            for j in range(4):
                tr(pt, pt[:, j * 128:(j + 1) * 128], hb[:, j, kc * 128:(kc + 1) * 128], identb[:], [hb, identb], inc=(j == 3))
            cp("act" if kc % 2 else "dve", hT[:, kc, :], pt[:], [pt], [hT])
        P.dma("pool", HT[t], hT[:], reads=[hT], on=hT)
    if upto <= 1:
        return finish(nc, P)

    P.phase()
    grp2 = T("grp2", None)
    w_i = load_w("w_i", w_in[:, O_MI:O_MI + 4], 1024, 4, grp2)
    w_f = load_w("w_f", w_in[:, O_MF:O_MF + 4], 1024, 4, grp2)
    bi = P.sb("bi", [4, 1]); bfn = P.sb("bfn", [4, 1])
    P.dma("sp", bi[:], b_igate, writes=[bi], on=grp2)
    P.dma("sp", bfn[:], b_fgate, writes=[bfn], on=grp2)
    cwl = P.sb("cwl", [4, 1024]); cbl = P.sb("cbl", [8, 128])
    P.dma("sp", cwl[:], conv_w, writes=[cwl], on=grp2)
    P.dma("sp", cbl[:], conv_b, writes=[cbl], on=grp2)
    cw = P.sb("cw", [128, 8, 4]); cb = P.sb("cb", [128, 8])
    gmn = bload("gmn", m_norm_g, 1024, grp2)
    w_q = load_w("w_q", w_in[:, O_MQ:O_MQ + 512], 1024, 512, grp2)
    w_k = load_w("w_k", w_in[:, O_MK:O_MK + 512], 1024, 512, grp2)
    w_v = load_w("w_v", w_in[:, O_MV:O_MV + 1024], 1024, 1024, grp2)
    w_o = load_w("w_o", w_in[:, O_MO:O_MO + 1024], 1024, 1024, grp2)
    P.seal(grp2, [w_i, w_f, bi, bfn, cwl, cbl, gmn, w_q, w_k, w_v, w_o])
    ts("dve", bfn[:], bfn[:], -1.0, None, ALU.mult, None, [bfn], [bfn])
    pcw = P.pst("pcw", 3, 0, [128, 8, 4]); pcb = P.pst("pcb", 3, 64, [128, 8])
    for g in range(8):
        tr(pcw, pcw[:, g, :], cwl[0:4, g * 128:(g + 1) * 128], cs[0:4, C_ID:C_ID + 4], [cwl, cs], inc=(g == 7))
    tr(pcb, pcb[:], cbl[0:8, :], cs[0:8, C_ID:C_ID + 8], [cbl, cs])
    cp("dve", cw[:], pcw[:], [pcw], [cw])
    cp("dve", cb[:], pcb[:], [pcb], [cb])
    hTg = [P.sb("hTg%d" % i, [128, 8, 512], BF16) for i in range(2)]
    decB = P.sb("decB", [128, 4, NCH]); decS = P.sb("decS", [128, 4, NCH])
    WT = P.sb("WT", [128, NCH, 4]); FT = P.sb("FT", [128, NCH, 4])
    GE = P.sb("GE", [4, NCH]); GP = P.sb("GP", [4, NCH]); dec = P.sb("dec", [4, NCH]); dbd = P.sb("dbd", [4, 4, NCH])
    carB = P.sb("carB", [4, 1]); carG = P.sb("carG", [4, 1])
    top_2a = P.top

    GB = min(1024, NTA)
    NGB = NTA // GB
    CPB = GB // 128
    I_b = P.sb("I_b", [4, GB]); E_b = P.sb("E_b", [4, GB]); ones4 = P.sb("ones4", [4, GB])
    Bn = P.sb("Bn", [4, GB]); A_b = P.sb("A_b", [4, GB]); G_b = P.sb("G_b", [4, GB])
    mset("pool", ones4[:], 1.0, [ones4])
    mset("dve", carB[:], 0.0, [carB]); mset("dve", carG[:], 0.0, [carG])
    pwt = P.pst("pwt", 1, 0, [128, 4 * NCH]); pft = P.pst("pft", 2, 0, [128, 4 * NCH])
    v3 = lambda T_: T_[:].rearrange("p (c l) -> p c l", l=128)
    for gb in range(NGB):
        for tb in range(GB // 512):
            t = gb * (GB // 512) + tb
            hT = hTg[t % 2]
            P.dma("sp", hT[:], HT[t], writes=[hT])
            pgi = P.pst("pgi", 4 + 2 * (t % 2), 0, [4, 512]); pgf = P.pst("pgf", 5 + 2 * (t % 2), 0, [4, 512])
            mm(pgi, pgi[:], [(w_i[:, kc, :], hT[:, kc, :]) for kc in range(8)], [w_i, hT])
            mm(pgf, pgf[:], [(w_f[:, kc, :], hT[:, kc, :]) for kc in range(8)], [w_f, hT])
            ts("dve", I_b[:, tb * 512:(tb + 1) * 512], pgi[:], bi[:, 0:1], None, ALU.add, None, [pgi, bi], [I_b])
            act(E_b[:, tb * 512:(tb + 1) * 512], pgf[:], AF.Exp, [pgf, bfn], [E_b], scale=-1.0, bias=bfn[:, 0:1])
        npre = max(0, min(GB, NT - gb * GB))
        act(E_b[:], E_b[:], AF.Ln, [E_b], [E_b], bias=1.0)
        if npre:
            ts("dve", E_b[:, 0:npre], E_b[:, 0:npre], cs[0:4, C_PF:C_PF + 1], None, ALU.mult, None, [E_b, cs], [E_b])
        scan(Bn, Bn[:], ones4[:], E_b, E_b[:], carB[:, 0:1], ALU.add, [ones4, carB])
        cp("dve", carB[:], Bn[:, GB - 1:GB], [Bn], [carB])
        tt("dve", A_b[:], I_b[:], Bn[:], ALU.add, [I_b, Bn], [A_b])
        if npre:
            ts("dve", A_b[:, 0:npre], A_b[:, 0:npre], cs[0:4, C_NB:C_NB + 1], None, ALU.add, None, [A_b, cs], [A_b])
        scan(G_b, G_b[:], ones4[:], A_b, A_b[:], carG[:, 0:1], ALU.max, [ones4, carG])
        cp("dve", carG[:], G_b[:, GB - 1:GB], [G_b], [carG])
        Gend = v3(G_b)[:, :, 127:128]
        tt("dve", v3(I_b), v3(A_b), Gend.to_broadcast([4, CPB, 128]), ALU.subtract, [A_b, G_b], [I_b])
        act(I_b[:], I_b[:], AF.Exp, [I_b], [I_b])
        tt("dve", v3(E_b), v3(Bn), Gend.to_broadcast([4, CPB, 128]), ALU.subtract, [Bn, G_b], [E_b])
        act(E_b[:], E_b[:], AF.Exp, [E_b], [E_b])
        cp("dve", GE[:, gb * CPB:(gb + 1) * CPB], v3(G_b)[:, :, 127], [G_b], [GE])
        for cl in range(CPB):
            c = gb * CPB + cl
            tr(pwt, pwt[:, 4 * c:4 * c + 4], I_b[0:4, cl * 128:(cl + 1) * 128], cs[0:4, C_ID:C_ID + 4], [I_b, cs], inc=(cl == CPB - 1))
        for cl in range(CPB):
            c = gb * CPB + cl
            tr(pft, pft[:, 4 * c:4 * c + 4], E_b[0:4, cl * 128:(cl + 1) * 128], cs[0:4, C_ID:C_ID + 4], [E_b, cs], inc=(cl == CPB - 1))
    cp("dve", WT[:].rearrange("p a b -> p (a b)"), pwt[:], [pwt], [WT])
    cp("act", FT[:].rearrange("p a b -> p (a b)"), pft[:], [pft], [FT])
    mset("dve", GP[:, 0:1], 0.0, [GP])
    cp("dve", GP[:, 1:NCH], GE[:, 0:NCH - 1], [GE], [GP])
    tt("dve", dec[:], GP[:], GE[:], ALU.subtract, [GP, GE], [dec])
    act(dec[:], dec[:], AF.Exp, [dec], [dec])
    for h in range(4):
        ts("dve", dbd[:, h, :], dec[:], cs[0:4, C_BD + h:C_BD + h + 1], None, ALU.mult, None, [dec, cs], [dbd])
    pdb = P.pst("pdb", 0, 0, [128, 4 * NCH])
    mm(pdb, pdb[:], [(cs[0:4, C_ONE:C_ONE + 128], dbd[:].rearrange("p a b -> p (a b)"))], [cs, dbd])
    cp("dve", decB[:].rearrange("p a b -> p (a b)"), pdb[:], [pdb], [decB])
    ts("dve", decS[:], decB[:], SC, None, ALU.mult, None, [decB], [decS])

    P.barrier()
    P.top = top_2a
    Cst = [P.sb("C%d" % h, [128, 257]) for h in range(4)]
    Cd = [P.sb("Cd%d" % h, [128, 257], BF16) for h in range(4)]
    for h in range(4):
        mset("dve", Cst[h][:], 0.0, [Cst[h]])
        mset("pool", Cd[h][:], 0.0, [Cd[h]])
    XP = [[P.sb("XP%d%d" % (g, h), [128, 515]) for h in range(4)] for g in range(2)]
    for g in range(2):
        for h in range(4):
            mset("pool", XP[g][h][:, 0:3], 0.0, [XP[g][h]])
    cv = [P.sb("cv%d" % i, [128, 512]) for i in range(2)]
    qkT = [[P.sb("qkT%d%d" % (g, i), [128, 4, 512], BF16) for i in range(2)] for g in range(2)]
    vw = [P.sb("vw%d" % i, [128, 4, 257], BF16) for i in range(2)]
    og = P.sb("og", [128, 1024])
    ktok = [P.sb("ktok%d" % i, [128, 4, 128], BF16) for i in range(2)]
    SM = [P.sb("SM%d" % i, [128, 4, 128], BF16) for i in range(2)]
    hm = [P.sb("hm%d" % i, [128, 4, 256]) for i in range(2)]
    st2 = [P.sb("st2_%d" % i, [128, 8]) for i in range(2)]
    sq2 = P.sb("sq2", [128, 256])
    t2 = P.sb("t2", [128, 1024])
    ymb = P.sb("ymb", [128, 1024], BF16)
    ymT = [P.sb("ymT%d" % i, [128, 8, 512], BF16) for i in range(2)]
    pj = [0]
    pend_tail = [None]

    def proj_group(t, g, h):
        hT = hTg[t % 2]
        wsel = w_k if g == 0 else w_q
        dst = qkT[g][t % 2]
        pp = P.pst("pp", pj[0] % 2, 0, [128, 512]); pj[0] += 1
        mm(pp, pp[:], [(wsel[:, kc, h * 128:(h + 1) * 128], hT[:, kc, :]) for kc in range(8)], [wsel, hT])
        xp = XP[g][h]
        ch = (1 - g) * 4 + h
        cp("act", xp[:, 3:515], pp[:], [pp], [xp])
        c_ = cv[(g * 4 + h) % 2]
        ts("dve", c_[:], xp[:, 0:512], cw[:, ch, 0:1], cb[:, ch:ch + 1], ALU.mult, ALU.add, [xp, cw, cb], [c_])
        for j in range(1, 4):
            stt("dve", c_[:], xp[:, j:j + 512], cw[:, ch, j:j + 1], c_[:], ALU.mult, ALU.add, [xp, cw, c_], [c_])
        act(dst[:, h, :], c_[:], AF.Silu, [c_], [dst])
        cp("pool", xp[:, 0:3], xp[:, 512:515], [xp], [xp])

    def groups_of(t):
        gs = (0, 1) if (t >= NTL_PRE or t == NTL_PRE - 1) else (0,)
        return [(t, g, h) for g in gs for h in range(4)]

    P.dma("sp", hTg[0][:], HT[0], writes=[hTg[0]])
    for a_ in groups_of(0):
        proj_group(*a_)
    for t in range(NTL_ALL):
        own = t >= NTL_PRE
        hT = hTg[t % 2]
        kT = qkT[0][t % 2]; qT = qkT[1][t % 2]
        nxt = []
        if t + 1 < NTL_ALL:
            P.dma("sp", hTg[(t + 1) % 2][:], HT[t + 1], writes=[hTg[(t + 1) % 2]])
            nxt = groups_of(t + 1)
        per = (len(nxt) + 3) // 4
        ymT_t = ymT[t % 2]
        for cc in range(4):
            c = 4 * t + cc
            par = c % 2
            cols = slice(cc * 128, (cc + 1) * 128)
            vw_c = vw[par]; ktok_c = ktok[par]
            for g in range(2):
                pv = P.pst("pv", 2 + g, 0, [128, 512])
                mm(pv, pv[:], [(hT[:, kc, cols], w_v[:, kc, g * 512:(g + 1) * 512]) for kc in range(8)], [hT, w_v])
                tt("dve", vw_c[:, 2 * g:2 * g + 2, 0:256], pv[:].rearrange("p (a b) -> p a b", a=2),
                   WT[:, c, 2 * g:2 * g + 2].unsqueeze(2).to_broadcast([128, 2, 256]), ALU.mult, [pv, WT], [vw_c])
            cp("pool", vw_c[:, :, 256], WT[:, c, :], [WT], [vw_c])
            ptk = P.pst("ptk", 5, 256, [128, 4, 128], BF16)
            for h in range(4):
                tr(ptk, ptk[:, h, :], kT[:, h, cols], identb[:], [kT, identb], inc=(h == 3))
            cp("act", ktok_c[:], ptk[:], [ptk], [ktok_c])
            if own:
                for g in range(2):
                    po = P.pst("pp", pj[0] % 2, 0, [128, 512]); pj[0] += 1
                    mm(po, po[:], [(hT[:, kc, cols], w_o[:, kc, g * 512:(g + 1) * 512]) for kc in range(8)], [hT, w_o])
                    act(og[:, g * 512:(g + 1) * 512], po[:], AF.Sigmoid, [po], [og])
                pS = P.pst("pS", 4, 0, [128, 4, 128])
                for h in range(4):
                    mm(pS, pS[:, h, :], [(kT[:, h, cols], qT[:, h, cols])], [kT, qT], inc=(h == 3))
                SM_c = SM[par]
                tt("dve", SM_c[:], pS[:], cs[:, C_TRIS:C_TRIS + 128].unsqueeze(1).to_broadcast([128, 4, 128]), ALU.mult, [pS, cs], [SM_c])
                hm_c = hm[par]; st_c = st2[par]
                if pend_tail[0] is not None:
                    pend_tail[0]()
                    pend_tail[0] = None
            for h in range(4):
                if own:
                    pn = P.pst("pn", 6 + h % 2, 0, [128, 257])
                    mm(pn, pn[:], [(SM_c[:, h, :], vw_c[:, h, :]), (qT[:, h, cols], Cd[h][:])], [SM_c, vw_c, qT, Cd[h]])
                    act(st_c[:, h:h + 1], pn[:, 256:257], AF.Abs, [pn], [st_c])
                    ts("dve", st_c[:, h:h + 1], st_c[:, h:h + 1], FT[:, c, h:h + 1], None, ALU.max, None, [st_c, FT], [st_c])
                    recip(st_c[:, h:h + 1], st_c[:, h:h + 1], [st_c], [st_c])
                    amul(hm_c[:, h, :], pn[:, 0:256], st_c[:, h:h + 1], [pn, st_c], [hm_c])
                    act(sq2[:], hm_c[:, h, :], AF.Square, [hm_c], [sq2, st_c], accum_out=st_c[:, 4 + h:5 + h])
                pu = P.pst("pu", 6 + (h + 1) % 2, 0, [128, 257])
                mm(pu, pu[:], [(ktok_c[:, h, :], vw_c[:, h, :])], [ktok_c, vw_c])
                stt("dve", Cst[h][:], Cst[h][:], decB[:, h, c:c + 1], pu[:], ALU.mult, ALU.add, [Cst[h], decB, pu], [Cst[h]])
                if c + 1 < NCH:
                    amul(Cd[h][:], Cst[h][:], decS[:, h, c + 1:c + 2], [Cst[h], decS], [Cd[h]])
            if own:
                rsqrt(st_c[:, 4:8], st_c[:, 4:8], 1.0 / 256, [st_c], [st_c])
                tt("pool", t2[:], gmn[:], og[:], ALU.mult, [gmn, og], [t2])
                tt("dve", hm_c[:], hm_c[:], st_c[:, 4:8].unsqueeze(2).to_broadcast([128, 4, 256]), ALU.mult, [hm_c, st_c], [hm_c])
                tt("dve", ymb[:], hm_c[:].rearrange("p a b -> p (a b)"), t2[:], ALU.mult, [hm_c, t2], [ymb])

                def tail(ymT_t=ymT_t, cols=cols):
                    for half in range(2):
                        pty = P.pst("pty", 5, 0, [128, 4, 128], BF16)
                        for k4 in range(4):
                            kc = half * 4 + k4
                            tr(pty, pty[:, k4, :], ymb[:, kc * 128:(kc + 1) * 128], identb[:], [ymb, identb], inc=(k4 == 3))
                        cp("act" if half else "dve", ymT_t[:, half * 4:half * 4 + 4, cols], pty[:], [pty], [ymT_t])
                if cc == 3:
                    tail()
                else:
                    pend_tail[0] = tail
            for a_ in nxt[cc * per:(cc + 1) * per]:
                proj_group(*a_)
        if own:
            P.dma("pool", YTm[t - NTL_PRE], ymT_t[:], reads=[ymT_t], on=ymT_t)
    if upto <= 2:
        return finish(nc, P)

    P.phase()
    grp3 = T("grp3", None)
    w_dk = load_w("w_dk", w_in[:, O_DK:O_DK + 1024], 1024, 1024, grp3)
    w_dv = load_w("w_dv", w_in[:, O_DV:O_DV + 1024], 1024, 1024, grp3)
    w_dq = load_w("w_dq", w_in[:, O_DQ:O_DQ + 1024], 1024, 1024, grp3)
    P.seal(grp3, [w_dk, w_dv, w_dq])
    hT3 = [P.sb("hT3_%d" % i, [128, 8, 512], BF16) for i in range(2)]
    kt3 = [P.sb("kt3_%d" % i, [128, 8, 512], BF16) for i in range(2)]
    qt3 = [P.sb("qt3_%d" % i, [128, 8, 512], BF16) for i in range(2)]
    vt3 = [P.sb("vt3_%d" % i, [128, 4, 8, 129], BF16) for i in range(2)]
    sq3 = [P.sb("sq3_%d" % i, [128, 512], BF16) for i in range(2)]
    rs3 = [P.sb("rs3_%d" % i, [128, 512]) for i in range(2)]
    for i in range(2):
        mset("pool", vt3[i][:, :, :, 128:129], 1.0, [vt3[i]])
    n3 = 0
    for t in range(NTL_ALL):
        own = t >= NTL_PRE
        hT = hT3[t % 2]
        P.dma("sp", hT[:], HT[t], writes=[hT])
        for (wsel, gcol, dstl, dram_T, tcol, enabled) in ((w_dk, gk2, kt3, KT, t * 512, True), (w_dq, gq2, qt3, QT, (t - NTL_PRE) * 512, own)):
            if not enabled:
                continue
            dst = dstl[t % 2]
            pend = []
            for h in range(10):
                if h < 8:
                    pk = P.pst("pk3", n3 % 4, 0, [128, 512])
                    mm(pk, pk[:], [(wsel[:, kc, h * 128:(h + 1) * 128], hT[:, kc, :]) for kc in range(8)], [wsel, hT])
                    pend.append((pk, h, n3))
                    n3 += 1
                if h >= 2:
                    pk_, h_, n_ = pend.pop(0)
                    pss = P.pst("pss3", 4 + n_ % 2, 0, [128, 512])
                    fm_norm(pk_, pk_[:], 512, blkb, 1.0 / 64, gcol, dst, dst[:, h_, :], sq3[n_ % 2], pss, rs3[n_ % 2])
            P.dma("pool", dram_T[:, :, tcol:tcol + 512].rearrange("h p n -> p h n"), dst[:], reads=[dst], on=dst)
        vt = vt3[t % 2]
        for j in range(4):
            for g in range(2):
                pv = P.pst("pv3", 6 + g, 0, [128, 512])
                mm(pv, pv[:], [(hT[:, kc, j * 128:(j + 1) * 128], w_dv[:, kc, g * 512:(g + 1) * 512]) for kc in range(8)], [hT, w_dv])
                cp("act" if g else "dve", vt[:, j, 4 * g:4 * g + 4, 0:128], pv[:].rearrange("p (a b) -> p a b", a=4), [pv], [vt])
            P.dma("pool", VA[:, :, 4 * t + j, :].rearrange("h p e -> p h e"), vt[:, j, :, :], reads=[vt], on=vt)

    P.phase()
    kh = [[P.sb("kh%d_%d" % (i, c), [128, NTA], BF16) for c in range(2)] for i in range(2)]
    for i in range(2):
        mset("pool", kh[i][0][64:128, :], 0.0, [kh[i][0]])
        mset("pool", kh[i][1][0:64, :], 0.0, [kh[i][1]])
    vh = [P.sb("vh%d" % i, [128, NBLK, 129], BF16) for i in range(2)]
    qh = [P.sb("qh%d" % i, [128, NT], BF16) for i in range(2)]
    NST = 4
    LOOK = 3
    NPT = 6
    pTb = [P.sb("pT%d" % i, [128, 512], BF16) for i in range(NPT)]
    rr = [P.sb("rr%d" % i, [128, 12]) for i in range(2)]
    od = [P.sb("od%d" % i, [128, 4, 128]) for i in range(2)]
    odb = [P.sb("odb%d" % i, [128, 4, 128], BF16) for i in range(2)]
    sq4 = P.sb("sq4", [128, 128])
    ydT = [P.sb("ydT%d" % i, [128, 512], BF16) for i in range(2)]
    stb = [P.pst("st%d" % i, i, 0, [128, 512]) for i in range(NST)]
    accb = [P.pst("accb%d" % i, 5 + i, 0, [128, 512]) for i in range(3)]
    accs = {}
    for c in range(2):
        for i in range(4):
            n = c * 4 + i
            accs[(c, i)] = (accb[n // 3], (n % 3) * 129)
    ptd = P.pst("ptd", 4, 0, [128, 512], BF16)
    accS = [P.sb("accS%d" % i, [128, 8, 129]) for i in range(2)]
    nq = 0
    pending_e2 = None

    def zero_acc():
        mset("dve", accb[0][:, 0:387], 0.0, [accb[0]])
        mset("dve", accb[1][:, 0:387], 0.0, [accb[1]])
        mset("dve", accb[2][:, 0:258], 0.0, [accb[2]])
    zero_acc()
    for h in range(8):
        k_h, v_h, q_h = kh[h % 2], vh[h % 2], qh[h % 2]
        P.dma("sp", k_h[0][0:64, :], KT[h][0:64, :], writes=[k_h[0]])
        P.dma("sp", k_h[1][64:128, :], KT[h][64:128, :], writes=[k_h[1]])
        P.dma("sp", v_h[:], VA[h], writes=[v_h])
        P.dma("sp", q_h[:], QT[h], writes=[q_h])
        for t in range(NTL_OWN):
            nkb = NPB + 4 * t + 4
            nun = 2 * nkb
            par = nq % 2
            r_ = rr[par]; od_ = od[par]; ob_ = odb[par]; aS = accS[par]

            def s_mm(u):
                kb, c = u // 2, u % 2
                i0 = max(0, kb - (NPB + 4 * t))
                st = stb[u % NST]
                mm(st, st[:, i0 * 128:512], [(k_h[c][:, kb * 128:(kb + 1) * 128],
                                               q_h[:, t * 512 + i0 * 128:(t + 1) * 512])], [k_h[c], q_h])
            for u in range(min(LOOK, nun)):
                s_mm(u)
            for u in range(nun):
                if u == 6 and pending_e2 is not None:
                    pending_e2()
                    pending_e2 = None
                if u + LOOK < nun:
                    s_mm(u + LOOK)
                kb, c = u // 2, u % 2
                i0 = max(0, kb - (NPB + 4 * t))
                diag = kb >= NPB + 4 * t
                bias_ap = cs[:, C_AB:C_AB + 1] if kb < NPB else zcol[:, 0:1]
                st = stb[u % NST]
                pT = pTb[u % NPT]
                act(pT[:, i0 * 128:512], st[:, i0 * 128:512], AF.Exp, [st, cs, zcol], [pT], scale=0.125, bias=bias_ap)
                if diag:
                    tt("dve", pT[:, i0 * 128:(i0 + 1) * 128], pT[:, i0 * 128:(i0 + 1) * 128], trib[:], ALU.mult, [pT, trib], [pT])
                for i in range(i0, 4):
                    a, lo = accs[(c, i)]
                    mm(a, a[:, lo:lo + 129], [(pT[:, i * 128:(i + 1) * 128], v_h[:, kb, :])], [pT, v_h], start=False, skip=True,
                       inc=(i == 3))
            cp("dve", aS[:, 0:3, :], accb[0][:, 0:387].rearrange("p (a b) -> p a b", a=3), [accb[0]], [aS])
            cp("act", aS[:, 3:6, :], accb[1][:, 0:387].rearrange("p (a b) -> p a b", a=3), [accb[1]], [aS])
            cp("dve", aS[:, 6:8, :], accb[2][:, 0:258].rearrange("p (a b) -> p a b", a=2), [accb[2]], [aS])
            zero_acc()
            recip(r_[:, 0:8], aS[:, :, 128], [aS], [r_])
            ts("dve", r_[:, 4:8], r_[:, 4:8], lam[:, 3:4], None, ALU.mult, None, [r_, lam], [r_])
            for i in range(4):
                ts("dve", od_[:, i, :], aS[:, i, 0:128], r_[:, i:i + 1], None, ALU.mult, None, [aS, r_], [od_])
                stt("dve", od_[:, i, :], aS[:, 4 + i, 0:128], r_[:, 4 + i:5 + i], od_[:, i, :], ALU.mult, ALU.add, [aS, r_, od_], [od_])
            yd = ydT[par]; nq += 1

            def e2(r_=r_, od_=od_, ob_=ob_, yd=yd, t=t, h=h):
                for i in range(4):
                    P.op("dve", lambda e, i=i: e.scalar_tensor_tensor(out=sq4[:], in0=od_[:, i, :], scalar=1.0, in1=od_[:, i, :],
                                                                       op0=ALU.mult, op1=ALU.mult, accum_out=r_[:, 8 + i:9 + i]),
                         reads=[od_], writes=[sq4, r_])
                rsqrt(r_[:, 8:12], r_[:, 8:12], 1.0 / 128, [r_], [r_])
                for i in range(4):
                    stt("dve", ob_[:, i, :], od_[:, i, :], r_[:, 8 + i:9 + i], gsub[:], ALU.mult, ALU.mult, [od_, r_, gsub], [ob_])
                    tr(ptd, ptd[:, i * 128:(i + 1) * 128], ob_[:, i, :], identb[:], [ob_, identb], inc=(i == 3))
                cp("dve", yd[:], ptd[:], [ptd], [yd])
                P.dma("pool", YTd[t][:, h, :], yd[:], reads=[yd], on=yd)
            pending_e2 = e2
    if pending_e2 is not None:
        pending_e2()
    if upto <= 3:
        return finish(nc, P)

    P.phase()
    grp4 = T("grp4", None)
    w_cq = load_w("w_cq", w_in[:, O_CQ:O_CQ + 512], 1024, 512, grp4)
    wkv = load_w("wkv", w_mem_kv, 1024, 1536, grp4)
    gmem = bload("gmem", mem_norm_g, 1024, grp4)
    memt = P.sb("memt", [128, 2, 1024])
    P.dma("sp", memt[:], mem.rearrange("(j p) d -> p j d", p=128), writes=[memt], on=grp4)
    P.seal(grp4, [w_cq, wkv, gmem, memt])
    mkT = P.sb("mkT", [128, 4, 256], BF16)
    mva = P.sb("mva", [128, 2, 4, 257], BF16)
    hT4 = [P.sb("hT4_%d" % i, [128, 8, 512], BF16) for i in range(2)]
    qc = [P.sb("qc%d" % i, [128, 512], BF16) for i in range(2)]
    sq5 = [P.sb("sq5_%d" % i, [128, 512], BF16) for i in range(2)]
    rs5 = [P.sb("rs5_%d" % i, [128, 512]) for i in range(2)]
    pc = [[P.sb("pc%d_%d" % (i, mb), [128, 512], BF16) for mb in range(2)] for i in range(2)]
    rc = [P.sb("rc%d" % i, [128, 1]) for i in range(2)]
    ycb = [P.sb("ycb%d" % i, [128, 256], BF16) for i in range(2)]
    ycT = [P.sb("ycT%d" % i, [128, 8, 512], BF16) for i in range(2)]
    pq4 = [P.pst("pq4", i, 0, [128, 512]) for i in range(2)]
    pss4 = P.pst("pss4", 2, 0, [128, 512])
    ps4 = [P.pst("ps4", 3 + i, 0, [128, 512]) for i in range(2)]
    mscr = P.sb("mscr", [128, 1024]); mss = P.sb("mss", [128, 4])
    mhb = P.sb("mhb", [128, 2, 1024], BF16)
    mhT = P.sb("mhT", [128, 8, 256], BF16)
    rms_rows(memt, lambda j: memt[:, j, :], 2, gmem, mhb, lambda j: mhb[:, j, :], mscr, mss)
    for kc in range(8):
        pt = P.pst("pt_m", 7, 256 + 128 * (kc % 2), [128, 256], BF16)
        for j in range(2):
            tr(pt, pt[:, j * 128:(j + 1) * 128], mhb[:, j, kc * 128:(kc + 1) * 128], identb[:], [mhb, identb], inc=(j == 1))
        cp("act" if kc % 2 else "dve", mhT[:, kc, :], pt[:], [pt], [mhT])
    for h in range(4):
        pk = pq4[h % 2]
        mm(pk, pk[:, 0:256], [(wkv[:, kc, h * 128:(h + 1) * 128], mhT[:, kc, :]) for kc in range(8)], [wkv, mhT])
        fm_norm(pk, pk[:, 0:256], 256, oneb, 1.0 / 128, gck, mkT, mkT[:, h, :], sq5[h % 2], pss4, rs5[h % 2])
    mset("dve", mva[:, :, :, 256:257], 1.0, [mva])
    for j in range(2):
        for g in range(2):
            pv = ps4[g]
            mm(pv, pv[:], [(mhT[:, kc, j * 128:(j + 1) * 128], wkv[:, kc, 512 + g * 512:512 + (g + 1) * 512]) for kc in range(8)], [mhT, wkv])
            cp("act" if g else "dve", mva[:, j, 2 * g:2 * g + 2, 0:256], pv[:].rearrange("p (a b) -> p a b", a=2), [pv], [mva])
    n4 = 0
    for t in range(NTL_OWN):
        hT = hT4[t % 2]
        P.dma("sp", hT[:], HT[NTL_PRE + t], writes=[hT])
        yT = ycT[t % 2]
        pq_next = pq4[n4 % 2]
        mm(pq_next, pq_next[:], [(w_cq[:, kc, 0:128], hT[:, kc, :]) for kc in range(8)], [w_cq, hT])
        for h in range(4):
            pq = pq_next
            if h < 3:
                pq_next = pq4[(n4 + 1) % 2]
                mm(pq_next, pq_next[:], [(w_cq[:, kc, (h + 1) * 128:(h + 2) * 128], hT[:, kc, :]) for kc in range(8)], [w_cq, hT])
            q_ = qc[n4 % 2]
            fm_norm(pq, pq[:], 512, oneb, 1.0 / 128, gcq, q_, q_[:], sq5[n4 % 2], pss4, rs5[n4 % 2])
            for mb in range(2):
                ps_ = ps4[mb]
                mm(ps_, ps_[:], [(mkT[:, h, mb * 128:(mb + 1) * 128], q_[:])], [mkT, q_])
                act(pc[n4 % 2][mb][:], ps_[:], AF.Exp, [ps_], [pc[n4 % 2][mb]], scale=SC)
            for i in range(4):
                pa = P.pst("pa4", 5 + i % 2, 0, [128, 257])
                mm(pa, pa[:], [(pc[n4 % 2][mb][:, i * 128:(i + 1) * 128], mva[:, mb, h, :]) for mb in range(2)], [pc[n4 % 2][0], pc[n4 % 2][1], mva])
                r_ = rc[i % 2]; yb = ycb[i % 2]
                recip(r_[:], pa[:, 256:257], [pa], [r_])
                amul(yb[:], pa[:, 0:256], r_[:, 0:1], [pa, r_], [yb])
                pty = P.pst("pty4", 7, (i % 2) * 128, [128, 2, 128], BF16)
                for k2 in range(2):
                    tr(pty, pty[:, k2, :], yb[:, k2 * 128:(k2 + 1) * 128], identb[:], [yb, identb], inc=(k2 == 1))
                cp("dve", yT[:, 2 * h:2 * h + 2, i * 128:(i + 1) * 128], pty[:], [pty], [yT])
            n4 += 1
        P.dma("pool", YTc[t], yT[:], reads=[yT], on=yT)
    if upto <= 4:
        return finish(nc, P)

    P.phase()
    grp5 = T("grp5", None)
    w_g = load_w("w_g", w_gate, 1024, 3072, grp5)
    w_pj = [load_w("w_pj%d" % i, w, 1024, 1024, grp5) for i, w in enumerate((w_proj_m, w_proj_d, w_proj_c))]
    w_ot = load_w("w_ot", w_out, 1024, 1024, grp5)
    bgb = P.sb("bgb", [1, 3072], BF16)
    P.dma("pool", bgb[:], b_gate, writes=[bgb], on=grp5)
    P.seal(grp5, [w_g, w_ot, bgb] + w_pj)
    hT5 = [P.sb("hT5_%d" % i, [128, 8, 512], BF16) for i in range(2)]
    yT5 = [P.sb("yT5_%d" % b, [128, 8, 512], BF16) for b in range(3)]
    x5 = [P.sb("x5_%d" % i, [128, 1024]) for i in range(2)]
    gs5 = [P.sb("gs5_%d" % i, [128, 512]) for i in range(2)]
    tm5 = [P.sb("tm5_%d" % i, [128, 512]) for i in range(2)]
    mg5 = P.sb("mg5", [128, 1024])
    mgb = [P.sb("mgb%d" % i, [128, 1024], BF16) for i in range(2)]
    mT5 = [P.sb("mT5_%d" % i, [128, 8, 128], BF16) for i in range(2)]
    YTs = (YTm, YTd, YTc)
    n5 = 0
    for t in range(NTL_OWN):
        hT = hT5[t % 2]
        P.dma("sp", hT[:], HT[NTL_PRE + t], writes=[hT])
        for b in range(3):
            P.dma("sp", yT5[b][:], YTs[b][t], writes=[yT5[b]])
        for j in range(4):
            sub = t * 4 + j
            cols = slice(j * 128, (j + 1) * 128)
            x_ = x5[sub % 2]; mg = mg5; mb_ = mgb[sub % 2]
            P.dma("sp", x_[:], x_own[sub * 128:(sub + 1) * 128, :], writes=[x_])
            for b in range(3):
                yT = yT5[b]
                for g in range(2):
                    pg = P.pst("pg5", n5 % 2, 0, [128, 512])
                    pp = P.pst("pp5", 2 + n5 % 2, 0, [128, 512])
                    gcols = slice(b * 1024 + g * 512, b * 1024 + (g + 1) * 512)
                    mm(pg, pg[:], [(hT[:, kc, cols], w_g[:, kc, gcols]) for kc in range(8)] + [(oneb[0:1, 0:128], bgb[0:1, gcols])], [hT, w_g, oneb, bgb])
                    mm(pp, pp[:], [(yT[:, kc, cols], w_pj[b][:, kc, g * 512:(g + 1) * 512]) for kc in range(8)], [yT, w_pj[b]])
                    gs = gs5[n5 % 2]; tm = tm5[n5 % 2]
                    act(gs[:], pg[:], AF.Sigmoid, [pg], [gs])
                    mcol = mg[:, g * 512:(g + 1) * 512]
                    if b == 0:
                        tt("dve", mcol, gs[:], pp[:], ALU.mult, [gs, pp], [mg])
                    else:
                        tt("dve", tm[:], gs[:], pp[:], ALU.mult, [gs, pp], [tm])
                        if b == 1:
                            tt("pool", mcol, mcol, tm[:], ALU.add, [mg, tm], [mg])
                        else:
                            tt("pool", mb_[:, g * 512:(g + 1) * 512], mcol, tm[:], ALU.add, [mg, tm], [mb_])
                    n5 += 1
            mT = mT5[sub % 2]
            ptm = P.pst("ptm5", 4, 0, [128, 8, 128], BF16)
            for kc in range(8):
                tr(ptm, ptm[:, kc, :], mb_[:, kc * 128:(kc + 1) * 128], identb[:], [mb_, identb], inc=(kc == 7))
            cp("act", mT[:], ptm[:], [ptm], [mT])
            for g in range(2):
                po = P.pst("po5", 5 + g, 0, [128, 512])
                mm(po, po[:], [(mT[:, kc, :], w_ot[:, kc, g * 512:(g + 1) * 512]) for kc in range(8)], [mT, w_ot])
                tt("dve", x_[:, g * 512:(g + 1) * 512], po[:], x_[:, g * 512:(g + 1) * 512], ALU.add, [po, x_], [x_])
            P.dma("pool", XM[sub * 128:(sub + 1) * 128, :], x_[:], reads=[x_], on=x_)
    if upto <= 5:
        return finish(nc, P)

    P.phase()
    grp6 = T("grp6", None)
    gmlp = bload("gmlp", norm_mlp_g, 1024, grp6)
    P.seal(grp6, [gmlp])
    wu = [load_w("wu%d" % i, w_up[:, i * 1024:(i + 1) * 1024], 1024, 1024, T("g6u%d" % i, None)) for i in range(4)]
    wd = [load_w("wd%d" % i, w_down[i * 1024:(i + 1) * 1024, :], 1024, 1024, T("g6d%d" % i, None)) for i in range(4)]
    T6 = 256
    NS6 = T6 // 128
    xt6 = [P.sb("xt6_%d" % i, [128, NS6, 1024]) for i in range(2)]
    hb6 = P.sb("hb6", [128, NS6, 1024], BF16)
    hT6 = P.sb("hT6", [128, 8, T6], BF16)
    ss6 = P.sb("ss6", [128, 4]); scr6 = P.sb("scr6", [128, 1024])
    uT = P.sb("uT", [128, 32, T6], BF16)
    rl = [P.sb("rl%d" % i, [128, T6], BF16) for i in range(2)]
    ob6 = [P.sb("ob6_%d" % i, [128, 512]) for i in range(2)]
    n6 = 0
    for t in range(NT // T6):
        xt = xt6[t % 2]
        P.dma("sp", xt[:], XM[t * T6:(t + 1) * T6, :].rearrange("(j p) d -> p j d", p=128), writes=[xt])
        rms_rows(xt, lambda j, xt=xt: xt[:, j, :], NS6, gmlp, hb6, lambda j: hb6[:, j, :], scr6, ss6)
        for kc in range(8):
            pt = P.pst("pt6", kc % 2, 0, [128, T6], BF16)
            for j in range(NS6):
                tr(pt, pt[:, j * 128:(j + 1) * 128], hb6[:, j, kc * 128:(kc + 1) * 128], identb[:], [hb6, identb], inc=(j == NS6 - 1))
            cp("act" if kc % 2 else "dve", hT6[:, kc, :], pt[:], [pt], [hT6])
        for fc in range(32):
            pu = P.pst("pu6", 2 + fc % 3, 0, [128, T6])
            wsel = wu[fc // 8]
            mm(pu, pu[:], [(wsel[:, kc, (fc % 8) * 128:(fc % 8 + 1) * 128], hT6[:, kc, :]) for kc in range(8)], [wsel, hT6])
            r_ = rl[fc % 2]
            act(r_[:], pu[:], AF.Relu, [pu], [r_])
            tt("pool" if fc % 2 else "dve", uT[:, fc, :], r_[:], r_[:], ALU.mult, [r_], [uT])
        for j in range(NS6):
            for g in range(2):
                pd = P.pst("pd6", 5 + n6 % 3, 0, [128, 512])
                mm(pd, pd[:], [(uT[:, fc, j * 128:(j + 1) * 128], wd[fc // 8][:, fc % 8, g * 512:(g + 1) * 512]) for fc in range(32)], [uT] + wd)
                o_ = ob6[n6 % 2]
                tt("dve", o_[:], pd[:], xt[:, j, g * 512:(g + 1) * 512], ALU.add, [pd, xt], [o_])
                r0 = t * T6 + j * 128
                P.dma("pool", out[r0:r0 + 128, g * 512:(g + 1) * 512], o_[:], reads=[o_], on=o_)
                n6 += 1
    return finish(nc, P)


def finish(nc, P):
    P.barrier()
    P.emit()
    P.es.close()
    return nc


def make_consts(pre_valid):
    c = np.zeros((128, NCST), np.float32)
    c[:, C_ID:C_ID + 128] = np.eye(128, dtype=np.float32)
    tri = np.triu(np.ones((128, 128), np.float32))
    c[:, C_TRI:C_TRI + 128] = tri
    c[:, C_TRIS:C_TRIS + 128] = tri * np.float32(128 ** -0.5)
    blk = np.zeros((128, 128), np.float32)
    blk[:64, :64] = 1.0
    blk[64:, 64:] = 1.0
    c[:, C_BLK:C_BLK + 128] = blk
    c[:, C_ONE:C_ONE + 128] = 1.0
    c[:, C_PF] = 1.0 if pre_valid else 0.0
    c[:, C_NB] = 0.0 if pre_valid else -1.0e30
    c[:, C_AB] = 0.0 if pre_valid else -30000.0
    c[0:4, C_BD:C_BD + 4] = np.eye(4, dtype=np.float32)
    return c


_NC_CACHE = {}


def run(inputs, NT, n_pairs, upto=99, dbg=False):
    f = lambda a: np.ascontiguousarray(np.asarray(a, dtype=np.float32))
    key = (NT, upto, dbg)
    if key not in _NC_CACHE:
        _NC_CACHE[key] = build(NT, upto, dbg)
    nc = _NC_CACHE[key]
    x = f(inputs["x"]); mem = f(inputs["mem"])
    shared = {
        "norm_mix_g": f(inputs["norm_mix_g"]).reshape(1, 1024), "w_in": f(inputs["w_in"]).reshape(1024, IN_COLS),
        "b_igate": f(inputs["b_igate"]).reshape(4, 1), "b_fgate": f(inputs["b_fgate"]).reshape(4, 1),
        "conv_w": f(inputs["conv_w"]).reshape(4, 1024), "conv_b": f(inputs["conv_b"]).reshape(8, 128),
        "m_norm_g": f(inputs["m_norm_g"]).reshape(1, 1024),
        "dq_norm_g": f(inputs["dq_norm_g"]).reshape(64, 1), "dk_norm_g": f(inputs["dk_norm_g"]).reshape(64, 1),
        "lam_q1": f(inputs["lam_q1"]).reshape(1, 64), "lam_k1": f(inputs["lam_k1"]).reshape(1, 64),
        "lam_q2": f(inputs["lam_q2"]).reshape(1, 64), "lam_k2": f(inputs["lam_k2"]).reshape(1, 64),
        "subln_g": f(inputs["subln_g"]).reshape(1, 128),
        "cq_norm_g": f(inputs["cq_norm_g"]).reshape(128, 1), "ck_norm_g": f(inputs["ck_norm_g"]).reshape(128, 1),
        "mem_norm_g": f(inputs["mem_norm_g"]).reshape(1, 1024), "w_mem_kv": f(inputs["w_mem_kv"]).reshape(1024, 1536),
        "w_gate": f(inputs["w_gate"]).reshape(1024, 3072), "b_gate": f(inputs["b_gate"]).reshape(1, 3072),
        "w_proj_m": f(inputs["w_proj_m"]).reshape(1024, 1024), "w_proj_d": f(inputs["w_proj_d"]).reshape(1024, 1024),
        "w_proj_c": f(inputs["w_proj_c"]).reshape(1024, 1024), "w_out": f(inputs["w_out"]).reshape(1024, 1024),
        "norm_mlp_g": f(inputs["norm_mlp_g"]).reshape(1, 1024),
        "w_up": f(inputs["w_up"]).reshape(1024, 4096), "w_down": f(inputs["w_down"]).reshape(4096, 1024),
    }
    in_maps = []
    for b in range(n_pairs):
        for half in range(2):
            d = dict(shared)
            d["x_own"] = np.ascontiguousarray(x[b, half * NT:(half + 1) * NT])
            d["x_pre"] = np.ascontiguousarray(x[b, 0:NT]) if half == 1 else np.zeros((NT, 1024), np.float32)
            d["mem"] = np.ascontiguousarray(mem[b])
            d["cst"] = make_consts(half == 1)
            in_maps.append(d)
    res = run_bass_kernel_spmd(nc, in_maps, core_ids=list(range(2 * n_pairs)))
    return res


def kernel(**inputs):
    res = run(inputs, 4096, 4)
    outp = np.empty((4, 8192, 1024), np.float32)
    for b in range(4):
        for half in range(2):
            outp[b, half * 4096:(half + 1) * 4096] = res.results[2 * b + half]["out"]
    return outp
```
